# Optimizing a Trainium2 kernel written in Bass

```python
import jax, jax.numpy as jnp
from jax import lax
import numpy as np

D_MODEL = 2048
BATCH = 2
SEQ = 4096
DEPTH = 1
DEC_BATCH = 32
DEC_SEQ = 1
PAST_LEN = 8192
PAGE_SIZE = 128

HEAD_DIM = 128
ATTN_W = D_MODEL // 2
ATTN_HEADS = ATTN_W // HEAD_DIM
CONV_W = D_MODEL - ATTN_W
CONV_GROUPS = CONV_W // HEAD_DIM
MIX_W = ATTN_W + CONV_W
CONV_K = 31
MOBA_BLOCK = 256
MOBA_TOPK = 3
Q_CHUNK = 32
PLE_DIM = 256
IN_W = 4 * ATTN_W + 3 * CONV_W
DN_ALPHA = (2 * DEPTH) ** 0.25
DN_BETA = (8 * DEPTH) ** -0.25
LN_EPS = 1e-5
NEG = -1e30

kernel_name = 'hymba_conformer_moba_decoder_step'


def layer_norm(x, g, b):
    xf = x.astype(jnp.float32)
    mu = jnp.mean(xf, axis=-1, keepdims=True)
    var = jnp.mean(jnp.square(xf - mu), axis=-1, keepdims=True)
    y = (xf - mu) * lax.rsqrt(var + LN_EPS) * g.astype(jnp.float32) + b.astype(jnp.float32)
    return y.astype(x.dtype)


def moba_attention(q, k, v, q_pos0):
    B, Sq, H, Dh = q.shape
    T = k.shape[1]
    nb = -(-T // MOBA_BLOCK)
    pad_t = nb * MOBA_BLOCK - T
    kb = jnp.pad(k, ((0, 0), (0, pad_t), (0, 0), (0, 0))).reshape(B, nb, MOBA_BLOCK, H, Dh)
    vb = jnp.pad(v, ((0, 0), (0, pad_t), (0, 0), (0, 0))).reshape(B, nb, MOBA_BLOCK, H, Dh)
    k_mean = jnp.mean(kb.astype(jnp.float32), axis=2)
    qc = min(Q_CHUNK, Sq)
    n_chunk = -(-Sq // qc)
    sq_pad = n_chunk * qc
    q_p = jnp.pad(q, ((0, 0), (0, sq_pad - Sq), (0, 0), (0, 0)))
    pos = q_pos0 + jnp.arange(sq_pad)
    own = pos // MOBA_BLOCK
    gate = jnp.einsum('bshd,bnhd->bhsn', q_p.astype(jnp.float32), k_mean)
    cand = jnp.arange(nb)[None, :] < own[:, None]
    gate = jnp.where(cand[None, None], gate, -jnp.inf)
    n_sel = min(MOBA_TOPK, nb)
    _, sel = lax.top_k(gate, n_sel)
    sel_valid = jnp.arange(n_sel)[None, :] < own[:, None]
    own_c = jnp.minimum(own, nb - 1)
    blocks = jnp.concatenate(
        [sel, jnp.broadcast_to(own_c[None, None, :, None], (B, H, sq_pad, 1))], axis=-1)
    valid = jnp.concatenate([sel_valid, jnp.ones((sq_pad, 1), dtype=bool)], axis=-1)
    n_blk = n_sel + 1
    scale = HEAD_DIM ** -0.5
    bi = jnp.arange(B)[:, None, None, None]
    hi = jnp.arange(H)[None, :, None, None]

    def attend(args):
        q_c, blk, val, p_c = args
        ks = kb[bi, blk, :, hi]
        vs = vb[bi, blk, :, hi]
        s = jnp.einsum('bqhd,bhqnld->bhqnl', q_c, ks, preferred_element_type=jnp.float32) * scale
        key_pos = blk[..., None] * MOBA_BLOCK + jnp.arange(MOBA_BLOCK)
        mask = val[None, None, :, :, None] & (key_pos <= p_c[None, None, :, None, None])
        s = jnp.where(mask, s, NEG)
        pr = jax.nn.softmax(s.reshape(B, H, qc, n_blk * MOBA_BLOCK), axis=-1).reshape(s.shape)
        return jnp.einsum('bhqnl,bhqnld->bqhd', pr.astype(vs.dtype), vs)

    q_chunks = q_p.reshape(B, n_chunk, qc, H, Dh).transpose(1, 0, 2, 3, 4)
    blk_chunks = blocks.reshape(B, H, n_chunk, qc, n_blk).transpose(2, 0, 1, 3, 4)
    val_chunks = valid.reshape(n_chunk, qc, n_blk)
    pos_chunks = pos.reshape(n_chunk, qc)
    out = lax.map(attend, (q_chunks, blk_chunks, val_chunks, pos_chunks))
    out = out.transpose(1, 0, 2, 3, 4).reshape(B, sq_pad, H, Dh)
    return out[:, :Sq]


def causal_depthwise_conv(u_ext, w, b):
    y = lax.conv_general_dilated(u_ext, w[:, None, :].astype(u_ext.dtype), window_strides=(1,), padding='VALID',
                                 dimension_numbers=('NWC', 'WIO', 'NWC'), feature_group_count=u_ext.shape[-1])
    return y + b


def mixer_layer(x, p, conv_hist, k_past, v_past, q_pos0,
                w_in, b_in, w_dw, b_dw, g_cn, b_cn, w_pw, b_pw, w_out, b_out, g_ln, b_ln, w_pe, w_pg, b_pg):
    B, S, _ = x.shape
    z = x @ w_in + b_in
    q, k, v, g_a, a, bg, g_c = jnp.split(
        z, [ATTN_W, 2 * ATTN_W, 3 * ATTN_W, 4 * ATTN_W, 4 * ATTN_W + CONV_W, 4 * ATTN_W + 2 * CONV_W], axis=-1)
    q = q.reshape(B, S, ATTN_HEADS, HEAD_DIM)
    k = k.reshape(B, S, ATTN_HEADS, HEAD_DIM)
    v = v.reshape(B, S, ATTN_HEADS, HEAD_DIM)
    k_all = k if k_past is None else jnp.concatenate([k_past, k], axis=1)
    v_all = v if v_past is None else jnp.concatenate([v_past, v], axis=1)
    attn = moba_attention(q, k_all, v_all, q_pos0).reshape(B, S, ATTN_W) * jax.nn.silu(g_a)
    u = a * jax.nn.sigmoid(bg)
    u_ext = jnp.concatenate([conv_hist.astype(u.dtype), u], axis=1)
    c = causal_depthwise_conv(u_ext, w_dw, b_dw)
    c = jax.nn.silu(layer_norm(c, g_cn, b_cn))
    c = (c @ w_pw + b_pw) * jax.nn.silu(g_c)
    mix = jnp.concatenate([attn, c], axis=-1) @ w_out + b_out
    h = layer_norm(DN_ALPHA * x + mix, g_ln, b_ln)
    h = h + jax.nn.sigmoid(h @ w_pg + b_pg) * (p @ w_pe)
    new_conv = u_ext[:, -(CONV_K - 1):]
    return h, k, v, new_conv


def setup_inputs(seed: int = 0) -> dict:
    key = jax.random.key(seed)
    ks = jax.random.split(key, 26)
    f32 = jnp.float32
    n_pages = PAST_LEN // PAGE_SIZE
    n_phys = (5 * DEC_BATCH * n_pages) // 4
    nrm = lambda k, shape, s: jax.random.normal(k, shape, f32) * s
    page_table = jax.random.permutation(ks[0], n_phys)[:DEC_BATCH * n_pages].reshape(DEC_BATCH, n_pages).astype(jnp.int32)
    return {
        'x_prompt': nrm(ks[1], (BATCH, SEQ, D_MODEL), 1.0),
        'x_sample': nrm(ks[2], (DEC_BATCH, DEC_SEQ, D_MODEL), 1.0),
        'p_prompt': nrm(ks[3], (DEPTH, BATCH, SEQ, PLE_DIM), 1.0),
        'p_sample': nrm(ks[4], (DEPTH, DEC_BATCH, DEC_SEQ, PLE_DIM), 1.0),
        'cache_k': nrm(ks[5], (DEPTH, n_phys, PAGE_SIZE, ATTN_HEADS, HEAD_DIM), 1.0),
        'cache_v': nrm(ks[6], (DEPTH, n_phys, PAGE_SIZE, ATTN_HEADS, HEAD_DIM), 1.0),
        'state_conv': nrm(ks[7], (DEPTH, DEC_BATCH, CONV_K - 1, CONV_W), 0.5),
        'page_table': page_table,
        'w_in': nrm(ks[8], (DEPTH, D_MODEL, IN_W), D_MODEL ** -0.5),
        'b_in': nrm(ks[9], (DEPTH, IN_W), 0.01),
        'w_dw': nrm(ks[10], (DEPTH, CONV_K, CONV_W), CONV_K ** -0.5),
        'b_dw': nrm(ks[11], (DEPTH, CONV_W), 0.01),
        'g_cn': 1.0 + nrm(ks[12], (DEPTH, CONV_W), 0.01),
        'b_cn': nrm(ks[13], (DEPTH, CONV_W), 0.01),
        'w_pw': nrm(ks[14], (DEPTH, CONV_W, CONV_W), DN_BETA * CONV_W ** -0.5),
        'b_pw': nrm(ks[15], (DEPTH, CONV_W), 0.01),
        'w_out': nrm(ks[16], (DEPTH, MIX_W, D_MODEL), DN_BETA * MIX_W ** -0.5),
        'b_out': nrm(ks[17], (DEPTH, D_MODEL), 0.01),
        'g_ln': 1.0 + nrm(ks[18], (DEPTH, D_MODEL), 0.01),
        'b_ln': nrm(ks[19], (DEPTH, D_MODEL), 0.01),
        'w_pe': nrm(ks[20], (DEPTH, PLE_DIM, D_MODEL), PLE_DIM ** -0.5),
        'w_pg': nrm(ks[21], (DEPTH, D_MODEL, D_MODEL), D_MODEL ** -0.5),
        'b_pg': nrm(ks[22], (DEPTH, D_MODEL), 0.01),
    }


def reference(x_prompt, x_sample, p_prompt, p_sample, cache_k, cache_v, state_conv, page_table,
              w_in, b_in, w_dw, b_dw, g_cn, b_cn, w_pw, b_pw, w_out, b_out, g_ln, b_ln, w_pe, w_pg, b_pg):
    n_pages = page_table.shape[1]
    past = n_pages * PAGE_SIZE
    dec_b = x_sample.shape[0]
    hp, hs = x_prompt, x_sample
    kp_l, vp_l, cp_l, ks_l, vs_l, cs_l = [], [], [], [], [], []
    for i in range(DEPTH):
        wts = (w_in[i], b_in[i], w_dw[i], b_dw[i], g_cn[i], b_cn[i], w_pw[i], b_pw[i],
               w_out[i], b_out[i], g_ln[i], b_ln[i], w_pe[i], w_pg[i], b_pg[i])
        hist0 = jnp.zeros((hp.shape[0], CONV_K - 1, CONV_W), hp.dtype)
        hp, kp, vp, cp = mixer_layer(hp, p_prompt[i], hist0, None, None, 0, *wts)
        k_past = cache_k[i][page_table].reshape(dec_b, past, ATTN_HEADS, HEAD_DIM)
        v_past = cache_v[i][page_table].reshape(dec_b, past, ATTN_HEADS, HEAD_DIM)
        hs, ksn, vsn, csn = mixer_layer(hs, p_sample[i], state_conv[i], k_past, v_past, past, *wts)
        kp_l.append(kp); vp_l.append(vp); cp_l.append(cp)
        ks_l.append(ksn); vs_l.append(vsn); cs_l.append(csn)
    return (hp, hs, jnp.stack(kp_l), jnp.stack(vp_l), jnp.stack(cp_l),
            jnp.stack(ks_l), jnp.stack(vs_l), jnp.stack(cs_l))
```

```python
import numpy as np
import concourse.bass as bass
import concourse.mybir as mybir
from concourse.bass_utils import run_bass_kernel_spmd

F32 = mybir.dt.float32
BF16 = mybir.dt.bfloat16
I32 = mybir.dt.int32
AF = mybir.ActivationFunctionType
ALU = mybir.AluOpType
AX = mybir.AxisListType

D = 2048
NT = 1156
NCc = 1028
ENG = ("pe", "act", "dve", "pool", "sp")


class Prog:
    def __init__(self, nc, es):
        self.nc = nc
        self.es = es
        self.sems = {}
        self.reset()

    def reset(self):
        self.ops = {k: [] for k in ENG}

    def _sem(self, name):
        if name not in self.sems:
            h = self.es.enter_context(self.nc.semaphore(name))
            self.sems[name] = [h, 0]
        return self.sems[name]

    def op(self, eng, fn, waits=(), sig=True):
        dep = None
        s = None
        if sig:
            s = self._sem("m_" + eng)
            s[1] += 1
            dep = (s[0], s[1])
        self.ops[eng].append((tuple(w for w in waits if w is not None), fn, s[0] if sig else None, 1))
        return dep

    def dma(self, eng, fn, semname, waits=()):
        s = self._sem("d_" + semname)
        s[1] += 16
        self.ops[eng].append((tuple(w for w in waits if w is not None), fn, s[0], 16))
        return (s[0], s[1])

    def wait(self, eng, waits):
        self.ops[eng].append((tuple(w for w in waits if w is not None), None, None, 0))

    def emit(self, blk):
        ops = self.ops

        def run(e, lst):
            seen = {}
            for waits, fn, sem, inc in lst:
                for (h, v) in waits:
                    k = id(h)
                    if seen.get(k, -1) >= v:
                        continue
                    seen[k] = v
                    e.wait_ge(h, v)
                if fn is not None:
                    ins = fn(e)
                    if sem is not None:
                        ins.then_inc(sem, inc)

        @blk.tensor
        def _(e):
            run(e, ops["pe"])

        @blk.scalar
        def _(e):
            run(e, ops["act"])

        @blk.vector
        def _(e):
            run(e, ops["dve"])

        @blk.gpsimd
        def _(e):
            run(e, ops["pool"])

        @blk.sync
        def _(e):
            run(e, ops["sp"])

        self.reset()


def emit_p4(nc, P, st4, ck, cv, ptrep_d, iota_d, dsel_d, ident_d, qs, ksT, vsT, gasT, write_out, dbgfn=None):
    def sb(name, shape, dtype):
        return st4.enter_context(nc.sbuf_tensor(name, list(shape), dtype))
    def ps(name, shape):
        return st4.enter_context(nc.psum_tensor(name, list(shape), F32))
    SC = 128.0 ** -0.5
    ptab = sb("ptab", [128, 256], I32)
    iota = sb("iota4", [128, 1], F32)
    idx = sb("idx4", [128, 256], I32)
    dsel = sb("dsel_sb", [128, 32, 16], F32)
    identf = sb("identf4", [128, 128], F32)
    onesf = sb("onesf4", [128, 128], F32)
    onesb = sb("onesb4", [128, 2], BF16)
    qcol = [sb(f"qcol{i}", [128, 128], F32) for i in range(2)]
    qrep = sb("qrep", [128, 8, 128], F32)
    vrep = sb("vrep", [128, 8, 128], F32)
    garep = sb("garep", [128, 8, 128], F32)
    kpg = [sb(f"kpg{i}", [128, 1024], F32) for i in range(3)]
    prod = [sb(f"prod{i}", [128, 8, 128], F32) for i in range(2)]
    sc = sb("sc4", [128, 32, 16], F32)
    psc = sb("psc4", [128, 32, 16], BF16)
    vpg = [sb(f"vpg{i}", [128, 1024], BF16) for i in range(4)]
    gsel = sb("gsel", [128, 32, 16], F32)
    gT = sb("gT4", [128, 32], F32)
    t8 = sb("t84", [128, 8], F32)
    wsl = sb("wsl4", [128, 32], F32)
    accs = sb("accs4", [128, 8, 128], F32)
    accden = sb("accden4", [128, 2], F32)
    qk = sb("qk4", [128, 8], F32)
    pnew = sb("pnew4", [128, 1], F32)
    rdn = sb("rdn4", [128, 1], F32)
    osT = sb("osT4", [128, 8], F32)
    rp_ps = [ps(f"rp_ps{i}", [128, 128]) for i in range(2)]
    gs_ps = ps("gs_ps", [128, 32, 16])
    ovA = ps("ovA", [128, 512])
    ovB = ps("ovB", [128, 512])
    den_ps = ps("den_ps", [128, 2])
    sn_ps = ps("sn_ps", [128, 1])
    tr_ps = ps("tr_ps", [128, 8])

    ckv = ck[:, :]
    cvv = cv[:, :]
    l0 = P.dma("sp", lambda e: e.dma_start(out=ptab[:], in_=ptrep_d[:, :]), "c0")
    l1 = P.dma("sp", lambda e: e.dma_start(out=iota[:], in_=iota_d[:, :]), "c1")
    l2 = P.dma("sp", lambda e: e.dma_start(out=dsel[:], in_=dsel_d[:, :, :]), "c2")
    l3 = P.dma("sp", lambda e: e.dma_start(out=identf[:], in_=ident_d[:, :]), "c3")
    m0 = P.op("dve", lambda e: e.memset(onesf[:], 1.0))
    m1 = P.op("dve", lambda e: e.memset(onesb[:], 1.0))
    ix = P.op("dve", lambda e: e.tensor_scalar(out=idx[:], in0=ptab[:], scalar1=128.0, scalar2=iota[:, 0:1], op0=ALU.mult, op1=ALU.add),
              waits=[l0, l1])
    P.wait("pool", [ix])
    P.wait("pe", [l3, m0, m1])
    P.wait("dve", [l2])
    kfree = [None] * 3
    vfree = [None] * 4
    pfree = [None] * 2
    qc_free = [None] * 2
    rp_free = [None] * 2
    kc = vc = pc = rc = 0
    prev_sample = None
    for b_ in range(4):
        rep_last = None
        for (srcT, dst) in ((qs, qrep), (vsT, vrep), (gasT, garep)):
            for h in range(8):
                s_ = rc % 2
                rc += 1
                a = P.op("dve", lambda e, s_=s_, srcT=srcT, h=h, b_=b_: e.tensor_scalar(
                    out=qcol[s_][:], in0=onesf[:], scalar1=srcT[:, h, b_:b_ + 1], scalar2=None, op0=ALU.mult), waits=[qc_free[s_], prev_sample])
                mm = P.op("pe", lambda e, s_=s_: e.matmul(rp_ps[s_][:, :], qcol[s_][:], identf[:], start=True, stop=True), waits=[a, rp_free[s_]])
                qc_free[s_] = mm
                cp = P.op("act", lambda e, s_=s_, dst=dst, h=h: e.activation(out=dst[:, h, :], in_=rp_ps[s_][:, :], func=AF.Identity), waits=[mm, prev_sample])
                rp_free[s_] = cp
                rep_last = cp
        red = None
        for pg in range(64):
            ks_ = kc % 3
            kc += 1
            col = 64 * b_ + pg
            dk = P.dma("pool", lambda e, ks_=ks_, col=col: e.indirect_dma_start(
                out=kpg[ks_][:], out_offset=None, in_=ckv, in_offset=bass.IndirectOffsetOnAxis(ap=idx[:, col:col + 1], axis=0)),
                f"kpg{ks_}", waits=[kfree[ks_]])
            pr = pc % 2
            pc += 1
            mu = P.op("dve", lambda e, ks_=ks_, pr=pr: e.tensor_tensor(out=prod[pr][:], in0=kpg[ks_][:].rearrange("p (h d) -> p h d", h=8), in1=qrep[:], op=ALU.mult),
                      waits=[dk, rep_last, pfree[pr]])
            kfree[ks_] = mu
            red = P.op("dve", lambda e, pr=pr, pg=pg: e.tensor_reduce(out=sc[:, pg // 2, 8 * (pg % 2):8 * (pg % 2) + 8], in_=prod[pr][:], axis=AX.X, op=ALU.add),
                       waits=[mu, prev_sample])
            pfree[pr] = red
        g1 = P.op("pe", lambda e: e.matmul(gs_ps[:, :, :], onesf[:], sc[:], start=True, stop=True), waits=[red, prev_sample])
        g2 = P.op("dve", lambda e: e.tensor_tensor(out=gsel[0:8], in0=gs_ps[0:8], in1=dsel[0:8], op=ALU.mult), waits=[g1])
        g3 = P.op("dve", lambda e: e.tensor_reduce(out=gT[0:8, :], in_=gsel[0:8], axis=AX.X, op=ALU.add), waits=[g2])
        g4 = P.op("dve", lambda e: e.max(out=t8[0:8, :], in_=gT[0:8, :]), waits=[g3])
        g5 = P.op("dve", lambda e: e.tensor_scalar(out=wsl[0:8, :], in0=gT[0:8, :], scalar1=t8[0:8, 2:3], scalar2=None, op0=ALU.is_ge), waits=[g4])
        ex = P.op("act", lambda e: e.activation(out=psc[:], in_=sc[:], func=AF.Exp, scale=SC), waits=[red, prev_sample])
        accd = None
        for n_ in range(32):
            dvs = []
            sl = []
            for e_ in range(2):
                vs_ = vc % 4
                vc += 1
                col = 64 * b_ + 2 * n_ + e_
                dvs.append(P.dma("pool", lambda e, vs_=vs_, col=col: e.indirect_dma_start(
                    out=vpg[vs_][:], out_offset=None, in_=cvv, in_offset=bass.IndirectOffsetOnAxis(ap=idx[:, col:col + 1], axis=0)),
                    f"vpg{vs_}", waits=[vfree[vs_]]))
                sl.append(vs_)
            last = None
            for (dst, lo) in ((ovA, 0), (ovB, 4)):
                for e_ in range(2):
                    last = P.op("pe", lambda e, dst=dst, lo=lo, e_=e_, n_=n_, v=sl[e_]: e.matmul(
                        dst[0:8, :], psc[:, n_, 8 * e_:8 * e_ + 8], vpg[v][:, 128 * lo:128 * lo + 512], start=(e_ == 0), stop=(e_ == 1)),
                        waits=[ex, dvs[0], dvs[1], accd] if (lo == 0 and e_ == 0) else [], sig=False)
            for e_ in range(2):
                last = P.op("pe", lambda e, e_=e_, n_=n_: e.matmul(den_ps[0:8, :], psc[:, n_, 8 * e_:8 * e_ + 8], onesb[:], start=(e_ == 0), stop=(e_ == 1)),
                            sig=(e_ == 1))
            vfree[sl[0]] = last
            vfree[sl[1]] = last
            if n_ == 0:
                a1 = P.op("dve", lambda e: e.tensor_scalar(out=accs[0:8, 0:4, :], in0=ovA[0:8, :].rearrange("p (h d) -> p h d", h=4), scalar1=wsl[0:8, 0:1], scalar2=None, op0=ALU.mult), waits=[last, g5, prev_sample])
                a2 = P.op("dve", lambda e: e.tensor_scalar(out=accs[0:8, 4:8, :], in0=ovB[0:8, :].rearrange("p (h d) -> p h d", h=4), scalar1=wsl[0:8, 0:1], scalar2=None, op0=ALU.mult), waits=[a1])
                accd = P.op("dve", lambda e: e.tensor_scalar(out=accden[0:8, :], in0=den_ps[0:8, :], scalar1=wsl[0:8, 0:1], scalar2=None, op0=ALU.mult), waits=[a2])
            else:
                a1 = P.op("dve", lambda e, n_=n_: e.scalar_tensor_tensor(out=accs[0:8, 0:4, :], in0=ovA[0:8, :].rearrange("p (h d) -> p h d", h=4), scalar=wsl[0:8, n_:n_ + 1], in1=accs[0:8, 0:4, :],
                                                                        op0=ALU.mult, op1=ALU.add), waits=[last, accd])
                a2 = P.op("dve", lambda e, n_=n_: e.scalar_tensor_tensor(out=accs[0:8, 4:8, :], in0=ovB[0:8, :].rearrange("p (h d) -> p h d", h=4), scalar=wsl[0:8, n_:n_ + 1], in1=accs[0:8, 4:8, :],
                                                                        op0=ALU.mult, op1=ALU.add), waits=[a1])
                accd = P.op("dve", lambda e, n_=n_: e.scalar_tensor_tensor(out=accden[0:8, :], in0=den_ps[0:8, :], scalar=wsl[0:8, n_:n_ + 1], in1=accden[0:8, :],
                                                                          op0=ALU.mult, op1=ALU.add), waits=[a2])
        n1 = P.op("dve", lambda e, b_=b_: e.tensor_tensor(out=qk[:], in0=qs[:, :, b_], in1=ksT[:, :, b_], op=ALU.mult), waits=[prev_sample])
        n2 = P.op("pe", lambda e: e.matmul(sn_ps[0:8, :], qk[:], onesf[:, 0:1], start=True, stop=True), waits=[n1])
        n3 = P.op("act", lambda e: e.activation(out=pnew[0:8, :], in_=sn_ps[0:8, :], func=AF.Exp, scale=SC), waits=[n2])
        n4 = P.op("dve", lambda e: e.scalar_tensor_tensor(out=accs[0:8], in0=vrep[0:8], scalar=pnew[0:8, 0:1],
                                                         in1=accs[0:8], op0=ALU.mult, op1=ALU.add), waits=[n3, accd, rep_last])
        n5 = P.op("dve", lambda e: e.tensor_tensor(out=accden[0:8, 0:1], in0=accden[0:8, 0:1], in1=pnew[0:8, 0:1], op=ALU.add), waits=[n4])
        n6 = P.op("dve", lambda e: e.reciprocal(out=rdn[0:8, :], in_=accden[0:8, 0:1]), waits=[n5])
        n7 = P.op("dve", lambda e: e.scalar_tensor_tensor(out=accs[0:8], in0=accs[0:8], scalar=rdn[0:8, 0:1], in1=garep[0:8],
                                                         op0=ALU.mult, op1=ALU.mult), waits=[n6])
        if dbgfn is not None and b_ == 0:
            dd_ = dbgfn(dict(idx=idx, qrep=qrep, vrep=vrep, garep=garep, sc=sc, gT=gT, t8=t8, wsl=wsl, accs=accs, accden=accden, pnew=pnew, rdn=rdn, kpg0=kpg[0]), n7)
            for en_ in ("dve", "act", "pool", "pe"):
                P.wait(en_, dd_)
        cpl = None
        for h in range(8):
            tp = P.op("pe", lambda e, h=h: e.transpose(tr_ps[:, :], accs[0:8, h, :], identf[0:8, 0:8]), waits=[n7, cpl])
            cpl = P.op("act", lambda e, h=h: e.activation(out=osT[:, h:h + 1], in_=tr_ps[:, h:h + 1], func=AF.Identity), waits=[tp, prev_sample])
        prev_sample = write_out(b_, osT, cpl)
    return prev_sample


def build():
    from contextlib import ExitStack
    nc = bass.Bass("TRN2", target_bir_lowering=False)
    dt = nc.dram_tensor

    def din(name, shape, dtype=F32):
        return dt(name, list(shape), dtype, kind="ExternalInput").ap()

    def dout(name, shape, dtype=F32):
        return dt(name, list(shape), dtype, kind="ExternalOutput").ap()

    xoT = din("xoT", [D, NT])
    w_in = din("w_in", [D, 7168])
    bcol = din("bcol", [128, 56])
    brep = din("brep", [128, 2048])
    hmask = din("hmask", [128, 4])
    xwT = din("xwT", [D, 4096])
    candb = din("candb", [128, 4, 16])
    cand01 = din("cand01", [128, 4, 16])
    own01 = din("own01", [128, 4, 16])
    negmask_d = din("negmask", [128, 512])
    ident_d = din("ident", [128, 128])
    kT_scr = dt("kT_scr", [8, 128, 4096], BF16, kind="Internal").ap()
    v_scr = dt("v_scr", [4096, 1024], BF16, kind="Internal").ap()
    mix_dbg = dout("mix_dbg", [128, 16, NCc], BF16)
    u_scr = dt("u_scr", [128, 8, NT], F32, kind="Internal").ap()
    wdw_d = din("wdw", [128, 8, 31])
    vec8 = din("vec8", [128, 4, 8])
    vec16 = din("vec16", [128, 4, 16])
    stT_d = din("stT", [128, 4, 8, 30])
    w_pw = din("w_pw", [1024, 1024])
    w_out = din("w_out", [D, D])
    w_pg = din("w_pg", [D, D])
    w_pe = din("w_pe", [256, D])
    xcT = din("xcT", [D, NCc])
    pT = din("pT", [256, NCc])
    yT_out = dout("yT_out", [D, NCc])
    ck_d = din("ck", [2560 * 128, 1024])
    cv_d = din("cv", [2560 * 128, 1024])
    ptrep_d = din("ptrep", [128, 256], I32)
    iota_d = din("iota", [128, 1])
    dsel_d = din("dsel", [128, 32, 16])
    st_raw = din("st_raw", [4, 30, 1024])
    cs_out = dout("cs_out", [4, 29, 1024])

    kT_out = dout("kT_out", [8, 128, NT])
    v_out = dout("v_out", [1024, 1024])
    vsT_out = dout("vsT_out", [128, 8, 4])
    uT_out = dout("uT_out", [128, 8, 34])

    es = ExitStack()
    with es:
        P = Prog(nc, es)

        def sb(name, shape, dtype):
            return es.enter_context(nc.sbuf_tensor(name, list(shape), dtype))

        mixT = sb("mixT", [128, 16, NCc], BF16)
        mid = ExitStack()

        def sbm(name, shape, dtype):
            return mid.enter_context(nc.sbuf_tensor(name, list(shape), dtype))
        qs = sb("qs", [128, 8, 4], F32)
        gasT = sb("gasT", [128, 8, 4], F32)
        vsT = sb("vsT", [128, 8, 4], F32)
        ksT = sb("ksT", [128, 8, 4], F32)
        bcol_t = sb("bcol_t", [128, 56], F32)
        brep_t = sb("brep_t", [128, 2048], F32)
        hmask_t = sb("hmask_t", [128, 4], F32)
        qT = sbm("qT", [128, 8, NCc], BF16)
        ga = sbm("ga", [128, 8, 1024], BF16)
        gcT = sbm("gcT", [128, 8, NCc], BF16)

        with ExitStack() as ps1:
            def sb1(name, shape, dtype):
                return ps1.enter_context(nc.sbuf_tensor(name, list(shape), dtype))
            xo = sb1("xo", [128, 16, NT], BF16)
            uT = sb1("uT", [128, 8, NT], F32)
            wt = [sb1(f"wt{i}", [128, 16, 512], BF16) for i in range(2)]
            kst = [sb1(f"kst{i}", [128, 292], F32) for i in range(2)]
            vst = [sb1(f"vst{i}", [128, 512], F32) for i in range(2)]
            sg = [sb1(f"sg{i}", [128, 292], F32) for i in range(2)]
            gtmp = [sb1(f"gtmp{i}", [128, 512], F32) for i in range(2)]
            pbank = [ps1.enter_context(nc.psum_tensor(f"p1b{i}", [128, 512], F32)) for i in range(4)]

            w_in_v = w_in.rearrange("(k p) n -> p k n", p=128)
            d_xo = P.dma("pool", lambda e: e.dma_start(out=xo[:], in_=xoT.rearrange("(k p) n -> p k n", p=128)), "xo")
            d_bc = P.dma("sp", lambda e: e.dma_start(out=bcol_t[:], in_=bcol[:, :]), "c0")
            d_br = P.dma("sp", lambda e: e.dma_start(out=brep_t[:], in_=brep[:, :]), "c1")
            d_hm = P.dma("sp", lambda e: e.dma_start(out=hmask_t[:], in_=hmask[:, :]), "c2")

            wt_free = [None] * 2
            bank_free = [None] * 4
            kst_free = [None] * 2
            vst_free = [None] * 2
            sg_free = [None] * 2
            gtmp_free = [None] * 2
            grp = [0]
            cnt = {"k": 0, "v": 0, "sg": 0, "g": 0}
            out_deps = []
            u_last = [None] * 8

            def mm_group(lhs_fn, rhs_fn, n_out, w_dep):
                b = grp[0] % 4
                grp[0] += 1
                dep = None
                for k in range(16):
                    waits = []
                    if k == 0:
                        waits = [w_dep, d_xo, bank_free[b]]
                    last = (k == 15)
                    dep = P.op("pe", (lambda e, k=k, b=b: e.matmul(pbank[b][:, 0:n_out], lhs_fn(k), rhs_fn(k), start=(k == 0), stop=(k == 15))),
                               waits=waits, sig=last)
                return b, dep

            for t in range(14):
                s = t % 2
                d_w = P.dma("pool", (lambda e, t=t, s=s: e.dma_start(out=wt[s][:], in_=w_in_v[:, :, t * 512:(t + 1) * 512])),
                            f"wt{s}", waits=[wt_free[s]])
                role = ["q", "k", "v", "ga", "a", "bg", "gc"][t // 2]
                last_pe = None
                if role in ("q", "k", "a", "bg", "gc"):
                    for cbl in range(4):
                        cb = 4 * t + cbl
                        hh = cb % 8
                        for i in range(4):
                            n = 292 if i == 3 else 288
                            c0 = 288 * i
                            b, dep = mm_group(lambda k, s=s, cbl=cbl: wt[s][:, k, cbl * 128:(cbl + 1) * 128],
                                              lambda k, c0=c0, n=n: xo[:, k, c0:c0 + n], n, d_w)
                            last_pe = dep
                            bias = bcol_t[:, cb:cb + 1]
                            if role == "q":
                                ev = P.op("act", lambda e, b=b, hh=hh, i=i, bias=bias: e.activation(
                                    out=qT[:, hh, 256 * i:256 * i + 256], in_=pbank[b][:, 32:288], func=AF.Identity, bias=bias, scale=1.0),
                                    waits=[dep, d_bc])
                                if i == 3:
                                    P.op("act", lambda e, b=b, hh=hh, bias=bias: e.activation(
                                        out=qT[:, hh, 1024:1028], in_=pbank[b][:, 288:292], func=AF.Identity, bias=bias, scale=1.0))
                                    ev = P.op("act", lambda e, b=b, hh=hh, bias=bias: e.activation(
                                        out=qs[:, hh, :], in_=pbank[b][:, 288:292], func=AF.Identity, bias=bias, scale=1.0))
                                bank_free[b] = ev
                            elif role == "k":
                                ks = cnt["k"] % 2
                                cnt["k"] += 1
                                ev = P.op("act", lambda e, b=b, ks=ks, n=n, bias=bias: e.activation(
                                    out=kst[ks][:, 0:n], in_=pbank[b][:, 0:n], func=AF.Identity, bias=bias, scale=1.0),
                                    waits=[dep, d_bc, kst_free[ks]])
                                if i == 3:
                                    ev2 = P.op("act", lambda e, b=b, hh=hh, bias=bias: e.activation(
                                        out=ksT[:, hh, :], in_=pbank[b][:, 288:292], func=AF.Identity, bias=bias, scale=1.0))
                                    bank_free[b] = ev2
                                else:
                                    bank_free[b] = ev
                                dd = P.dma("sp", lambda e, ks=ks, hh=hh, c0=c0, n=n: e.dma_start(
                                    out=kT_out[hh, :, c0:c0 + n], in_=kst[ks][:, 0:n]), f"kst{ks}", waits=[ev])
                                kst_free[ks] = dd
                            elif role == "a":
                                ev = P.op("act", lambda e, b=b, hh=hh, c0=c0, n=n, bias=bias: e.activation(
                                    out=uT[:, hh, c0:c0 + n], in_=pbank[b][:, 0:n], func=AF.Identity, bias=bias, scale=1.0),
                                    waits=[dep, d_bc])
                                bank_free[b] = ev
                            elif role == "bg":
                                ss = cnt["sg"] % 2
                                cnt["sg"] += 1
                                ev = P.op("act", lambda e, b=b, ss=ss, n=n, bias=bias: e.activation(
                                    out=sg[ss][:, 0:n], in_=pbank[b][:, 0:n], func=AF.Sigmoid, bias=bias, scale=1.0),
                                    waits=[dep, d_bc, sg_free[ss]])
                                bank_free[b] = ev
                                mu = P.op("dve", lambda e, ss=ss, hh=hh, c0=c0, n=n: e.tensor_tensor(
                                    out=uT[:, hh, c0:c0 + n], in0=uT[:, hh, c0:c0 + n], in1=sg[ss][:, 0:n], op=ALU.mult),
                                    waits=[ev])
                                sg_free[ss] = mu
                                u_last[hh] = mu
                            elif role == "gc":
                                ev = P.op("act", lambda e, b=b, hh=hh, i=i, bias=bias: e.activation(
                                    out=gcT[:, hh, 256 * i:256 * i + 256], in_=pbank[b][:, 32:288], func=AF.Silu, bias=bias, scale=1.0),
                                    waits=[dep, d_bc])
                                if i == 3:
                                    ev = P.op("act", lambda e, b=b, hh=hh, bias=bias: e.activation(
                                        out=gcT[:, hh, 1024:1028], in_=pbank[b][:, 288:292], func=AF.Silu, bias=bias, scale=1.0))
                                bank_free[b] = ev
                else:
                    half = t % 2
                    boff = (0 if role == "v" else 1024) + half * 512
                    for tt in range(8):
                        i, hf = tt // 2, tt % 2
                        c0 = 288 * i + 32 + 128 * hf
                        b, dep = mm_group(lambda k, c0=c0: xo[:, k, c0:c0 + 128],
                                          lambda k, s=s: wt[s][:, k, :], 512, d_w)
                        last_pe = dep
                        if role == "v":
                            vs = cnt["v"] % 2
                            cnt["v"] += 1
                            ev = P.op("dve", lambda e, b=b, vs=vs, boff=boff: e.tensor_tensor(
                                out=vst[vs][:], in0=pbank[b][:, :], in1=brep_t[:, boff:boff + 512], op=ALU.add),
                                waits=[dep, d_br, vst_free[vs]])
                            bank_free[b] = ev
                            dd = P.dma("sp", lambda e, vs=vs, tt=tt, half=half: e.dma_start(
                                out=v_out[tt * 128:(tt + 1) * 128, half * 512:(half + 1) * 512], in_=vst[vs][:]),
                                f"vst{vs}", waits=[ev])
                            vst_free[vs] = dd
                        else:
                            gs = cnt["g"] % 2
                            cnt["g"] += 1
                            ev = P.op("dve", lambda e, b=b, gs=gs, boff=boff: e.tensor_tensor(
                                out=gtmp[gs][:], in0=pbank[b][:, :], in1=brep_t[:, boff:boff + 512], op=ALU.add),
                                waits=[dep, d_br, gtmp_free[gs]])
                            bank_free[b] = ev
                            a2 = P.op("act", lambda e, gs=gs, tt=tt, half=half: e.activation(
                                out=ga[:, tt, half * 512:(half + 1) * 512], in_=gtmp[gs][:], func=AF.Silu),
                                waits=[ev])
                            gtmp_free[gs] = a2
                    for cbl in range(4):
                        cb = 4 * t + cbl
                        hh = cb % 8
                        b, dep = mm_group(lambda k, s=s, cbl=cbl: wt[s][:, k, cbl * 128:(cbl + 1) * 128],
                                          lambda k: xo[:, k, 1152:1156], 4, d_w)
                        last_pe = dep
                        bias = bcol_t[:, cb:cb + 1]
                        if role == "v":
                            ev = P.op("act", lambda e, b=b, hh=hh, bias=bias: e.activation(
                                out=vsT[:, hh, :], in_=pbank[b][:, 0:4], func=AF.Identity, bias=bias, scale=1.0),
                                waits=[dep, d_bc])
                        else:
                            ev = P.op("act", lambda e, b=b, hh=hh, bias=bias: e.activation(
                                out=gasT[:, hh, :], in_=pbank[b][:, 0:4], func=AF.Silu, bias=bias, scale=1.0),
                                waits=[dep, d_bc])
                        bank_free[b] = ev
                wt_free[s] = last_pe

            hm = None
            for i in range(4):
                hm = P.op("dve", lambda e, i=i: e.tensor_scalar(
                    out=uT[:, :, 288 * i:288 * i + 32], in0=uT[:, :, 288 * i:288 * i + 32],
                    scalar1=hmask_t[:, i:i + 1], scalar2=None, op0=ALU.mult),
                    waits=[d_hm] + [u for u in u_last])
            fin = []
            fin.append(P.dma("sp", lambda e: e.dma_start(out=uT_out[:, :, 0:30], in_=uT[:, :, 1122:1152]), "fo0", waits=[hm]))
            fin.append(P.dma("sp", lambda e: e.dma_start(out=uT_out[:, :, 30:34], in_=uT[:, :, 1152:1156]), "fo1", waits=[hm]))
            fin.append(P.dma("sp", lambda e: e.dma_start(out=u_scr[:, :, :], in_=uT[:]), "fo3", waits=[hm]))
            fin.append(P.dma("sp", lambda e: e.dma_start(out=cs_out[:, :, :], in_=st_raw[:, 1:30, :]), "fo4"))
            lastact = P.op("act", lambda e: e.activation(out=sg[0][:, 0:4], in_=vsT[:, 0, :], func=AF.Identity), waits=[sg_free[0], sg_free[1]])
            fin.append(P.dma("sp", lambda e: e.dma_start(out=vsT_out[:, :, :], in_=vsT[:]), "fo2", waits=[lastact]))
            P.wait("sp", fin + [kst_free[0], kst_free[1], vst_free[0], vst_free[1]])
            P.wait("act", [gtmp_free[0], gtmp_free[1], lastact])
            P.wait("dve", [hm])
            with nc.Block() as blk:
                P.emit(blk)

        kmT = sbm("kmT", [128, 8, 16], BF16)
        ksum = sbm("ksum", [128, 8, 16], F32)
        with ExitStack() as ps2:
            def sb2(name, shape, dtype):
                return ps2.enter_context(nc.sbuf_tensor(name, list(shape), dtype))
            wk = sb2("wk", [128, 16, 1024], BF16)
            wv = sb2("wv", [128, 16, 1024], BF16)
            xw = [sb2(f"xw{i}", [128, 16, 512], BF16) for i in range(2)]
            kstg = [sb2(f"kstg{i}", [128, 8, 512], BF16) for i in range(1)]
            vstg = [sb2(f"vstg{i}", [128, 4, 1024], BF16) for i in range(1)]
            pb2 = [ps2.enter_context(nc.psum_tensor(f"p2b{i}", [128, 512], F32)) for i in range(4)]
            xwT_v = xwT.rearrange("(k p) n -> p k n", p=128)
            d_wk = P.dma("pool", lambda e: e.dma_start(out=wk[:], in_=w_in_v[:, :, 1024:2048]), "wk")
            d_x = [None] * 8
            d_x[0] = P.dma("pool", lambda e: e.dma_start(out=xw[0][:], in_=xwT_v[:, :, 0:512]), "xw0")
            d_wv = P.dma("pool", lambda e: e.dma_start(out=wv[:], in_=w_in_v[:, :, 2048:3072]), "wv")
            xw_free = [None, None]
            kstg_free = [None, None]
            vstg_free = [None, None]
            bfree = [None] * 4
            g2 = [0]
            red_last = None

            def mm2(lhs_fn, rhs_fn, waits):
                b = g2[0] % 4
                g2[0] += 1
                dep = None
                for k in range(16):
                    dep = P.op("pe", (lambda e, k=k, b=b: e.matmul(pb2[b][:, :], lhs_fn(k), rhs_fn(k), start=(k == 0), stop=(k == 15))),
                               waits=(list(waits) + [bfree[b]]) if k == 0 else [], sig=(k == 15))
                return b, dep

            for c in range(8):
                s_ = c % 2
                if c + 1 < 8:
                    d_x[c + 1] = P.dma("pool", (lambda e, c=c: e.dma_start(out=xw[(c + 1) % 2][:], in_=xwT_v[:, :, 512 * (c + 1):512 * (c + 2)])),
                                       f"xw{(c + 1) % 2}", waits=[xw_free[(c + 1) % 2]])
                evs = []
                for h in range(8):
                    b, dep = mm2(lambda k, h=h: wk[:, k, 128 * h:128 * h + 128], lambda k, s_=s_: xw[s_][:, k, :], [d_wk, d_x[c]])
                    ev = P.op("act", lambda e, b=b, s_=s_, h=h: e.activation(
                        out=kstg[0][:, h, :], in_=pb2[b][:, :], func=AF.Identity, bias=bcol_t[:, 8 + h:9 + h], scale=1.0),
                        waits=[dep, kstg_free[0]])
                    bfree[b] = ev
                    evs.append(ev)
                r = None
                for blk_ in range(2):
                    r = P.op("dve", lambda e, s_=s_, c=c, blk_=blk_: e.tensor_reduce(
                        out=ksum[:, :, 2 * c + blk_], in_=kstg[0][:, :, 256 * blk_:256 * blk_ + 256], axis=AX.X, op=ALU.add),
                        waits=[evs[-1]])
                red_last = r
                dk = P.dma("sp", lambda e, s_=s_, c=c: e.dma_start(
                    out=kT_scr.rearrange("h d t -> d h t")[:, :, 512 * c:512 * c + 512], in_=kstg[0][:]), "kstg0", waits=[evs[-1]])
                kstg_free[0] = dk
                P.wait("act", [r]) if False else None
                kred = r
                vev = None
                lastpe = None
                for tt in range(4):
                    for half in range(2):
                        b, dep = mm2(lambda k, s_=s_, tt=tt: xw[s_][:, k, 128 * tt:128 * tt + 128],
                                     lambda k, half=half: wv[:, k, 512 * half:512 * half + 512], [d_wv, d_x[c]])
                        lastpe = dep
                        vev = P.op("dve", lambda e, b=b, s_=s_, tt=tt, half=half: e.tensor_tensor(
                            out=vstg[0][:, tt, 512 * half:512 * half + 512], in0=pb2[b][:, :], in1=brep_t[:, 512 * half:512 * half + 512], op=ALU.add),
                            waits=[dep, vstg_free[0]])
                        bfree[b] = vev
                dv = P.dma("sp", lambda e, s_=s_, c=c: e.dma_start(
                    out=v_scr[512 * c:512 * c + 512, :].rearrange("(t p) n -> p t n", p=128), in_=vstg[0][:]), "vstg0", waits=[vev])
                vstg_free[0] = dv
                xw_free[s_] = lastpe
                P.wait("act", [kred])
            km = P.op("dve", lambda e: e.tensor_scalar(out=kmT[:], in0=ksum[:], scalar1=1.0 / 256.0, scalar2=None, op0=ALU.mult),
                      waits=[red_last])
            P.wait("sp", [kstg_free[0], vstg_free[0]])
            P.wait("dve", [km])
            with nc.Block() as blk:
                P.emit(blk)

        with ExitStack() as ps3:
            def sb3(name, shape, dtype):
                return ps3.enter_context(nc.sbuf_tensor(name, list(shape), dtype))
            KTh = [sb3(f"KTh{i}", [128, 4096], BF16) for i in range(2)]
            Vh = [sb3(f"Vh{i}", [128, 32, 130], BF16) for i in range(2)]
            Pt = [sb3(f"Pt{i}", [128, 512], BF16) for i in range(3)]
            gm = [sb3(f"gm{i}", [128, 16], F32) for i in range(2)]
            top8 = [sb3(f"top8{i}", [128, 8], F32) for i in range(2)]
            wsel = [sb3(f"wsel{i}", [128, 2, 16], F32) for i in range(2)]
            acc = [sb3(f"acc{i}", [128, 2, 129], F32) for i in range(2)]
            rden = [sb3(f"rden{i}", [128, 2], F32) for i in range(2)]
            attn_n = [sb3(f"attn_n{i}", [128, 128], F32) for i in range(2)]
            candb_t = sb3("candb_t", [128, 4, 16], F32)
            cand01_t = sb3("cand01_t", [128, 4, 16], F32)
            own01_t = sb3("own01_t", [128, 4, 16], F32)
            negm = sb3("negm", [128, 512], BF16)
            identb = sb3("identb", [128, 128], BF16)
            identf = sb3("identf", [128, 128], F32)
            gps = ps3.enter_context(nc.psum_tensor("gps", [128, 32], F32))
            Sps = [ps3.enter_context(nc.psum_tensor(f"Sps{i}", [128, 512], F32)) for i in range(2)]
            Ops = [ps3.enter_context(nc.psum_tensor(f"Ops{i}", [128, 2, 256], F32)) for i in range(2)]
            Tps = [ps3.enter_context(nc.psum_tensor(f"Tps{i}", [128, 128], F32)) for i in range(2)]

            cdeps = [
                P.dma("sp", lambda e: e.dma_start(out=candb_t[:], in_=candb[:, :, :]), "c0"),
                P.dma("sp", lambda e: e.dma_start(out=cand01_t[:], in_=cand01[:, :, :]), "c1"),
                P.dma("sp", lambda e: e.dma_start(out=own01_t[:], in_=own01[:, :, :]), "c2"),
                P.dma("sp", lambda e: e.dma_start(out=identf[:], in_=ident_d[:, :]), "c3"),
                P.dma("pool", lambda e: e.dma_start(out=negm[:], in_=negmask_d[:, :]), "c4"),
                P.dma("pool", lambda e: e.dma_start(out=identb[:], in_=ident_d[:, :]), "c5"),
            ]
            ones_dep = [P.op("dve", lambda e, i=i: e.memset(Vh[i][:, :, 128:130], 1.0)) for i in range(2)]
            P.wait("dve", cdeps[0:3])
            P.wait("pe", cdeps[3:6])
            kv_free = [None, None]
            S_free = [None, None]
            O_free = [None, None]
            T_free = [None, None]
            Pt_free = [None, None, None]
            an_free = [None, None]
            gps_free = None
            cS = cO = cP = cT = 0
            SCALE = 128.0 ** -0.5
            mix_last = None
            qb = 0
            for h in range(8):
                s_ = h % 2
                d_k = P.dma("sp", lambda e, s_=s_, h=h: e.dma_start(out=KTh[s_][:], in_=kT_scr[h]), f"KTh{s_}", waits=[kv_free[s_]])
                d_v = P.dma("sp", lambda e, s_=s_, h=h: e.dma_start(
                    out=Vh[s_][:, :, 0:128], in_=v_scr.rearrange("(t p) n -> p t n", p=128)[:, :, 128 * h:128 * h + 128]),
                    f"Vh{s_}", waits=[kv_free[s_]])
                last_pe_h = None
                for i in range(4):
                    par = qb % 2
                    qb += 1
                    qc = 256 * i
                    gdep = None
                    for t in range(2):
                        gdep = P.op("pe", lambda e, t=t, h=h, qc=qc: e.matmul(
                            gps[:, 16 * t:16 * t + 16], qT[:, h, qc + 128 * t:qc + 128 * t + 128], kmT[:, h, :], start=True, stop=True),
                            waits=[gps_free] if t == 0 else [])
                    wdeps = []
                    for t in range(2):
                        a1 = P.op("dve", lambda e, t=t, i=i: e.tensor_tensor(out=gm[t][:], in0=gps[:, 16 * t:16 * t + 16], in1=candb_t[:, i, :], op=ALU.add),
                                  waits=[gdep])
                        a2 = P.op("dve", lambda e, t=t: e.max(out=top8[t][:], in_=gm[t][:]), waits=[a1])
                        a3 = P.op("dve", lambda e, t=t, i=i, par=par: e.scalar_tensor_tensor(
                            out=wsel[par][:, t, :], in0=gm[t][:], scalar=top8[t][:, 2:3], in1=cand01_t[:, i, :], op0=ALU.is_ge, op1=ALU.mult),
                            waits=[a2])
                        a4 = P.op("dve", lambda e, t=t, i=i, par=par: e.tensor_tensor(
                            out=wsel[par][:, t, :], in0=wsel[par][:, t, :], in1=own01_t[:, i, :], op=ALU.add), waits=[a3])
                        wdeps.append(a4)
                        gps_free = a1
                    accdep = [None, None]
                    nblk = 4 * i + 4
                    for n_ in range(nblk):
                        own = (n_ == nblk - 1)
                        sbk = cS % 2
                        cS += 1
                        first_w = [S_free[sbk], d_k, ones_dep[s_]]
                        if own:
                            P.op("pe", lambda e, sbk=sbk: e.matmul(Sps[sbk][:, :], identb[:], negm[:], start=True, stop=False),
                                 waits=first_w, sig=False)
                        sdep = None
                        for kt in range(2):
                            sdep = P.op("pe", lambda e, sbk=sbk, kt=kt, n_=n_, s_=s_, h=h, qc=qc, own=own: e.matmul(
                                Sps[sbk][:, 256 * kt:256 * kt + 256], KTh[s_][:, 256 * n_ + 128 * kt:256 * n_ + 128 * kt + 128],
                                qT[:, h, qc:qc + 256], start=(not own), stop=((not own) or kt == 1)),
                                waits=(first_w if (kt == 0 and not own) else []), sig=(kt == 1))
                        pp = cP % 3
                        cP += 1
                        edep = P.op("act", lambda e, pp=pp, sbk=sbk: e.activation(out=Pt[pp][:], in_=Sps[sbk][:, :], func=AF.Exp, scale=SCALE),
                                    waits=[sdep, Pt_free[pp]])
                        S_free[sbk] = edep
                        ob = cO % 2
                        cO += 1
                        pvdep = None
                        for t in range(2):
                            for kt in range(2):
                                pvdep = P.op("pe", lambda e, ob=ob, t=t, kt=kt, pp=pp, s_=s_, n_=n_: e.matmul(
                                    Ops[ob][:, t, 0:129], Pt[pp][:, 256 * kt + 128 * t:256 * kt + 128 * t + 128],
                                    Vh[s_][:, 2 * n_ + kt, 0:129], start=(kt == 0), stop=(kt == 1)),
                                    waits=[edep, O_free[ob], d_v] if (t == 0 and kt == 0) else [], sig=(t == 1 and kt == 1))
                        Pt_free[pp] = pvdep
                        last_pe_h = pvdep
                        for t in range(2):
                            if n_ == 0:
                                accdep[t] = P.op("dve", lambda e, t=t, ob=ob, par=par: e.tensor_scalar(
                                    out=acc[par][:, t, :], in0=Ops[ob][:, t, 0:129], scalar1=wsel[par][:, t, 0:1], scalar2=None, op0=ALU.mult),
                                    waits=[pvdep, wdeps[t], an_free[par]])
                            else:
                                accdep[t] = P.op("dve", lambda e, t=t, ob=ob, par=par, n_=n_: e.scalar_tensor_tensor(
                                    out=acc[par][:, t, :], in0=Ops[ob][:, t, 0:129], scalar=wsel[par][:, t, n_:n_ + 1], in1=acc[par][:, t, :],
                                    op0=ALU.mult, op1=ALU.add), waits=[pvdep, accdep[t]])
                        O_free[ob] = accdep[1]
                    fin_last = None
                    for t in range(2):
                        r1 = P.op("dve", lambda e, t=t, par=par: e.reciprocal(out=rden[par][:, t:t + 1], in_=acc[par][:, t, 128:129]),
                                  waits=[accdep[t]])
                        asl = cT % 2
                        r2 = P.op("dve", lambda e, t=t, par=par, asl=asl, i=i, h=h: e.scalar_tensor_tensor(
                            out=attn_n[asl][:], in0=acc[par][:, t, 0:128], scalar=rden[par][:, t:t + 1], in1=ga[:, 2 * i + t, 128 * h:128 * h + 128],
                            op0=ALU.mult, op1=ALU.mult), waits=[r1, T_free[asl]])
                        tb = cT % 2
                        cT += 1
                        tp = P.op("pe", lambda e, tb=tb, asl=asl: e.transpose(Tps[tb][:, :], attn_n[asl][:], identf[:]),
                                  waits=[r2, T_free[tb]])
                        cp = P.op("act", lambda e, tb=tb, h=h, qc=qc, t=t: e.activation(
                            out=mixT[:, h, qc + 128 * t:qc + 128 * t + 128], in_=Tps[tb][:, :], func=AF.Identity), waits=[tp])
                        T_free[tb] = cp
                        mix_last = cp
                        fin_last = r2
                    an_free[par] = fin_last
                kv_free[s_] = last_pe_h
            mz = P.op("dve", lambda e: e.memset(mixT[:, 0:8, 1024:1028], 0.0))
            P.wait("act", [mix_last])
            P.wait("dve", [mz])
            with nc.Block() as blk:
                P.emit(blk)

        with ExitStack() as ps4:
            def write_out(b_, osT, dep):
                return P.op("act", lambda e, b_=b_: e.activation(out=mixT[:, 0:8, 1024 + b_], in_=osT[:, :], func=AF.Identity), waits=[dep])
            last4 = emit_p4(nc, P, ps4, ck_d, cv_d, ptrep_d, iota_d, dsel_d, ident_d, qs, ksT, vsT, gasT, write_out)
            P.wait("act", [last4])
            with nc.Block() as blk:
                P.emit(blk)

        CH = [(0, 344), (344, 342), (686, 342)]
        with ExitStack() as ps5:
            def sb5(name, shape, dtype):
                return ps5.enter_context(nc.sbuf_tensor(name, list(shape), dtype))
            uTb = sb5("uTb", [128, 8, 4, 288], F32)
            uTs = sb5("uTs", [128, 8, 4], F32)
            cTb = sb5("cTb", [128, 8, 4, 256], F32)
            cTs = sb5("cTs", [128, 8, 4], F32)
            cnT = sb5("cnT", [128, 8, NCc], BF16)
            wdw_t = sb5("wdw_t", [128, 8, 31], F32)
            vec8_t = sb5("vec8_t", [128, 4, 8], F32)
            stT = sb5("stT_sb", [128, 4, 8, 30], F32)
            stmp = sb5("stmp", [128, 8, 30], F32)
            s1t = sb5("s1t", [128, 8], F32)
            wpw = sb5("wpw", [128, 8, 1024], BF16)
            onesf = sb5("onesf", [128, 128], F32)
            epsT = sb5("epsT", [128, 1], F32)
            sqt = [sb5(f"sqt{i}", [128, 256], F32) for i in range(2)]
            mean_t = sb5("mean_t", [128, 256], F32)
            m2_t = sb5("m2_t", [128, 256], F32)
            rstd_t = sb5("rstd_t", [128, 256], F32)
            ntmp = [sb5(f"ntmp{i}", [128, 256], F32) for i in range(2)]
            S1p = ps5.enter_context(nc.psum_tensor("S1p", [128, 256], F32))
            S2p = ps5.enter_context(nc.psum_tensor("S2p", [128, 256], F32))
            pwb = [ps5.enter_context(nc.psum_tensor(f"pwb{i}", [128, 512], F32)) for i in range(2)]

            l_u = P.dma("sp", lambda e: e.dma_start(out=uTb[:], in_=u_scr[:, :, 0:1152].rearrange("p g (i c) -> p g i c", c=288)), "c0")
            l_us = P.dma("sp", lambda e: e.dma_start(out=uTs[:], in_=u_scr[:, :, 1152:1156]), "c1")
            l_w = P.dma("sp", lambda e: e.dma_start(out=wdw_t[:], in_=wdw_d[:, :, :]), "c2")
            l_v8 = P.dma("sp", lambda e: e.dma_start(out=vec8_t[:], in_=vec8[:, :, :]), "c3")
            l_st = P.dma("sp", lambda e: e.dma_start(out=stT[:], in_=stT_d[:, :, :, :]), "c4")
            l_pw = P.dma("pool", lambda e: e.dma_start(out=wpw[:], in_=w_pw.rearrange("(k p) n -> p k n", p=128)), "c5")
            m1 = P.op("dve", lambda e: e.memset(onesf[:], 1.0))
            m2 = P.op("dve", lambda e: e.memset(epsT[:], 1e-5))
            P.wait("dve", [l_u, l_us, l_w, l_v8, l_st])
            cdep = [None] * 8
            for g in range(8):
                d = P.op("dve", lambda e, g=g: e.tensor_scalar(
                    out=cTb[:, g, :, :], in0=uTb[:, g, :, 2:258], scalar1=wdw_t[:, g, 0:1], scalar2=vec8_t[:, 0, g:g + 1],
                    op0=ALU.mult, op1=ALU.add))
                for tap in range(1, 31):
                    d = P.op("dve", lambda e, g=g, tap=tap: e.scalar_tensor_tensor(
                        out=cTb[:, g, :, :], in0=uTb[:, g, :, 2 + tap:258 + tap], scalar=wdw_t[:, g, tap:tap + 1], in1=cTb[:, g, :, :],
                        op0=ALU.mult, op1=ALU.add), waits=[d])
                cdep[g] = d
            sdep = None
            for b_ in range(4):
                d1 = P.op("dve", lambda e, b_=b_: e.tensor_tensor(out=stmp[:], in0=stT[:, b_, :, :], in1=wdw_t[:, :, 0:30], op=ALU.mult), waits=[sdep])
                d2 = P.op("dve", lambda e: e.tensor_reduce(out=s1t[:], in_=stmp[:], axis=AX.X, op=ALU.add), waits=[d1])
                d3 = P.op("dve", lambda e, b_=b_: e.tensor_tensor(out=cTs[:, :, b_], in0=uTs[:, :, b_], in1=wdw_t[:, :, 30], op=ALU.mult), waits=[d2])
                d4 = P.op("dve", lambda e, b_=b_: e.tensor_tensor(out=cTs[:, :, b_], in0=cTs[:, :, b_], in1=s1t[:], op=ALU.add), waits=[d3])
                sdep = P.op("dve", lambda e, b_=b_: e.tensor_tensor(out=cTs[:, :, b_], in0=cTs[:, :, b_], in1=vec8_t[:, 0, :], op=ALU.add), waits=[d4])
            groups = [(lambda g, i=i: cTb[:, g, i, :], 256, 256 * i) for i in range(4)] + [(lambda g: cTs[:, g, :], 4, 1024)]
            sq_free = [None, None]
            st_free = None
            nt_free = [None, None]
            cnt5 = 0
            cn_last = None
            for (src, n, c0) in groups:
                mm = None
                for g in range(8):
                    sl = cnt5 % 2
                    cnt5 += 1
                    a_sq = P.op("act", lambda e, g=g, sl=sl, src=src, n=n: e.activation(out=sqt[sl][:, 0:n], in_=src(g), func=AF.Square),
                                waits=[cdep[g], sdep, sq_free[sl]])
                    P.op("pe", lambda e, g=g, src=src, n=n: e.matmul(S1p[:, 0:n], onesf[:], src(g), start=(g == 0), stop=(g == 7)),
                         waits=[cdep[g], sdep, m1, st_free] if g == 0 else [cdep[g]], sig=False)
                    mm = P.op("pe", lambda e, g=g, sl=sl, n=n: e.matmul(S2p[:, 0:n], onesf[:], sqt[sl][:, 0:n], start=(g == 0), stop=(g == 7)),
                              waits=[a_sq])
                    sq_free[sl] = mm
                e1 = P.op("act", lambda e, n=n: e.activation(out=mean_t[:, 0:n], in_=S1p[:, 0:n], func=AF.Identity, scale=1.0 / 1024.0), waits=[mm, cn_last])
                e2 = P.op("dve", lambda e, n=n: e.tensor_tensor(out=m2_t[:, 0:n], in0=mean_t[:, 0:n], in1=mean_t[:, 0:n], op=ALU.mult), waits=[e1, cn_last])
                e3 = P.op("dve", lambda e, n=n: e.scalar_tensor_tensor(out=m2_t[:, 0:n], in0=S2p[:, 0:n], scalar=1.0 / 1024.0, in1=m2_t[:, 0:n],
                                                                      op0=ALU.mult, op1=ALU.subtract), waits=[e2, mm])
                st_free = e3
                e4 = P.op("act", lambda e, n=n: e.activation(out=rstd_t[:, 0:n], in_=m2_t[:, 0:n], func=AF.Sqrt, bias=epsT[:, 0:1], scale=1.0), waits=[e3, m2])
                e5 = P.op("dve", lambda e, n=n: e.reciprocal(out=rstd_t[:, 0:n], in_=rstd_t[:, 0:n]), waits=[e4])
                for g in range(8):
                    sl = cnt5 % 2
                    cnt5 += 1
                    f1 = P.op("dve", lambda e, g=g, sl=sl, src=src, n=n: e.tensor_tensor(out=ntmp[sl][:, 0:n], in0=src(g), in1=mean_t[:, 0:n], op=ALU.subtract),
                              waits=[e5, nt_free[sl]])
                    f2 = P.op("dve", lambda e, sl=sl, n=n: e.tensor_tensor(out=ntmp[sl][:, 0:n], in0=ntmp[sl][:, 0:n], in1=rstd_t[:, 0:n], op=ALU.mult), waits=[f1])
                    f3 = P.op("act", lambda e, g=g, sl=sl, n=n, c0=c0: e.activation(
                        out=cnT[:, g, c0:c0 + n], in_=ntmp[sl][:, 0:n], func=AF.Silu, bias=vec8_t[:, 2, g:g + 1], scale=vec8_t[:, 1, g:g + 1]), waits=[f2])
                    nt_free[sl] = f3
                    cn_last = f3
            pw_free = [None, None]
            cntp = 0
            pw_last = None
            for cb in range(8):
                for (c0, n) in CH:
                    bsl = cntp % 2
                    cntp += 1
                    mm = None
                    for k in range(8):
                        mm = P.op("pe", lambda e, k=k, cb=cb, c0=c0, n=n, bsl=bsl: e.matmul(
                            pwb[bsl][:, 0:n], wpw[:, k, 128 * cb:128 * cb + 128], cnT[:, k, c0:c0 + n], start=(k == 0), stop=(k == 7)),
                            waits=[l_pw, cn_last, pw_free[bsl]] if k == 0 else [], sig=(k == 7))
                    pw_last = P.op("dve", lambda e, cb=cb, c0=c0, n=n, bsl=bsl: e.scalar_tensor_tensor(
                        out=mixT[:, 8 + cb, c0:c0 + n], in0=pwb[bsl][:, 0:n], scalar=vec8_t[:, 3, cb:cb + 1], in1=gcT[:, cb, c0:c0 + n],
                        op0=ALU.add, op1=ALU.mult), waits=[mm])
                    pw_free[bsl] = pw_last
            dbg = P.dma("sp", lambda e: e.dma_start(out=mix_dbg[:, :, :], in_=mixT[:]), "dbg", waits=[pw_last])
            P.wait("sp", [dbg])
            P.wait("act", [cn_last])
            P.wait("dve", [pw_last])
            with nc.Block() as blk:
                P.emit(blk)

        ALPHA = 2.0 ** 0.25
        mid.close()
        with ExitStack() as ps6:
            def sb6(name, shape, dtype):
                return ps6.enter_context(nc.sbuf_tensor(name, list(shape), dtype))
            rT = sb6("rT", [128, 16, NCc], F32)
            hbf = sb6("hbf", [128, 16, NCc], BF16)
            wo = [sb6(f"wo{i}", [128, 16, 512], BF16) for i in range(2)]
            wpe = sb6("wpe", [128, 2, D], BF16)
            pTb = sb6("pTb", [128, 2, NCc], BF16)
            vec16_t = sb6("vec16_t", [128, 4, 16], F32)
            xr = [sb6(f"xr{i}", [128, 344], F32) for i in range(2)]
            rtmp = [sb6(f"rtmp{i}", [128, 344], F32) for i in range(2)]
            sq6 = [sb6(f"sq6{i}", [128, 344], F32) for i in range(2)]
            onesf6 = sb6("onesf6", [128, 128], F32)
            eps6 = sb6("eps6", [128, 1], F32)
            mean6 = sb6("mean6", [128, NCc], F32)
            rstd6 = sb6("rstd6", [128, NCc], F32)
            sgm = [sb6(f"sgm{i}", [128, 344], F32) for i in range(2)]
            yst = [sb6(f"yst{i}", [128, 344], F32) for i in range(2)]
            S1 = [ps6.enter_context(nc.psum_tensor(f"S1_{i}", [128, 512], F32)) for i in range(3)]
            S2 = [ps6.enter_context(nc.psum_tensor(f"S2_{i}", [128, 512], F32)) for i in range(3)]
            mb = [ps6.enter_context(nc.psum_tensor(f"mb{i}", [128, 512], F32)) for i in range(2)]

            l_v16 = P.dma("sp", lambda e: e.dma_start(out=vec16_t[:], in_=vec16[:, :, :]), "c0")
            l_pe = P.dma("pool", lambda e: e.dma_start(out=wpe[:], in_=w_pe.rearrange("(k p) n -> p k n", p=128)), "c1")
            l_pt = P.dma("pool", lambda e: e.dma_start(out=pTb[:], in_=pT.rearrange("(k p) n -> p k n", p=128)), "c2")
            o1 = P.op("dve", lambda e: e.memset(onesf6[:], 1.0))
            o2 = P.op("dve", lambda e: e.memset(eps6[:], 1e-5))
            wo_free = [None, None]
            mb_free = [None, None]
            xr_free = [None, None]
            rt_free = [None, None]
            sq_free6 = [None, None]
            w_out_v = w_out.rearrange("(k p) n -> p k n", p=128)
            w_pg_v = w_pg.rearrange("(k p) n -> p k n", p=128)
            cnt6 = 0
            wt_i = 0
            stat_last = None
            for t in range(4):
                ws = wt_i % 2
                wt_i += 1
                d_w = P.dma("pool", lambda e, t=t, ws=ws: e.dma_start(out=wo[ws][:], in_=w_out_v[:, :, 512 * t:512 * t + 512]), f"wo{ws}", waits=[wo_free[ws]])
                lastpe = None
                for cbl in range(4):
                    cb = 4 * t + cbl
                    for ci, (c0, n) in enumerate(CH):
                        sl = cnt6 % 2
                        cnt6 += 1
                        d_x = P.dma("sp", lambda e, sl=sl, cb=cb, c0=c0, n=n: e.dma_start(out=xr[sl][:, 0:n], in_=xcT[128 * cb:128 * cb + 128, c0:c0 + n]),
                                    f"xr{sl}", waits=[xr_free[sl]])
                        mm = None
                        for k in range(16):
                            mm = P.op("pe", lambda e, k=k, ws=ws, cbl=cbl, c0=c0, n=n, sl=sl: e.matmul(
                                mb[sl][:, 0:n], wo[ws][:, k, 128 * cbl:128 * cbl + 128], mixT[:, k, c0:c0 + n], start=(k == 0), stop=(k == 15)),
                                waits=[d_w, mb_free[sl]] if k == 0 else [], sig=(k == 15))
                        lastpe = mm
                        r1 = P.op("dve", lambda e, sl=sl, n=n: e.scalar_tensor_tensor(
                            out=rtmp[sl][:, 0:n], in0=xr[sl][:, 0:n], scalar=ALPHA, in1=mb[sl][:, 0:n], op0=ALU.mult, op1=ALU.add),
                            waits=[mm, d_x, rt_free[sl]])
                        mb_free[sl] = r1
                        xr_free[sl] = r1
                        r2 = P.op("act", lambda e, sl=sl, cb=cb, c0=c0, n=n: e.activation(
                            out=rT[:, cb, c0:c0 + n], in_=rtmp[sl][:, 0:n], func=AF.Identity, bias=vec16_t[:, 0, cb:cb + 1], scale=1.0), waits=[r1, l_v16])
                        r3 = P.op("act", lambda e, sl=sl, cb=cb, n=n: e.activation(
                            out=sq6[sl][:, 0:n], in_=rtmp[sl][:, 0:n], func=AF.Square, bias=vec16_t[:, 0, cb:cb + 1], scale=1.0), waits=[sq_free6[sl]])
                        rt_free[sl] = r3
                        P.op("pe", lambda e, cb=cb, ci=ci, c0=c0, n=n: e.matmul(S1[ci][:, 0:n], onesf6[:], rT[:, cb, c0:c0 + n], start=(cb == 0), stop=(cb == 15)),
                             waits=[r2, o1], sig=False)
                        stat_last = P.op("pe", lambda e, cb=cb, ci=ci, n=n, sl=sl: e.matmul(S2[ci][:, 0:n], onesf6[:], sq6[sl][:, 0:n], start=(cb == 0), stop=(cb == 15)),
                                         waits=[r3])
                        sq_free6[sl] = stat_last
                wo_free[ws] = lastpe
            h_last = None
            for ci, (c0, n) in enumerate(CH):
                e1 = P.op("act", lambda e, ci=ci, c0=c0, n=n: e.activation(out=mean6[:, c0:c0 + n], in_=S1[ci][:, 0:n], func=AF.Identity, scale=1.0 / D), waits=[stat_last])
                e2 = P.op("dve", lambda e, c0=c0, n=n: e.tensor_tensor(out=rstd6[:, c0:c0 + n], in0=mean6[:, c0:c0 + n], in1=mean6[:, c0:c0 + n], op=ALU.mult), waits=[e1])
                e3 = P.op("dve", lambda e, ci=ci, c0=c0, n=n: e.scalar_tensor_tensor(out=rstd6[:, c0:c0 + n], in0=S2[ci][:, 0:n], scalar=1.0 / D, in1=rstd6[:, c0:c0 + n],
                                                                                    op0=ALU.mult, op1=ALU.subtract), waits=[e2, stat_last])
                e4 = P.op("act", lambda e, c0=c0, n=n: e.activation(out=rstd6[:, c0:c0 + n], in_=rstd6[:, c0:c0 + n], func=AF.Sqrt, bias=eps6[:, 0:1], scale=1.0), waits=[e3, o2])
                e5 = P.op("dve", lambda e, c0=c0, n=n: e.reciprocal(out=rstd6[:, c0:c0 + n], in_=rstd6[:, c0:c0 + n]), waits=[e4])
                for cb in range(16):
                    f1 = P.op("dve", lambda e, cb=cb, c0=c0, n=n: e.tensor_tensor(out=rT[:, cb, c0:c0 + n], in0=rT[:, cb, c0:c0 + n], in1=mean6[:, c0:c0 + n], op=ALU.subtract), waits=[e5])
                    f2 = P.op("dve", lambda e, cb=cb, c0=c0, n=n: e.tensor_tensor(out=rT[:, cb, c0:c0 + n], in0=rT[:, cb, c0:c0 + n], in1=rstd6[:, c0:c0 + n], op=ALU.mult), waits=[f1])
                    f3 = P.op("act", lambda e, cb=cb, c0=c0, n=n: e.activation(out=rT[:, cb, c0:c0 + n], in_=rT[:, cb, c0:c0 + n], func=AF.Identity,
                                                                               bias=vec16_t[:, 2, cb:cb + 1], scale=vec16_t[:, 1, cb:cb + 1]), waits=[f2])
                    h_last = P.op("act", lambda e, cb=cb, c0=c0, n=n: e.activation(out=hbf[:, cb, c0:c0 + n], in_=rT[:, cb, c0:c0 + n], func=AF.Identity), waits=[f3])
            sg_free = [None, None]
            ys_free = [None, None]
            for t in range(4):
                ws = wt_i % 2
                wt_i += 1
                d_w = P.dma("pool", lambda e, t=t, ws=ws: e.dma_start(out=wo[ws][:], in_=w_pg_v[:, :, 512 * t:512 * t + 512]), f"wo{ws}", waits=[wo_free[ws]])
                lastpe = None
                for cbl in range(4):
                    cb = 4 * t + cbl
                    for ci, (c0, n) in enumerate(CH):
                        mmA = None
                        for k in range(16):
                            mmA = P.op("pe", lambda e, k=k, ws=ws, cbl=cbl, c0=c0, n=n: e.matmul(
                                mb[0][:, 0:n], wo[ws][:, k, 128 * cbl:128 * cbl + 128], hbf[:, k, c0:c0 + n], start=(k == 0), stop=(k == 15)),
                                waits=[d_w, h_last, mb_free[0]] if k == 0 else [], sig=(k == 15))
                        mmB = None
                        for k in range(2):
                            mmB = P.op("pe", lambda e, k=k, cb=cb, c0=c0, n=n: e.matmul(
                                mb[1][:, 0:n], wpe[:, k, 128 * cb:128 * cb + 128], pTb[:, k, c0:c0 + n], start=(k == 0), stop=(k == 1)),
                                waits=[l_pe, l_pt, mb_free[1]] if k == 0 else [], sig=(k == 1))
                        lastpe = mmB
                        sl = cnt6 % 2
                        cnt6 += 1
                        g1 = P.op("act", lambda e, sl=sl, cb=cb, n=n: e.activation(out=sgm[sl][:, 0:n], in_=mb[0][:, 0:n], func=AF.Sigmoid,
                                                                                 bias=vec16_t[:, 3, cb:cb + 1], scale=1.0), waits=[mmA, sg_free[sl]])
                        mb_free[0] = g1
                        g2 = P.op("dve", lambda e, sl=sl, n=n: e.tensor_tensor(out=yst[sl][:, 0:n], in0=sgm[sl][:, 0:n], in1=mb[1][:, 0:n], op=ALU.mult),
                                  waits=[g1, mmB, ys_free[sl]])
                        mb_free[1] = g2
                        sg_free[sl] = g2
                        g3 = P.op("dve", lambda e, sl=sl, cb=cb, c0=c0, n=n: e.tensor_tensor(out=yst[sl][:, 0:n], in0=yst[sl][:, 0:n], in1=rT[:, cb, c0:c0 + n], op=ALU.add),
                                  waits=[g2])
                        dd = P.dma("sp", lambda e, sl=sl, cb=cb, c0=c0, n=n: e.dma_start(out=yT_out[128 * cb:128 * cb + 128, c0:c0 + n], in_=yst[sl][:, 0:n]),
                                   f"yst{sl}", waits=[g3])
                        ys_free[sl] = dd
                wo_free[ws] = lastpe
            P.wait("sp", [ys_free[0], ys_free[1]])
            P.wait("act", [h_last])
            with nc.Block() as blk:
                P.emit(blk)
    return nc


_NC_CACHE = {}


def _prep_core(c, inp):
    b, j = c // 4, c % 4
    x = inp["x_prompt"][b]
    xs = inp["x_sample"][4 * c:4 * c + 4, 0, :]
    xoT = np.zeros((D, NT), np.float32)
    hmask = np.ones((128, 4), np.float32)
    for i in range(4):
        g = 4 * i + j
        t0 = 256 * g
        if g > 0:
            xoT[:, 288 * i:288 * i + 32] = x[t0 - 32:t0].T
        else:
            hmask[:, i] = 0.0
        xoT[:, 288 * i + 32:288 * i + 288] = x[t0:t0 + 256].T
    xoT[:, 1152:1156] = xs.T
    npad = 3 - j
    xwT = np.zeros((D, 4096), np.float32)
    for n_ in range(16):
        g = n_ - npad
        if g >= 0:
            xwT[:, 256 * n_:256 * n_ + 256] = x[256 * g:256 * g + 256].T
    candb = np.full((128, 4, 16), -1e30, np.float32)
    cand01 = np.zeros((128, 4, 16), np.float32)
    own01 = np.zeros((128, 4, 16), np.float32)
    for i in range(4):
        candb[:, i, npad:4 * i + 3] = 0.0
        cand01[:, i, npad:4 * i + 3] = 1.0
        own01[:, i, 4 * i + 3] = 1.0
    xcT = np.zeros((D, NCc), np.float32)
    pTm = np.zeros((256, NCc), np.float32)
    pp = inp["p_prompt"][0, b]
    for i in range(4):
        g = 4 * i + j
        xcT[:, 256 * i:256 * i + 256] = x[256 * g:256 * g + 256].T
        pTm[:, 256 * i:256 * i + 256] = pp[256 * g:256 * g + 256].T
    xcT[:, 1024:1028] = xs.T
    pTm[:, 1024:1028] = inp["p_sample"][0, 4 * c:4 * c + 4, 0, :].T
    st = inp["state_conv"][0, 4 * c:4 * c + 4]
    stT = np.ascontiguousarray(st.reshape(4, 30, 8, 128).transpose(3, 0, 2, 1))
    ptrep = np.ascontiguousarray(np.broadcast_to(inp["page_table"][4 * c:4 * c + 4].reshape(1, 256), (128, 256)).astype(np.int32))
    return {"xoT": xoT, "hmask": hmask, "xwT": xwT, "candb": candb, "cand01": cand01, "own01": own01,
            "xcT": xcT, "pT": pTm, "stT": stT, "ptrep": ptrep, "st_raw": np.ascontiguousarray(st)}


def kernel(**inp):
    inp = {k: np.asarray(v) for k, v in inp.items()}
    if "nc" not in _NC_CACHE:
        _NC_CACHE["nc"] = build()
    nc = _NC_CACHE["nc"]
    b_in = inp["b_in"][0]
    bcol = np.ascontiguousarray(b_in.reshape(56, 128).T)
    brep = np.ascontiguousarray(np.broadcast_to(np.concatenate([b_in[2048:3072], b_in[3072:4096]])[None, :], (128, 2048)))
    kk = np.arange(128)[:, None]
    ss = np.arange(256)[None, :]
    negmask = np.concatenate([np.where(kk <= ss, 0.0, -1e30), np.where(kk + 128 <= ss, 0.0, -1e30)], axis=1).astype(np.float32)
    shared = {"w_in": np.ascontiguousarray(inp["w_in"][0]), "bcol": bcol, "brep": brep,
              "negmask": negmask, "ident": np.eye(128, dtype=np.float32)}
    def pcol(v, n):
        return v.reshape(n, 128).T
    shared["wdw"] = np.ascontiguousarray(inp["w_dw"][0].reshape(31, 8, 128).transpose(2, 1, 0))
    shared["vec8"] = np.ascontiguousarray(np.stack([pcol(inp[k][0], 8) for k in ("b_dw", "g_cn", "b_cn", "b_pw")], axis=1))
    shared["vec16"] = np.ascontiguousarray(np.stack([pcol(inp[k][0], 16) for k in ("b_out", "g_ln", "b_ln", "b_pg")], axis=1))
    for k in ("w_pw", "w_out", "w_pg", "w_pe"):
        shared[k] = np.ascontiguousarray(inp[k][0])
    shared["ck"] = inp["cache_k"][0].reshape(2560 * 128, 1024)
    shared["cv"] = inp["cache_v"][0].reshape(2560 * 128, 1024)
    shared["iota"] = np.arange(128, dtype=np.float32).reshape(128, 1)
    dsel = np.zeros((128, 32, 16), np.float32)
    for h in range(8):
        dsel[h, :, h] = 1.0
        dsel[h, :, 8 + h] = 1.0
    shared["dsel"] = dsel
    in_maps = []
    for c in range(8):
        m = dict(shared)
        m.update(_prep_core(c, inp))
        in_maps.append(m)
    res = run_bass_kernel_spmd(nc, in_maps, core_ids=list(range(8)))
    R = res.results

    y_prompt = np.zeros((2, 4096, 2048), np.float32)
    y_sample = np.zeros((32, 1, 2048), np.float32)
    k_p = np.zeros((1, 2, 4096, 8, 128), np.float32)
    v_p = np.zeros((1, 2, 4096, 8, 128), np.float32)
    c_p = np.zeros((1, 2, 30, 1024), np.float32)
    k_s = np.zeros((1, 32, 1, 8, 128), np.float32)
    v_s = np.zeros((1, 32, 1, 8, 128), np.float32)
    c_s = np.zeros((1, 32, 30, 1024), np.float32)
    for c in range(8):
        b, j = c // 4, c % 4
        r = R[c]
        kT = r["kT_out"]
        vv = r["v_out"]
        for i in range(4):
            g = 4 * i + j
            t0 = 256 * g
            k_p[0, b, t0:t0 + 256] = kT[:, :, 288 * i + 32:288 * i + 288].transpose(2, 0, 1)
            v_p[0, b, t0:t0 + 256] = vv[256 * i:256 * i + 256].reshape(256, 8, 128)
        k_s[0, 4 * c:4 * c + 4, 0] = kT[:, :, 1152:1156].transpose(2, 0, 1)
        v_s[0, 4 * c:4 * c + 4, 0] = r["vsT_out"].transpose(2, 1, 0)
        yT = r["yT_out"]
        for i in range(4):
            g = 4 * i + j
            y_prompt[b, 256 * g:256 * g + 256] = yT[:, 256 * i:256 * i + 256].T
        y_sample[4 * c:4 * c + 4, 0] = yT[:, 1024:1028].T
        u = r["uT_out"]
        if j == 3:
            c_p[0, b] = u[:, :, 0:30].transpose(2, 1, 0).reshape(30, 1024)
        c_s[0, 4 * c:4 * c + 4, 29] = u[:, :, 30:34].transpose(2, 1, 0).reshape(4, 1024)
    for c in range(8):
        c_s[0, 4 * c:4 * c + 4, 0:29] = R[c]["cs_out"]
    return (y_prompt, y_sample, k_p, v_p, c_p, k_s, v_s, c_s)
```

```python
import numpy as np
import concourse.bass as bass
import concourse.mybir as mybir
from concourse.bass_utils import run_bass_kernel_spmd

F32 = mybir.dt.float32
BF16 = mybir.dt.bfloat16
I32 = mybir.dt.int32
AF = mybir.ActivationFunctionType
ALU = mybir.AluOpType
AX = mybir.AxisListType

D = 2048
NT = 1156
NCc = 1028
ENG = ("pe", "act", "dve", "pool", "sp")


class Prog:
    def __init__(self, nc, es):
        self.nc = nc
        self.es = es
        self.sems = {}
        self.reset()

    def reset(self):
        self.ops = {k: [] for k in ENG}

    def _sem(self, name):
        if name not in self.sems:
            h = self.es.enter_context(self.nc.semaphore(name))
            self.sems[name] = [h, 0]
        return self.sems[name]

    def op(self, eng, fn, waits=(), sig=True):
        dep = None
        s = None
        if sig:
            s = self._sem("m_" + eng)
            s[1] += 1
            dep = (s[0], s[1])
        self.ops[eng].append((tuple(w for w in waits if w is not None), fn, s[0] if sig else None, 1))
        return dep

    def dma(self, eng, fn, semname, waits=()):
        s = self._sem("d_" + semname)
        s[1] += 16
        self.ops[eng].append((tuple(w for w in waits if w is not None), fn, s[0], 16))
        return (s[0], s[1])

    def wait(self, eng, waits):
        self.ops[eng].append((tuple(w for w in waits if w is not None), None, None, 0))

    def emit(self, blk):
        ops = self.ops

        def run(e, lst):
            seen = {}
            for waits, fn, sem, inc in lst:
                for (h, v) in waits:
                    k = id(h)
                    if seen.get(k, -1) >= v:
                        continue
                    seen[k] = v
                    e.wait_ge(h, v)
                if fn is not None:
                    ins = fn(e)
                    if sem is not None:
                        ins.then_inc(sem, inc)

        @blk.tensor
        def _(e):
            run(e, ops["pe"])

        @blk.scalar
        def _(e):
            run(e, ops["act"])

        @blk.vector
        def _(e):
            run(e, ops["dve"])

        @blk.gpsimd
        def _(e):
            run(e, ops["pool"])

        @blk.sync
        def _(e):
            run(e, ops["sp"])

        self.reset()


def emit_p4(nc, P, st4, ck, cv, ptrep_d, iota_d, dsel_d, ident_d, qs, ksT, vsT, gasT, write_out, dbgfn=None, holder=None, shared_ps=None):
    def sb(name, shape, dtype):
        return st4.enter_context(nc.sbuf_tensor(name, list(shape), dtype))
    def ps(name, shape):
        return st4.enter_context(nc.psum_tensor(name, list(shape), F32))
    SC = 128.0 ** -0.5
    ptab = sb("ptab", [128, 256], I32)
    iota = sb("iota4", [128, 1], F32)
    idx = sb("idx4", [128, 256], I32)
    dsel = sb("dsel_sb", [128, 32, 16], F32)
    identf = sb("identf4", [128, 128], F32)
    onesf = sb("onesf4", [128, 128], F32)
    onesb = sb("onesb4", [128, 2], BF16)
    qcol = [sb(f"qcol{i}", [128, 128], F32) for i in range(2)]
    qrep = sb("qrep", [128, 8, 128], F32)
    vrep = sb("vrep", [128, 8, 128], F32)
    garep = sb("garep", [128, 8, 128], F32)
    kpg = [sb(f"kpg{i}", [128, 1024], F32) for i in range(3)]
    prod = [sb(f"prod{i}", [128, 8, 128], F32) for i in range(2)]
    sc = sb("sc4", [128, 32, 16], F32)
    psc = sb("psc4", [128, 32, 16], BF16)
    vpg = [sb(f"vpg{i}", [128, 1024], BF16) for i in range(4)]
    gsel = sb("gsel", [128, 32, 16], F32)
    gT = sb("gT4", [128, 32], F32)
    t8 = sb("t84", [128, 8], F32)
    wsl = sb("wsl4", [128, 32], F32)
    accs = sb("accs4", [128, 8, 128], F32)
    accden = sb("accden4", [128, 2], F32)
    qk = sb("qk4", [128, 8], F32)
    pnew = sb("pnew4", [128, 1], F32)
    rdn = sb("rdn4", [128, 1], F32)
    osT = sb("osT4", [128, 8], F32)
    misc4 = ps("misc4", [128, 512])
    rp_ps = [misc4[:, 0:128], misc4[:, 0:128]]
    ovA = ps("ovA", [128, 512])
    gs_ps = ovA[:, :].rearrange("p (n c) -> p n c", c=16)
    ovB = ps("ovB", [128, 512])
    den_ps = misc4[:, 256:258]
    sn_ps = misc4[:, 260:261]
    tr_ps = misc4[:, 264:272]

    ckv = ck[:, :]
    cvv = cv[:, :]
    l0 = P.dma("sp", lambda e: e.dma_start(out=ptab[:], in_=ptrep_d[:, :]), "c0")
    l1 = P.dma("sp", lambda e: e.dma_start(out=iota[:], in_=iota_d[:, :]), "c1")
    l2 = P.dma("sp", lambda e: e.dma_start(out=dsel[:], in_=dsel_d[:, :, :]), "c2")
    l3 = P.dma("sp", lambda e: e.dma_start(out=identf[:], in_=ident_d[:, :]), "c3")
    m0 = P.op("dve", lambda e: e.memset(onesf[:], 1.0))
    m1 = P.op("dve", lambda e: e.memset(onesb[:], 1.0))
    ix = P.op("dve", lambda e: e.tensor_scalar(out=idx[:], in0=ptab[:], scalar1=128.0, scalar2=iota[:, 0:1], op0=ALU.mult, op1=ALU.add),
              waits=[l0, l1])
    P.wait("pool", [ix])
    P.wait("pe", [l3, m0, m1])
    P.wait("dve", [l2])
    yield
    kfree = [None] * 3
    vfree = [None] * 4
    pfree = [None] * 2
    qc_free = [None] * 2
    rp_free = [None] * 2
    kc = vc = pc = rc = 0
    prev_sample = None
    accd_prev = [None]
    for b_ in range(4):
        rep_last = None
        for (srcT, dst) in ((qs, qrep), (vsT, vrep), (gasT, garep)):
            for h in range(8):
                s_ = rc % 2
                rc += 1
                a = P.op("dve", lambda e, s_=s_, srcT=srcT, h=h, b_=b_: e.tensor_scalar(
                    out=qcol[s_][:], in0=onesf[:], scalar1=srcT[:, h, b_:b_ + 1], scalar2=None, op0=ALU.mult), waits=[qc_free[s_], prev_sample])
                mm = P.op("pe", lambda e, s_=s_: e.matmul(rp_ps[s_][:, :], qcol[s_][:], identf[:], start=True, stop=True), waits=[a, rp_free[0], rp_free[1]])
                qc_free[s_] = mm
                cp = P.op("act", lambda e, s_=s_, dst=dst, h=h: e.activation(out=dst[:, h, :], in_=rp_ps[s_][:, :], func=AF.Identity), waits=[mm, prev_sample])
                rp_free[s_] = cp
                rep_last = cp
            yield
        red = None
        for pg in range(64):
            ks_ = kc % 3
            kc += 1
            col = 64 * b_ + pg
            dk = P.dma("pool", lambda e, ks_=ks_, col=col: e.indirect_dma_start(
                out=kpg[ks_][:], out_offset=None, in_=ckv, in_offset=bass.IndirectOffsetOnAxis(ap=idx[:, col:col + 1], axis=0)),
                f"kpg{ks_}", waits=[kfree[ks_]])
            pr = pc % 2
            pc += 1
            mu = P.op("dve", lambda e, ks_=ks_, pr=pr: e.tensor_tensor(out=prod[pr][:], in0=kpg[ks_][:].rearrange("p (h d) -> p h d", h=8), in1=qrep[:], op=ALU.mult),
                      waits=[dk, rep_last, pfree[pr]])
            kfree[ks_] = mu
            red = P.op("dve", lambda e, pr=pr, pg=pg: e.tensor_reduce(out=sc[:, pg // 2, 8 * (pg % 2):8 * (pg % 2) + 8], in_=prod[pr][:], axis=AX.X, op=ALU.add),
                       waits=[mu, prev_sample])
            pfree[pr] = red
            yield
        g1 = P.op("pe", lambda e: e.matmul(gs_ps, onesf[:], sc[:], start=True, stop=True), waits=[red, prev_sample, accd_prev[0]])
        g2 = P.op("dve", lambda e: e.tensor_tensor(out=gsel[0:8], in0=gs_ps[0:8], in1=dsel[0:8], op=ALU.mult), waits=[g1])
        g3 = P.op("dve", lambda e: e.tensor_reduce(out=gT[0:8, :], in_=gsel[0:8], axis=AX.X, op=ALU.add), waits=[g2])
        g4 = P.op("dve", lambda e: e.max(out=t8[0:8, :], in_=gT[0:8, :]), waits=[g3])
        g5 = P.op("dve", lambda e: e.tensor_scalar(out=wsl[0:8, :], in0=gT[0:8, :], scalar1=t8[0:8, 2:3], scalar2=None, op0=ALU.is_ge), waits=[g4])
        ex = P.op("act", lambda e: e.activation(out=psc[:], in_=sc[:], func=AF.Exp, scale=SC), waits=[red, prev_sample])
        accd = None
        for n_ in range(32):
            dvs = []
            sl = []
            for e_ in range(2):
                vs_ = vc % 4
                vc += 1
                col = 64 * b_ + 2 * n_ + e_
                dvs.append(P.dma("pool", lambda e, vs_=vs_, col=col: e.indirect_dma_start(
                    out=vpg[vs_][:], out_offset=None, in_=cvv, in_offset=bass.IndirectOffsetOnAxis(ap=idx[:, col:col + 1], axis=0)),
                    f"vpg{vs_}", waits=[vfree[vs_]]))
                sl.append(vs_)
            last = None
            for (dst, lo) in ((ovA, 0), (ovB, 4)):
                for e_ in range(2):
                    last = P.op("pe", lambda e, dst=dst, lo=lo, e_=e_, n_=n_, v=sl[e_]: e.matmul(
                        dst[0:8, :], psc[:, n_, 8 * e_:8 * e_ + 8], vpg[v][:, 128 * lo:128 * lo + 512], start=(e_ == 0), stop=(e_ == 1)),
                        waits=[ex, dvs[0], dvs[1], accd, g2] if (lo == 0 and e_ == 0) else [], sig=False)
            for e_ in range(2):
                last = P.op("pe", lambda e, e_=e_, n_=n_: e.matmul(den_ps[0:8, :], psc[:, n_, 8 * e_:8 * e_ + 8], onesb[:], start=(e_ == 0), stop=(e_ == 1)),
                            sig=(e_ == 1))
            vfree[sl[0]] = last
            vfree[sl[1]] = last
            if n_ == 0:
                a1 = P.op("dve", lambda e: e.tensor_scalar(out=accs[0:8, 0:4, :], in0=ovA[0:8, :].rearrange("p (h d) -> p h d", h=4), scalar1=wsl[0:8, 0:1], scalar2=None, op0=ALU.mult), waits=[last, g5, prev_sample])
                a2 = P.op("dve", lambda e: e.tensor_scalar(out=accs[0:8, 4:8, :], in0=ovB[0:8, :].rearrange("p (h d) -> p h d", h=4), scalar1=wsl[0:8, 0:1], scalar2=None, op0=ALU.mult), waits=[a1])
                accd = P.op("dve", lambda e: e.tensor_scalar(out=accden[0:8, :], in0=den_ps[0:8, :], scalar1=wsl[0:8, 0:1], scalar2=None, op0=ALU.mult), waits=[a2])
            else:
                a1 = P.op("dve", lambda e, n_=n_: e.scalar_tensor_tensor(out=accs[0:8, 0:4, :], in0=ovA[0:8, :].rearrange("p (h d) -> p h d", h=4), scalar=wsl[0:8, n_:n_ + 1], in1=accs[0:8, 0:4, :],
                                                                        op0=ALU.mult, op1=ALU.add), waits=[last, accd])
                a2 = P.op("dve", lambda e, n_=n_: e.scalar_tensor_tensor(out=accs[0:8, 4:8, :], in0=ovB[0:8, :].rearrange("p (h d) -> p h d", h=4), scalar=wsl[0:8, n_:n_ + 1], in1=accs[0:8, 4:8, :],
                                                                        op0=ALU.mult, op1=ALU.add), waits=[a1])
                accd = P.op("dve", lambda e, n_=n_: e.scalar_tensor_tensor(out=accden[0:8, :], in0=den_ps[0:8, :], scalar=wsl[0:8, n_:n_ + 1], in1=accden[0:8, :],
                                                                          op0=ALU.mult, op1=ALU.add), waits=[a2])
            yield
        accd_prev[0] = accd
        n1 = P.op("dve", lambda e, b_=b_: e.tensor_tensor(out=qk[:], in0=qs[:, :, b_], in1=ksT[:, :, b_], op=ALU.mult), waits=[prev_sample])
        n2 = P.op("pe", lambda e: e.matmul(sn_ps[0:8, :], qk[:], onesf[:, 0:1], start=True, stop=True), waits=[n1])
        n3 = P.op("act", lambda e: e.activation(out=pnew[0:8, :], in_=sn_ps[0:8, :], func=AF.Exp, scale=SC), waits=[n2])
        n4 = P.op("dve", lambda e: e.scalar_tensor_tensor(out=accs[0:8], in0=vrep[0:8], scalar=pnew[0:8, 0:1],
                                                         in1=accs[0:8], op0=ALU.mult, op1=ALU.add), waits=[n3, accd, rep_last])
        n5 = P.op("dve", lambda e: e.tensor_tensor(out=accden[0:8, 0:1], in0=accden[0:8, 0:1], in1=pnew[0:8, 0:1], op=ALU.add), waits=[n4])
        n6 = P.op("dve", lambda e: e.reciprocal(out=rdn[0:8, :], in_=accden[0:8, 0:1]), waits=[n5])
        n7 = P.op("dve", lambda e: e.scalar_tensor_tensor(out=accs[0:8], in0=accs[0:8], scalar=rdn[0:8, 0:1], in1=garep[0:8],
                                                         op0=ALU.mult, op1=ALU.mult), waits=[n6])
        if dbgfn is not None and b_ == 0:
            dd_ = dbgfn(dict(idx=idx, qrep=qrep, vrep=vrep, garep=garep, sc=sc, gT=gT, t8=t8, wsl=wsl, accs=accs, accden=accden, pnew=pnew, rdn=rdn, kpg0=kpg[0]), n7)
            for en_ in ("dve", "act", "pool", "pe"):
                P.wait(en_, dd_)
        cpl = None
        for h in range(8):
            tp = P.op("pe", lambda e, h=h: e.transpose(tr_ps[:, :], accs[0:8, h, :], identf[0:8, 0:8]), waits=[n7, cpl])
            cpl = P.op("act", lambda e, h=h: e.activation(out=osT[:, h:h + 1], in_=tr_ps[:, h:h + 1], func=AF.Identity), waits=[tp, prev_sample])
        prev_sample = write_out(b_, osT, cpl)
        if holder is not None:
            holder["last"] = prev_sample
        yield


def build(skip4=False, scopes=False):
    from contextlib import ExitStack
    nc = bass.Bass("TRN2", target_bir_lowering=False)
    dt = nc.dram_tensor

    def din(name, shape, dtype=F32):
        return dt(name, list(shape), dtype, kind="ExternalInput").ap()

    def dout(name, shape, dtype=F32):
        return dt(name, list(shape), dtype, kind="ExternalOutput").ap()

    xoT = din("xoT", [D, NT])
    w_in = din("w_in", [D, 7168])
    bcol = din("bcol", [128, 56])
    brep = din("brep", [128, 2048])
    hmask = din("hmask", [128, 4])
    xwT = din("xwT", [D, 4096])
    candb = din("candb", [128, 4, 16])
    cand01 = din("cand01", [128, 4, 16])
    own01 = din("own01", [128, 4, 16])
    negmask_d = din("negmask", [128, 512])
    ident_d = din("ident", [128, 128])
    kT_scr = dt("kT_scr", [8, 128, 4096], BF16, kind="Internal").ap()
    v_scr = dt("v_scr", [4096, 1024], BF16, kind="Internal").ap()
    mix_dbg = dout("mix_dbg", [128, 16, NCc], BF16)
    u_scr = dt("u_scr", [128, 8, NT], F32, kind="Internal").ap()
    wdw_d = din("wdw", [128, 8, 31])
    vec8 = din("vec8", [128, 4, 8])
    vec16 = din("vec16", [128, 4, 16])
    stT_d = din("stT", [128, 4, 8, 30])
    w_pw = din("w_pw", [1024, 1024])
    w_out = din("w_out", [D, D])
    w_pg = din("w_pg", [D, D])
    w_pe = din("w_pe", [256, D])
    xcT = din("xcT", [D, NCc])
    pT = din("pT", [256, NCc])
    yT_out = dout("yT_out", [D, NCc])
    if not skip4:
        ck_d = din("ck", [2560 * 128, 1024])
        cv_d = din("cv", [2560 * 128, 1024])
    ptrep_d = din("ptrep", [128, 256], I32)
    iota_d = din("iota", [128, 1])
    dsel_d = din("dsel", [128, 32, 16])
    st_raw = din("st_raw", [4, 30, 1024])
    cs_out = dout("cs_out", [4, 29, 1024])

    kT_out = dout("kT_out", [8, 128, NT])
    v_out = dout("v_out", [1024, 1024])
    vsT_out = dout("vsT_out", [128, 8, 4])
    uT_out = dout("uT_out", [128, 8, 34])

    es = ExitStack()
    with es:
        P = Prog(nc, es)

        def sb(name, shape, dtype):
            return es.enter_context(nc.sbuf_tensor(name, list(shape), dtype))

        mixT = sb("mixT", [128, 16, NCc], BF16)
        mid = ExitStack()

        def sbm(name, shape, dtype):
            return mid.enter_context(nc.sbuf_tensor(name, list(shape), dtype))
        qs = sb("qs", [128, 8, 4], F32)
        gasT = sb("gasT", [128, 8, 4], F32)
        vsT = sb("vsT", [128, 8, 4], F32)
        ksT = sb("ksT", [128, 8, 4], F32)
        bcol_t = sb("bcol_t", [128, 56], F32)
        brep_t = sb("brep_t", [128, 2048], F32)
        hmask_t = sb("hmask_t", [128, 4], F32)
        qT = sbm("qT", [128, 8, NCc], BF16)
        ga = sbm("ga", [128, 8, 1024], BF16)
        gcT = sbm("gcT", [128, 8, NCc], BF16)

        with ExitStack() as ps1:
            def sb1(name, shape, dtype):
                return ps1.enter_context(nc.sbuf_tensor(name, list(shape), dtype))
            xo = sb1("xo", [128, 16, NT], BF16)
            uT = sb1("uT", [128, 8, NT], F32)
            wt = [sb1(f"wt{i}", [128, 16, 512], BF16) for i in range(2)]
            kst = [sb1(f"kst{i}", [128, 292], F32) for i in range(2)]
            vst = [sb1(f"vst{i}", [128, 512], F32) for i in range(2)]
            sg = [sb1(f"sg{i}", [128, 292], F32) for i in range(2)]
            gtmp = [sb1(f"gtmp{i}", [128, 512], F32) for i in range(2)]
            pbank = [ps1.enter_context(nc.psum_tensor(f"p1b{i}", [128, 512], F32)) for i in range(4)]

            w_in_v = w_in.rearrange("(k p) n -> p k n", p=128)
            d_xo = P.dma("pool", lambda e: e.dma_start(out=xo[:], in_=xoT.rearrange("(k p) n -> p k n", p=128)), "xo")
            d_bc = P.dma("sp", lambda e: e.dma_start(out=bcol_t[:], in_=bcol[:, :]), "c0")
            d_br = P.dma("sp", lambda e: e.dma_start(out=brep_t[:], in_=brep[:, :]), "c1")
            d_hm = P.dma("sp", lambda e: e.dma_start(out=hmask_t[:], in_=hmask[:, :]), "c2")

            wt_free = [None] * 2
            bank_free = [None] * 4
            kst_free = [None] * 2
            vst_free = [None] * 2
            sg_free = [None] * 2
            gtmp_free = [None] * 2
            grp = [0]
            cnt = {"k": 0, "v": 0, "sg": 0, "g": 0}
            out_deps = []
            u_last = [None] * 8

            def mm_group(lhs_fn, rhs_fn, n_out, w_dep):
                b = grp[0] % 4
                grp[0] += 1
                dep = None
                for k in range(16):
                    waits = []
                    if k == 0:
                        waits = [w_dep, d_xo, bank_free[b]]
                    last = (k == 15)
                    dep = P.op("pe", (lambda e, k=k, b=b: e.matmul(pbank[b][:, 0:n_out], lhs_fn(k), rhs_fn(k), start=(k == 0), stop=(k == 15))),
                               waits=waits, sig=last)
                return b, dep

            for t in range(14):
                s = t % 2
                d_w = P.dma("pool", (lambda e, t=t, s=s: e.dma_start(out=wt[s][:], in_=w_in_v[:, :, t * 512:(t + 1) * 512])),
                            f"wt{s}", waits=[wt_free[s]])
                role = ["q", "k", "v", "ga", "a", "bg", "gc"][t // 2]
                last_pe = None
                if role in ("q", "k", "a", "bg", "gc"):
                    for cbl in range(4):
                        cb = 4 * t + cbl
                        hh = cb % 8
                        for i in range(4):
                            n = 292 if i == 3 else 288
                            c0 = 288 * i
                            b, dep = mm_group(lambda k, s=s, cbl=cbl: wt[s][:, k, cbl * 128:(cbl + 1) * 128],
                                              lambda k, c0=c0, n=n: xo[:, k, c0:c0 + n], n, d_w)
                            last_pe = dep
                            bias = bcol_t[:, cb:cb + 1]
                            if role == "q":
                                ev = P.op("act", lambda e, b=b, hh=hh, i=i, bias=bias: e.activation(
                                    out=qT[:, hh, 256 * i:256 * i + 256], in_=pbank[b][:, 32:288], func=AF.Identity, bias=bias, scale=1.0),
                                    waits=[dep, d_bc])
                                if i == 3:
                                    P.op("act", lambda e, b=b, hh=hh, bias=bias: e.activation(
                                        out=qT[:, hh, 1024:1028], in_=pbank[b][:, 288:292], func=AF.Identity, bias=bias, scale=1.0))
                                    ev = P.op("act", lambda e, b=b, hh=hh, bias=bias: e.activation(
                                        out=qs[:, hh, :], in_=pbank[b][:, 288:292], func=AF.Identity, bias=bias, scale=1.0))
                                bank_free[b] = ev
                            elif role == "k":
                                ks = cnt["k"] % 2
                                cnt["k"] += 1
                                ev = P.op("act", lambda e, b=b, ks=ks, n=n, bias=bias: e.activation(
                                    out=kst[ks][:, 0:n], in_=pbank[b][:, 0:n], func=AF.Identity, bias=bias, scale=1.0),
                                    waits=[dep, d_bc, kst_free[ks]])
                                if i == 3:
                                    ev2 = P.op("act", lambda e, b=b, hh=hh, bias=bias: e.activation(
                                        out=ksT[:, hh, :], in_=pbank[b][:, 288:292], func=AF.Identity, bias=bias, scale=1.0))
                                    bank_free[b] = ev2
                                else:
                                    bank_free[b] = ev
                                dd = P.dma("sp", lambda e, ks=ks, hh=hh, c0=c0, n=n: e.dma_start(
                                    out=kT_out[hh, :, c0:c0 + n], in_=kst[ks][:, 0:n]), f"kst{ks}", waits=[ev])
                                kst_free[ks] = dd
                            elif role == "a":
                                ev = P.op("act", lambda e, b=b, hh=hh, c0=c0, n=n, bias=bias: e.activation(
                                    out=uT[:, hh, c0:c0 + n], in_=pbank[b][:, 0:n], func=AF.Identity, bias=bias, scale=1.0),
                                    waits=[dep, d_bc])
                                bank_free[b] = ev
                            elif role == "bg":
                                ss = cnt["sg"] % 2
                                cnt["sg"] += 1
                                ev = P.op("act", lambda e, b=b, ss=ss, n=n, bias=bias: e.activation(
                                    out=sg[ss][:, 0:n], in_=pbank[b][:, 0:n], func=AF.Sigmoid, bias=bias, scale=1.0),
                                    waits=[dep, d_bc, sg_free[ss]])
                                bank_free[b] = ev
                                mu = P.op("dve", lambda e, ss=ss, hh=hh, c0=c0, n=n: e.tensor_tensor(
                                    out=uT[:, hh, c0:c0 + n], in0=uT[:, hh, c0:c0 + n], in1=sg[ss][:, 0:n], op=ALU.mult),
                                    waits=[ev])
                                sg_free[ss] = mu
                                u_last[hh] = mu
                            elif role == "gc":
                                ev = P.op("act", lambda e, b=b, hh=hh, i=i, bias=bias: e.activation(
                                    out=gcT[:, hh, 256 * i:256 * i + 256], in_=pbank[b][:, 32:288], func=AF.Silu, bias=bias, scale=1.0),
                                    waits=[dep, d_bc])
                                if i == 3:
                                    ev = P.op("act", lambda e, b=b, hh=hh, bias=bias: e.activation(
                                        out=gcT[:, hh, 1024:1028], in_=pbank[b][:, 288:292], func=AF.Silu, bias=bias, scale=1.0))
                                bank_free[b] = ev
                else:
                    half = t % 2
                    boff = (0 if role == "v" else 1024) + half * 512
                    for tt in range(8):
                        i, hf = tt // 2, tt % 2
                        c0 = 288 * i + 32 + 128 * hf
                        b, dep = mm_group(lambda k, c0=c0: xo[:, k, c0:c0 + 128],
                                          lambda k, s=s: wt[s][:, k, :], 512, d_w)
                        last_pe = dep
                        if role == "v":
                            vs = cnt["v"] % 2
                            cnt["v"] += 1
                            ev = P.op("dve", lambda e, b=b, vs=vs, boff=boff: e.tensor_tensor(
                                out=vst[vs][:], in0=pbank[b][:, :], in1=brep_t[:, boff:boff + 512], op=ALU.add),
                                waits=[dep, d_br, vst_free[vs]])
                            bank_free[b] = ev
                            dd = P.dma("sp", lambda e, vs=vs, tt=tt, half=half: e.dma_start(
                                out=v_out[tt * 128:(tt + 1) * 128, half * 512:(half + 1) * 512], in_=vst[vs][:]),
                                f"vst{vs}", waits=[ev])
                            vst_free[vs] = dd
                        else:
                            gs = cnt["g"] % 2
                            cnt["g"] += 1
                            ev = P.op("dve", lambda e, b=b, gs=gs, boff=boff: e.tensor_tensor(
                                out=gtmp[gs][:], in0=pbank[b][:, :], in1=brep_t[:, boff:boff + 512], op=ALU.add),
                                waits=[dep, d_br, gtmp_free[gs]])
                            bank_free[b] = ev
                            a2 = P.op("act", lambda e, gs=gs, tt=tt, half=half: e.activation(
                                out=ga[:, tt, half * 512:(half + 1) * 512], in_=gtmp[gs][:], func=AF.Silu),
                                waits=[ev])
                            gtmp_free[gs] = a2
                    for cbl in range(4):
                        cb = 4 * t + cbl
                        hh = cb % 8
                        b, dep = mm_group(lambda k, s=s, cbl=cbl: wt[s][:, k, cbl * 128:(cbl + 1) * 128],
                                          lambda k: xo[:, k, 1152:1156], 4, d_w)
                        last_pe = dep
                        bias = bcol_t[:, cb:cb + 1]
                        if role == "v":
                            ev = P.op("act", lambda e, b=b, hh=hh, bias=bias: e.activation(
                                out=vsT[:, hh, :], in_=pbank[b][:, 0:4], func=AF.Identity, bias=bias, scale=1.0),
                                waits=[dep, d_bc])
                        else:
                            ev = P.op("act", lambda e, b=b, hh=hh, bias=bias: e.activation(
                                out=gasT[:, hh, :], in_=pbank[b][:, 0:4], func=AF.Silu, bias=bias, scale=1.0),
                                waits=[dep, d_bc])
                        bank_free[b] = ev
                wt_free[s] = last_pe

            hm = None
            for i in range(4):
                hm = P.op("dve", lambda e, i=i: e.tensor_scalar(
                    out=uT[:, :, 288 * i:288 * i + 32], in0=uT[:, :, 288 * i:288 * i + 32],
                    scalar1=hmask_t[:, i:i + 1], scalar2=None, op0=ALU.mult),
                    waits=[d_hm] + [u for u in u_last])
            fin = []
            fin.append(P.dma("sp", lambda e: e.dma_start(out=uT_out[:, :, 0:30], in_=uT[:, :, 1122:1152]), "fo0", waits=[hm]))
            fin.append(P.dma("sp", lambda e: e.dma_start(out=uT_out[:, :, 30:34], in_=uT[:, :, 1152:1156]), "fo1", waits=[hm]))
            fin.append(P.dma("sp", lambda e: e.dma_start(out=u_scr[:, :, :], in_=uT[:]), "fo3", waits=[hm]))
            fin.append(P.dma("sp", lambda e: e.dma_start(out=cs_out[:, :, :], in_=st_raw[:, 1:30, :]), "fo4"))
            lastact = P.op("act", lambda e: e.activation(out=sg[0][:, 0:4], in_=vsT[:, 0, :], func=AF.Identity), waits=[sg_free[0], sg_free[1]])
            fin.append(P.dma("sp", lambda e: e.dma_start(out=vsT_out[:, :, :], in_=vsT[:]), "fo2", waits=[lastact]))
            P.wait("sp", fin + [kst_free[0], kst_free[1], vst_free[0], vst_free[1]])
            P.wait("act", [gtmp_free[0], gtmp_free[1], lastact])
            P.wait("dve", [hm])
            with (nc.named_scope(f'PH1') if scopes else ExitStack()):
                with nc.Block() as blk:
                    P.emit(blk)

        kmT = sbm("kmT", [128, 8, 16], BF16)
        ksum = sbm("ksum", [128, 8, 16], F32)
        with ExitStack() as ps2:
            def sb2(name, shape, dtype):
                return ps2.enter_context(nc.sbuf_tensor(name, list(shape), dtype))
            wk = sb2("wk", [128, 16, 1024], BF16)
            wv = sb2("wv", [128, 16, 1024], BF16)
            xw = [sb2(f"xw{i}", [128, 16, 512], BF16) for i in range(2)]
            kstg = [sb2(f"kstg{i}", [128, 8, 512], BF16) for i in range(1)]
            vstg = [sb2(f"vstg{i}", [128, 4, 1024], BF16) for i in range(1)]
            pb2 = [ps2.enter_context(nc.psum_tensor(f"p2b{i}", [128, 512], F32)) for i in range(4)]
            xwT_v = xwT.rearrange("(k p) n -> p k n", p=128)
            d_wk = P.dma("pool", lambda e: e.dma_start(out=wk[:], in_=w_in_v[:, :, 1024:2048]), "wk")
            d_x = [None] * 8
            d_x[0] = P.dma("pool", lambda e: e.dma_start(out=xw[0][:], in_=xwT_v[:, :, 0:512]), "xw0")
            d_wv = P.dma("pool", lambda e: e.dma_start(out=wv[:], in_=w_in_v[:, :, 2048:3072]), "wv")
            xw_free = [None, None]
            kstg_free = [None, None]
            vstg_free = [None, None]
            bfree = [None] * 4
            g2 = [0]
            red_last = None

            def mm2(lhs_fn, rhs_fn, waits):
                b = g2[0] % 4
                g2[0] += 1
                dep = None
                for k in range(16):
                    dep = P.op("pe", (lambda e, k=k, b=b: e.matmul(pb2[b][:, :], lhs_fn(k), rhs_fn(k), start=(k == 0), stop=(k == 15))),
                               waits=(list(waits) + [bfree[b]]) if k == 0 else [], sig=(k == 15))
                return b, dep

            for c in range(8):
                s_ = c % 2
                if c + 1 < 8:
                    d_x[c + 1] = P.dma("pool", (lambda e, c=c: e.dma_start(out=xw[(c + 1) % 2][:], in_=xwT_v[:, :, 512 * (c + 1):512 * (c + 2)])),
                                       f"xw{(c + 1) % 2}", waits=[xw_free[(c + 1) % 2]])
                evs = []
                for h in range(8):
                    b, dep = mm2(lambda k, h=h: wk[:, k, 128 * h:128 * h + 128], lambda k, s_=s_: xw[s_][:, k, :], [d_wk, d_x[c]])
                    ev = P.op("act", lambda e, b=b, s_=s_, h=h: e.activation(
                        out=kstg[0][:, h, :], in_=pb2[b][:, :], func=AF.Identity, bias=bcol_t[:, 8 + h:9 + h], scale=1.0),
                        waits=[dep, kstg_free[0]])
                    bfree[b] = ev
                    evs.append(ev)
                r = None
                for blk_ in range(2):
                    r = P.op("dve", lambda e, s_=s_, c=c, blk_=blk_: e.tensor_reduce(
                        out=ksum[:, :, 2 * c + blk_], in_=kstg[0][:, :, 256 * blk_:256 * blk_ + 256], axis=AX.X, op=ALU.add),
                        waits=[evs[-1]])
                red_last = r
                dk = P.dma("sp", lambda e, s_=s_, c=c: e.dma_start(
                    out=kT_scr.rearrange("h d t -> d h t")[:, :, 512 * c:512 * c + 512], in_=kstg[0][:]), "kstg0", waits=[evs[-1]])
                kstg_free[0] = dk
                P.wait("act", [r]) if False else None
                kred = r
                vev = None
                lastpe = None
                for tt in range(4):
                    for half in range(2):
                        b, dep = mm2(lambda k, s_=s_, tt=tt: xw[s_][:, k, 128 * tt:128 * tt + 128],
                                     lambda k, half=half: wv[:, k, 512 * half:512 * half + 512], [d_wv, d_x[c]])
                        lastpe = dep
                        vev = P.op("dve", lambda e, b=b, s_=s_, tt=tt, half=half: e.tensor_tensor(
                            out=vstg[0][:, tt, 512 * half:512 * half + 512], in0=pb2[b][:, :], in1=brep_t[:, 512 * half:512 * half + 512], op=ALU.add),
                            waits=[dep, vstg_free[0]])
                        bfree[b] = vev
                dv = P.dma("sp", lambda e, s_=s_, c=c: e.dma_start(
                    out=v_scr[512 * c:512 * c + 512, :].rearrange("(t p) n -> p t n", p=128), in_=vstg[0][:]), "vstg0", waits=[vev])
                vstg_free[0] = dv
                xw_free[s_] = lastpe
                P.wait("act", [kred])
            km = P.op("dve", lambda e: e.tensor_scalar(out=kmT[:], in0=ksum[:], scalar1=1.0 / 256.0, scalar2=None, op0=ALU.mult),
                      waits=[red_last])
            P.wait("sp", [kstg_free[0], vstg_free[0]])
            P.wait("dve", [km])
            with (nc.named_scope(f'PH2') if scopes else ExitStack()):
                with nc.Block() as blk:
                    P.emit(blk)

        with ExitStack() as ps3:
            def sb3(name, shape, dtype):
                return ps3.enter_context(nc.sbuf_tensor(name, list(shape), dtype))
            KTh = [sb3(f"KTh{i}", [128, 4096], BF16) for i in range(2)]
            Vh = [sb3(f"Vh{i}", [128, 32, 130], BF16) for i in range(2)]
            Pt = [sb3(f"Pt{i}", [128, 512], BF16) for i in range(3)]
            gm = [sb3(f"gm{i}", [128, 16], F32) for i in range(2)]
            top8 = [sb3(f"top8{i}", [128, 8], F32) for i in range(2)]
            wsel = [sb3(f"wsel{i}", [128, 2, 16], F32) for i in range(2)]
            acc = [sb3(f"acc{i}", [128, 2, 129], F32) for i in range(2)]
            rden = [sb3(f"rden{i}", [128, 2], F32) for i in range(2)]
            attn_n = [sb3(f"attn_n{i}", [128, 128], F32) for i in range(2)]
            candb_t = sb3("candb_t", [128, 4, 16], F32)
            cand01_t = sb3("cand01_t", [128, 4, 16], F32)
            own01_t = sb3("own01_t", [128, 4, 16], F32)
            negm = sb3("negm", [128, 512], BF16)
            identb = sb3("identb", [128, 128], BF16)
            identf = sb3("identf", [128, 128], F32)
            misc3 = ps3.enter_context(nc.psum_tensor("misc3", [128, 512], F32))
            gps = misc3
            Sps = [ps3.enter_context(nc.psum_tensor(f"Sps{i}", [128, 512], F32)) for i in range(2)]
            Ops = [ps3.enter_context(nc.psum_tensor(f"Ops{i}", [128, 2, 256], F32)) for i in range(2)]
            Tps = [misc3[:, 128:256], misc3[:, 128:256]]

            cdeps = [
                P.dma("sp", lambda e: e.dma_start(out=candb_t[:], in_=candb[:, :, :]), "c0"),
                P.dma("sp", lambda e: e.dma_start(out=cand01_t[:], in_=cand01[:, :, :]), "c1"),
                P.dma("sp", lambda e: e.dma_start(out=own01_t[:], in_=own01[:, :, :]), "c2"),
                P.dma("sp", lambda e: e.dma_start(out=identf[:], in_=ident_d[:, :]), "c3"),
                P.dma("pool", lambda e: e.dma_start(out=negm[:], in_=negmask_d[:, :]), "c4"),
                P.dma("pool", lambda e: e.dma_start(out=identb[:], in_=ident_d[:, :]), "c5"),
            ]
            ones_dep = [P.op("dve", lambda e, i=i: e.memset(Vh[i][:, :, 128:130], 1.0)) for i in range(2)]
            P.wait("dve", cdeps[0:3])
            P.wait("pe", cdeps[3:6])
            holder4 = {}
            g4 = None
            if not skip4:
                def write_out(b_, osT, dep):
                    return P.op("act", lambda e, b_=b_: e.activation(out=mixT[:, 0:8, 1024 + b_], in_=osT[:, :], func=AF.Identity), waits=[dep])
                g4 = emit_p4(nc, P, ps3, ck_d, cv_d, ptrep_d, iota_d, dsel_d, ident_d, qs, ksT, vsT, gasT, write_out, holder=holder4)
            vis4 = [0]

            def step4():
                if g4 is None:
                    return
                vis4[0] += 1
                next(g4, None)
                if vis4[0] % 4 == 0:
                    next(g4, None)
            kv_free = [None, None]
            S_free = [None, None]
            O_free = [None, None]
            T_free = [None, None]
            Pt_free = [None, None, None]
            an_free = [None, None]
            gps_free = None
            cS = cO = cP = cT = 0
            SCALE = 128.0 ** -0.5
            mix_last = None
            qb = 0
            for h in range(8):
                s_ = h % 2
                d_k = P.dma("sp", lambda e, s_=s_, h=h: e.dma_start(out=KTh[s_][:], in_=kT_scr[h]), f"KTh{s_}", waits=[kv_free[s_]])
                d_v = P.dma("sp", lambda e, s_=s_, h=h: e.dma_start(
                    out=Vh[s_][:, :, 0:128], in_=v_scr.rearrange("(t p) n -> p t n", p=128)[:, :, 128 * h:128 * h + 128]),
                    f"Vh{s_}", waits=[kv_free[s_]])
                last_pe_h = None
                for i in range(4):
                    par = qb % 2
                    qb += 1
                    qc = 256 * i
                    gdep = None
                    for t in range(2):
                        gdep = P.op("pe", lambda e, t=t, h=h, qc=qc: e.matmul(
                            gps[:, 16 * t:16 * t + 16], qT[:, h, qc + 128 * t:qc + 128 * t + 128], kmT[:, h, :], start=True, stop=True),
                            waits=[gps_free, T_free[0], T_free[1]] if t == 0 else [])
                    wdeps = []
                    for t in range(2):
                        a1 = P.op("dve", lambda e, t=t, i=i: e.tensor_tensor(out=gm[t][:], in0=gps[:, 16 * t:16 * t + 16], in1=candb_t[:, i, :], op=ALU.add),
                                  waits=[gdep])
                        a2 = P.op("dve", lambda e, t=t: e.max(out=top8[t][:], in_=gm[t][:]), waits=[a1])
                        a3 = P.op("dve", lambda e, t=t, i=i, par=par: e.scalar_tensor_tensor(
                            out=wsel[par][:, t, :], in0=gm[t][:], scalar=top8[t][:, 2:3], in1=cand01_t[:, i, :], op0=ALU.is_ge, op1=ALU.mult),
                            waits=[a2])
                        a4 = P.op("dve", lambda e, t=t, i=i, par=par: e.tensor_tensor(
                            out=wsel[par][:, t, :], in0=wsel[par][:, t, :], in1=own01_t[:, i, :], op=ALU.add), waits=[a3])
                        wdeps.append(a4)
                        gps_free = a1
                    accdep = [None, None]
                    nblk = 4 * i + 4

                    def emit_qk(n_, h=h, s_=s_, qc=qc, nblk=nblk, d_k=d_k):
                        nonlocal cS
                        own = (n_ == nblk - 1)
                        sbk = cS % 2
                        cS += 1
                        first_w = [S_free[sbk], d_k, ones_dep[s_]]
                        if own:
                            P.op("pe", lambda e, sbk=sbk: e.matmul(Sps[sbk][:, :], identb[:], negm[:], start=True, stop=False),
                                 waits=first_w, sig=False)
                        sdep = None
                        for kt in range(2):
                            sdep = P.op("pe", lambda e, sbk=sbk, kt=kt, n_=n_, s_=s_, h=h, qc=qc, own=own: e.matmul(
                                Sps[sbk][:, 256 * kt:256 * kt + 256], KTh[s_][:, 256 * n_ + 128 * kt:256 * n_ + 128 * kt + 128],
                                qT[:, h, qc:qc + 256], start=(not own), stop=((not own) or kt == 1)),
                                waits=(first_w if (kt == 0 and not own) else []), sig=(kt == 1))
                        return sbk, sdep

                    pend = emit_qk(0)
                    for n_ in range(nblk):
                        sbk, sdep = pend
                        pp = cP % 3
                        cP += 1
                        edep = P.op("act", lambda e, pp=pp, sbk=sbk: e.activation(out=Pt[pp][:], in_=Sps[sbk][:, :], func=AF.Exp, scale=SCALE),
                                    waits=[sdep, Pt_free[pp]])
                        S_free[sbk] = edep
                        if n_ + 1 < nblk:
                            pend = emit_qk(n_ + 1)
                        ob = cO % 2
                        cO += 1
                        pvdep = None
                        for t in range(2):
                            for kt in range(2):
                                pvdep = P.op("pe", lambda e, ob=ob, t=t, kt=kt, pp=pp, s_=s_, n_=n_: e.matmul(
                                    Ops[ob][:, t, 0:129], Pt[pp][:, 256 * kt + 128 * t:256 * kt + 128 * t + 128],
                                    Vh[s_][:, 2 * n_ + kt, 0:129], start=(kt == 0), stop=(kt == 1)),
                                    waits=[edep, O_free[ob], d_v] if (t == 0 and kt == 0) else [], sig=(t == 1 and kt == 1))
                        Pt_free[pp] = pvdep
                        last_pe_h = pvdep
                        for t in range(2):
                            if n_ == 0:
                                accdep[t] = P.op("dve", lambda e, t=t, ob=ob, par=par: e.tensor_scalar(
                                    out=acc[par][:, t, :], in0=Ops[ob][:, t, 0:129], scalar1=wsel[par][:, t, 0:1], scalar2=None, op0=ALU.mult),
                                    waits=[pvdep, wdeps[t], an_free[par]])
                            else:
                                accdep[t] = P.op("dve", lambda e, t=t, ob=ob, par=par, n_=n_: e.scalar_tensor_tensor(
                                    out=acc[par][:, t, :], in0=Ops[ob][:, t, 0:129], scalar=wsel[par][:, t, n_:n_ + 1], in1=acc[par][:, t, :],
                                    op0=ALU.mult, op1=ALU.add), waits=[pvdep, accdep[t]])
                        O_free[ob] = accdep[1]
                        step4()
                    fin_last = None
                    for t in range(2):
                        r1 = P.op("dve", lambda e, t=t, par=par: e.reciprocal(out=rden[par][:, t:t + 1], in_=acc[par][:, t, 128:129]),
                                  waits=[accdep[t]])
                        asl = cT % 2
                        r2 = P.op("dve", lambda e, t=t, par=par, asl=asl, i=i, h=h: e.scalar_tensor_tensor(
                            out=attn_n[asl][:], in0=acc[par][:, t, 0:128], scalar=rden[par][:, t:t + 1], in1=ga[:, 2 * i + t, 128 * h:128 * h + 128],
                            op0=ALU.mult, op1=ALU.mult), waits=[r1, T_free[asl]])
                        tb = cT % 2
                        cT += 1
                        tp = P.op("pe", lambda e, tb=tb, asl=asl: e.transpose(Tps[tb][:, :], attn_n[asl][:], identf[:]),
                                  waits=[r2, T_free[0], T_free[1]])
                        cp = P.op("act", lambda e, tb=tb, h=h, qc=qc, t=t: e.activation(
                            out=mixT[:, h, qc + 128 * t:qc + 128 * t + 128], in_=Tps[tb][:, :], func=AF.Identity), waits=[tp])
                        T_free[tb] = cp
                        mix_last = cp
                        fin_last = r2
                    an_free[par] = fin_last
                kv_free[s_] = last_pe_h
            if g4 is not None:
                for _ in g4:
                    pass
                P.wait("act", [holder4["last"]])
            else:
                mz = P.op("dve", lambda e: e.memset(mixT[:, 0:8, 1024:1028], 0.0))
                P.wait("dve", [mz])
            P.wait("act", [mix_last])
            with (nc.named_scope(f'PH3') if scopes else ExitStack()):
                with nc.Block() as blk:
                    P.emit(blk)

        CH = [(0, 344), (344, 342), (686, 342)]
        with ExitStack() as ps5:
            def sb5(name, shape, dtype):
                return ps5.enter_context(nc.sbuf_tensor(name, list(shape), dtype))
            uTb = sb5("uTb", [128, 8, 4, 288], F32)
            uTs = sb5("uTs", [128, 8, 4], F32)
            cTb = sb5("cTb", [128, 8, 4, 256], F32)
            cTs = sb5("cTs", [128, 8, 4], F32)
            cnT = sb5("cnT", [128, 8, NCc], BF16)
            wdw_t = sb5("wdw_t", [128, 8, 31], F32)
            vec8_t = sb5("vec8_t", [128, 4, 8], F32)
            stT = sb5("stT_sb", [128, 4, 8, 30], F32)
            stmp = sb5("stmp", [128, 8, 30], F32)
            s1t = sb5("s1t", [128, 8], F32)
            wpw = sb5("wpw", [128, 8, 1024], BF16)
            onesf = sb5("onesf", [128, 128], F32)
            epsT = sb5("epsT", [128, 1], F32)
            sqt = [sb5(f"sqt{i}", [128, 256], F32) for i in range(2)]
            mean_t = sb5("mean_t", [128, 256], F32)
            m2_t = sb5("m2_t", [128, 256], F32)
            rstd_t = sb5("rstd_t", [128, 256], F32)
            ntmp = [sb5(f"ntmp{i}", [128, 256], F32) for i in range(2)]
            S1p = ps5.enter_context(nc.psum_tensor("S1p", [128, 256], F32))
            S2p = ps5.enter_context(nc.psum_tensor("S2p", [128, 256], F32))
            pwb = [ps5.enter_context(nc.psum_tensor(f"pwb{i}", [128, 512], F32)) for i in range(2)]

            l_u = P.dma("sp", lambda e: e.dma_start(out=uTb[:], in_=u_scr[:, :, 0:1152].rearrange("p g (i c) -> p g i c", c=288)), "c0")
            l_us = P.dma("sp", lambda e: e.dma_start(out=uTs[:], in_=u_scr[:, :, 1152:1156]), "c1")
            l_w = P.dma("sp", lambda e: e.dma_start(out=wdw_t[:], in_=wdw_d[:, :, :]), "c2")
            l_v8 = P.dma("sp", lambda e: e.dma_start(out=vec8_t[:], in_=vec8[:, :, :]), "c3")
            l_st = P.dma("sp", lambda e: e.dma_start(out=stT[:], in_=stT_d[:, :, :, :]), "c4")
            l_pw = P.dma("pool", lambda e: e.dma_start(out=wpw[:], in_=w_pw.rearrange("(k p) n -> p k n", p=128)), "c5")
            m1 = P.op("dve", lambda e: e.memset(onesf[:], 1.0))
            m2 = P.op("dve", lambda e: e.memset(epsT[:], 1e-5))
            P.wait("dve", [l_u, l_us, l_w, l_v8, l_st])
            cdep = [None] * 8
            for g in range(8):
                d = P.op("dve", lambda e, g=g: e.tensor_scalar(
                    out=cTb[:, g, :, :], in0=uTb[:, g, :, 2:258], scalar1=wdw_t[:, g, 0:1], scalar2=vec8_t[:, 0, g:g + 1],
                    op0=ALU.mult, op1=ALU.add))
                for tap in range(1, 31):
                    d = P.op("dve", lambda e, g=g, tap=tap: e.scalar_tensor_tensor(
                        out=cTb[:, g, :, :], in0=uTb[:, g, :, 2 + tap:258 + tap], scalar=wdw_t[:, g, tap:tap + 1], in1=cTb[:, g, :, :],
                        op0=ALU.mult, op1=ALU.add), waits=[d])
                cdep[g] = d
            sdep = None
            for b_ in range(4):
                d1 = P.op("dve", lambda e, b_=b_: e.tensor_tensor(out=stmp[:], in0=stT[:, b_, :, :], in1=wdw_t[:, :, 0:30], op=ALU.mult), waits=[sdep])
                d2 = P.op("dve", lambda e: e.tensor_reduce(out=s1t[:], in_=stmp[:], axis=AX.X, op=ALU.add), waits=[d1])
                d3 = P.op("dve", lambda e, b_=b_: e.tensor_tensor(out=cTs[:, :, b_], in0=uTs[:, :, b_], in1=wdw_t[:, :, 30], op=ALU.mult), waits=[d2])
                d4 = P.op("dve", lambda e, b_=b_: e.tensor_tensor(out=cTs[:, :, b_], in0=cTs[:, :, b_], in1=s1t[:], op=ALU.add), waits=[d3])
                sdep = P.op("dve", lambda e, b_=b_: e.tensor_tensor(out=cTs[:, :, b_], in0=cTs[:, :, b_], in1=vec8_t[:, 0, :], op=ALU.add), waits=[d4])
            groups = [(lambda g, i=i: cTb[:, g, i, :], 256, 256 * i) for i in range(4)] + [(lambda g: cTs[:, g, :], 4, 1024)]
            sq_free = [None, None]
            st_free = None
            nt_free = [None, None]
            cnt5 = 0
            cn_last = None
            for (src, n, c0) in groups:
                mm = None
                for g in range(8):
                    sl = cnt5 % 2
                    cnt5 += 1
                    a_sq = P.op("act", lambda e, g=g, sl=sl, src=src, n=n: e.activation(out=sqt[sl][:, 0:n], in_=src(g), func=AF.Square),
                                waits=[cdep[g], sdep, sq_free[sl]])
                    P.op("pe", lambda e, g=g, src=src, n=n: e.matmul(S1p[:, 0:n], onesf[:], src(g), start=(g == 0), stop=(g == 7)),
                         waits=[cdep[g], sdep, m1, st_free] if g == 0 else [cdep[g]], sig=False)
                    mm = P.op("pe", lambda e, g=g, sl=sl, n=n: e.matmul(S2p[:, 0:n], onesf[:], sqt[sl][:, 0:n], start=(g == 0), stop=(g == 7)),
                              waits=[a_sq])
                    sq_free[sl] = mm
                e1 = P.op("act", lambda e, n=n: e.activation(out=mean_t[:, 0:n], in_=S1p[:, 0:n], func=AF.Identity, scale=1.0 / 1024.0), waits=[mm, cn_last])
                e2 = P.op("dve", lambda e, n=n: e.tensor_tensor(out=m2_t[:, 0:n], in0=mean_t[:, 0:n], in1=mean_t[:, 0:n], op=ALU.mult), waits=[e1, cn_last])
                e3 = P.op("dve", lambda e, n=n: e.scalar_tensor_tensor(out=m2_t[:, 0:n], in0=S2p[:, 0:n], scalar=1.0 / 1024.0, in1=m2_t[:, 0:n],
                                                                      op0=ALU.mult, op1=ALU.subtract), waits=[e2, mm])
                st_free = e3
                e4 = P.op("act", lambda e, n=n: e.activation(out=rstd_t[:, 0:n], in_=m2_t[:, 0:n], func=AF.Sqrt, bias=epsT[:, 0:1], scale=1.0), waits=[e3, m2])
                e5 = P.op("dve", lambda e, n=n: e.reciprocal(out=rstd_t[:, 0:n], in_=rstd_t[:, 0:n]), waits=[e4])
                for g in range(8):
                    sl = cnt5 % 2
                    cnt5 += 1
                    f1 = P.op("dve", lambda e, g=g, sl=sl, src=src, n=n: e.tensor_tensor(out=ntmp[sl][:, 0:n], in0=src(g), in1=mean_t[:, 0:n], op=ALU.subtract),
                              waits=[e5, nt_free[sl]])
                    f2 = P.op("dve", lambda e, sl=sl, n=n: e.tensor_tensor(out=ntmp[sl][:, 0:n], in0=ntmp[sl][:, 0:n], in1=rstd_t[:, 0:n], op=ALU.mult), waits=[f1])
                    f3 = P.op("act", lambda e, g=g, sl=sl, n=n, c0=c0: e.activation(
                        out=cnT[:, g, c0:c0 + n], in_=ntmp[sl][:, 0:n], func=AF.Silu, bias=vec8_t[:, 2, g:g + 1], scale=vec8_t[:, 1, g:g + 1]), waits=[f2])
                    nt_free[sl] = f3
                    cn_last = f3
            pw_free = [None, None]
            cntp = 0
            pw_last = None
            for cb in range(8):
                for (c0, n) in CH:
                    bsl = cntp % 2
                    cntp += 1
                    mm = None
                    for k in range(8):
                        mm = P.op("pe", lambda e, k=k, cb=cb, c0=c0, n=n, bsl=bsl: e.matmul(
                            pwb[bsl][:, 0:n], wpw[:, k, 128 * cb:128 * cb + 128], cnT[:, k, c0:c0 + n], start=(k == 0), stop=(k == 7)),
                            waits=[l_pw, cn_last, pw_free[bsl]] if k == 0 else [], sig=(k == 7))
                    pw_last = P.op("dve", lambda e, cb=cb, c0=c0, n=n, bsl=bsl: e.scalar_tensor_tensor(
                        out=mixT[:, 8 + cb, c0:c0 + n], in0=pwb[bsl][:, 0:n], scalar=vec8_t[:, 3, cb:cb + 1], in1=gcT[:, cb, c0:c0 + n],
                        op0=ALU.add, op1=ALU.mult), waits=[mm])
                    pw_free[bsl] = pw_last
            dbg = P.dma("sp", lambda e: e.dma_start(out=mix_dbg[:, :, :], in_=mixT[:]), "dbg", waits=[pw_last])
            P.wait("sp", [dbg])
            P.wait("act", [cn_last])
            P.wait("dve", [pw_last])
            with (nc.named_scope(f'PH5') if scopes else ExitStack()):
                with nc.Block() as blk:
                    P.emit(blk)

        ALPHA = 2.0 ** 0.25
        mid.close()
        with ExitStack() as ps6:
            def sb6(name, shape, dtype):
                return ps6.enter_context(nc.sbuf_tensor(name, list(shape), dtype))
            rT = sb6("rT", [128, 16, NCc], F32)
            hbf = sb6("hbf", [128, 16, NCc], BF16)
            wo = [sb6(f"wo{i}", [128, 16, 512], BF16) for i in range(2)]
            wpe = sb6("wpe", [128, 2, D], BF16)
            pTb = sb6("pTb", [128, 2, NCc], BF16)
            vec16_t = sb6("vec16_t", [128, 4, 16], F32)
            xr = [sb6(f"xr{i}", [128, 344], F32) for i in range(2)]
            rtmp = [sb6(f"rtmp{i}", [128, 344], F32) for i in range(2)]
            sq6 = [sb6(f"sq6{i}", [128, 344], F32) for i in range(2)]
            onesf6 = sb6("onesf6", [128, 128], F32)
            eps6 = sb6("eps6", [128, 1], F32)
            mean6 = sb6("mean6", [128, NCc], F32)
            rstd6 = sb6("rstd6", [128, NCc], F32)
            sgm = [sb6(f"sgm{i}", [128, 344], F32) for i in range(2)]
            yst = [sb6(f"yst{i}", [128, 344], F32) for i in range(2)]
            S1 = [ps6.enter_context(nc.psum_tensor(f"S1_{i}", [128, 512], F32)) for i in range(3)]
            S2 = [ps6.enter_context(nc.psum_tensor(f"S2_{i}", [128, 512], F32)) for i in range(3)]
            mb = [ps6.enter_context(nc.psum_tensor(f"mb{i}", [128, 512], F32)) for i in range(2)]

            l_v16 = P.dma("sp", lambda e: e.dma_start(out=vec16_t[:], in_=vec16[:, :, :]), "c0")
            l_pe = P.dma("pool", lambda e: e.dma_start(out=wpe[:], in_=w_pe.rearrange("(k p) n -> p k n", p=128)), "c1")
            l_pt = P.dma("pool", lambda e: e.dma_start(out=pTb[:], in_=pT.rearrange("(k p) n -> p k n", p=128)), "c2")
            o1 = P.op("dve", lambda e: e.memset(onesf6[:], 1.0))
            o2 = P.op("dve", lambda e: e.memset(eps6[:], 1e-5))
            wo_free = [None, None]
            mb_free = [None, None]
            xr_free = [None, None]
            rt_free = [None, None]
            sq_free6 = [None, None]
            w_out_v = w_out.rearrange("(k p) n -> p k n", p=128)
            w_pg_v = w_pg.rearrange("(k p) n -> p k n", p=128)
            cnt6 = 0
            wt_i = 0
            stat_last = None
            for t in range(4):
                ws = wt_i % 2
                wt_i += 1
                d_w = P.dma("pool", lambda e, t=t, ws=ws: e.dma_start(out=wo[ws][:], in_=w_out_v[:, :, 512 * t:512 * t + 512]), f"wo{ws}", waits=[wo_free[ws]])
                lastpe = None
                for cbl in range(4):
                    cb = 4 * t + cbl
                    for ci, (c0, n) in enumerate(CH):
                        sl = cnt6 % 2
                        cnt6 += 1
                        d_x = P.dma("sp", lambda e, sl=sl, cb=cb, c0=c0, n=n: e.dma_start(out=xr[sl][:, 0:n], in_=xcT[128 * cb:128 * cb + 128, c0:c0 + n]),
                                    f"xr{sl}", waits=[xr_free[sl]])
                        mm = None
                        for k in range(16):
                            mm = P.op("pe", lambda e, k=k, ws=ws, cbl=cbl, c0=c0, n=n, sl=sl: e.matmul(
                                mb[sl][:, 0:n], wo[ws][:, k, 128 * cbl:128 * cbl + 128], mixT[:, k, c0:c0 + n], start=(k == 0), stop=(k == 15)),
                                waits=[d_w, mb_free[sl]] if k == 0 else [], sig=(k == 15))
                        lastpe = mm
                        r1 = P.op("dve", lambda e, sl=sl, n=n: e.scalar_tensor_tensor(
                            out=rtmp[sl][:, 0:n], in0=xr[sl][:, 0:n], scalar=ALPHA, in1=mb[sl][:, 0:n], op0=ALU.mult, op1=ALU.add),
                            waits=[mm, d_x, rt_free[sl]])
                        mb_free[sl] = r1
                        xr_free[sl] = r1
                        r2 = P.op("act", lambda e, sl=sl, cb=cb, c0=c0, n=n: e.activation(
                            out=rT[:, cb, c0:c0 + n], in_=rtmp[sl][:, 0:n], func=AF.Identity, bias=vec16_t[:, 0, cb:cb + 1], scale=1.0), waits=[r1, l_v16])
                        r3 = P.op("act", lambda e, sl=sl, cb=cb, n=n: e.activation(
                            out=sq6[sl][:, 0:n], in_=rtmp[sl][:, 0:n], func=AF.Square, bias=vec16_t[:, 0, cb:cb + 1], scale=1.0), waits=[sq_free6[sl]])
                        rt_free[sl] = r3
                        P.op("pe", lambda e, cb=cb, ci=ci, c0=c0, n=n: e.matmul(S1[ci][:, 0:n], onesf6[:], rT[:, cb, c0:c0 + n], start=(cb == 0), stop=(cb == 15)),
                             waits=[r2, o1], sig=False)
                        stat_last = P.op("pe", lambda e, cb=cb, ci=ci, n=n, sl=sl: e.matmul(S2[ci][:, 0:n], onesf6[:], sq6[sl][:, 0:n], start=(cb == 0), stop=(cb == 15)),
                                         waits=[r3])
                        sq_free6[sl] = stat_last
                wo_free[ws] = lastpe
            h_last = None
            for ci, (c0, n) in enumerate(CH):
                e1 = P.op("act", lambda e, ci=ci, c0=c0, n=n: e.activation(out=mean6[:, c0:c0 + n], in_=S1[ci][:, 0:n], func=AF.Identity, scale=1.0 / D), waits=[stat_last])
                e2 = P.op("dve", lambda e, c0=c0, n=n: e.tensor_tensor(out=rstd6[:, c0:c0 + n], in0=mean6[:, c0:c0 + n], in1=mean6[:, c0:c0 + n], op=ALU.mult), waits=[e1])
                e3 = P.op("dve", lambda e, ci=ci, c0=c0, n=n: e.scalar_tensor_tensor(out=rstd6[:, c0:c0 + n], in0=S2[ci][:, 0:n], scalar=1.0 / D, in1=rstd6[:, c0:c0 + n],
                                                                                    op0=ALU.mult, op1=ALU.subtract), waits=[e2, stat_last])
                e4 = P.op("act", lambda e, c0=c0, n=n: e.activation(out=rstd6[:, c0:c0 + n], in_=rstd6[:, c0:c0 + n], func=AF.Sqrt, bias=eps6[:, 0:1], scale=1.0), waits=[e3, o2])
                e5 = P.op("dve", lambda e, c0=c0, n=n: e.reciprocal(out=rstd6[:, c0:c0 + n], in_=rstd6[:, c0:c0 + n]), waits=[e4])
                for cb in range(16):
                    f1 = P.op("dve", lambda e, cb=cb, c0=c0, n=n: e.tensor_tensor(out=rT[:, cb, c0:c0 + n], in0=rT[:, cb, c0:c0 + n], in1=mean6[:, c0:c0 + n], op=ALU.subtract), waits=[e5])
                    f2 = P.op("dve", lambda e, cb=cb, c0=c0, n=n: e.tensor_tensor(out=rT[:, cb, c0:c0 + n], in0=rT[:, cb, c0:c0 + n], in1=rstd6[:, c0:c0 + n], op=ALU.mult), waits=[f1])
                    f3 = P.op("act", lambda e, cb=cb, c0=c0, n=n: e.activation(out=rT[:, cb, c0:c0 + n], in_=rT[:, cb, c0:c0 + n], func=AF.Identity,
                                                                               bias=vec16_t[:, 2, cb:cb + 1], scale=vec16_t[:, 1, cb:cb + 1]), waits=[f2])
                    h_last = P.op("act", lambda e, cb=cb, c0=c0, n=n: e.activation(out=hbf[:, cb, c0:c0 + n], in_=rT[:, cb, c0:c0 + n], func=AF.Identity), waits=[f3])
            sg_free = [None, None]
            ys_free = [None, None]
            for t in range(4):
                ws = wt_i % 2
                wt_i += 1
                d_w = P.dma("pool", lambda e, t=t, ws=ws: e.dma_start(out=wo[ws][:], in_=w_pg_v[:, :, 512 * t:512 * t + 512]), f"wo{ws}", waits=[wo_free[ws]])
                lastpe = None
                for cbl in range(4):
                    cb = 4 * t + cbl
                    for ci, (c0, n) in enumerate(CH):
                        mmA = None
                        for k in range(16):
                            mmA = P.op("pe", lambda e, k=k, ws=ws, cbl=cbl, c0=c0, n=n: e.matmul(
                                mb[0][:, 0:n], wo[ws][:, k, 128 * cbl:128 * cbl + 128], hbf[:, k, c0:c0 + n], start=(k == 0), stop=(k == 15)),
                                waits=[d_w, h_last, mb_free[0]] if k == 0 else [], sig=(k == 15))
                        mmB = None
                        for k in range(2):
                            mmB = P.op("pe", lambda e, k=k, cb=cb, c0=c0, n=n: e.matmul(
                                mb[1][:, 0:n], wpe[:, k, 128 * cb:128 * cb + 128], pTb[:, k, c0:c0 + n], start=(k == 0), stop=(k == 1)),
                                waits=[l_pe, l_pt, mb_free[1]] if k == 0 else [], sig=(k == 1))
                        lastpe = mmB
                        sl = cnt6 % 2
                        cnt6 += 1
                        g1 = P.op("act", lambda e, sl=sl, cb=cb, n=n: e.activation(out=sgm[sl][:, 0:n], in_=mb[0][:, 0:n], func=AF.Sigmoid,
                                                                                 bias=vec16_t[:, 3, cb:cb + 1], scale=1.0), waits=[mmA, sg_free[sl]])
                        mb_free[0] = g1
                        g2 = P.op("dve", lambda e, sl=sl, n=n: e.tensor_tensor(out=yst[sl][:, 0:n], in0=sgm[sl][:, 0:n], in1=mb[1][:, 0:n], op=ALU.mult),
                                  waits=[g1, mmB, ys_free[sl]])
                        mb_free[1] = g2
                        sg_free[sl] = g2
                        g3 = P.op("dve", lambda e, sl=sl, cb=cb, c0=c0, n=n: e.tensor_tensor(out=yst[sl][:, 0:n], in0=yst[sl][:, 0:n], in1=rT[:, cb, c0:c0 + n], op=ALU.add),
                                  waits=[g2])
                        dd = P.dma("sp", lambda e, sl=sl, cb=cb, c0=c0, n=n: e.dma_start(out=yT_out[128 * cb:128 * cb + 128, c0:c0 + n], in_=yst[sl][:, 0:n]),
                                   f"yst{sl}", waits=[g3])
                        ys_free[sl] = dd
                wo_free[ws] = lastpe
            P.wait("sp", [ys_free[0], ys_free[1]])
            P.wait("act", [h_last])
            with (nc.named_scope(f'PH6') if scopes else ExitStack()):
                with nc.Block() as blk:
                    P.emit(blk)
    return nc


_NC_CACHE = {}


def _prep_core(c, inp):
    b, j = c // 4, c % 4
    x = inp["x_prompt"][b]
    xs = inp["x_sample"][4 * c:4 * c + 4, 0, :]
    xoT = np.zeros((D, NT), np.float32)
    hmask = np.ones((128, 4), np.float32)
    for i in range(4):
        g = 4 * i + j
        t0 = 256 * g
        if g > 0:
            xoT[:, 288 * i:288 * i + 32] = x[t0 - 32:t0].T
        else:
            hmask[:, i] = 0.0
        xoT[:, 288 * i + 32:288 * i + 288] = x[t0:t0 + 256].T
    xoT[:, 1152:1156] = xs.T
    npad = 3 - j
    xwT = np.zeros((D, 4096), np.float32)
    for n_ in range(16):
        g = n_ - npad
        if g >= 0:
            xwT[:, 256 * n_:256 * n_ + 256] = x[256 * g:256 * g + 256].T
    candb = np.full((128, 4, 16), -1e30, np.float32)
    cand01 = np.zeros((128, 4, 16), np.float32)
    own01 = np.zeros((128, 4, 16), np.float32)
    for i in range(4):
        candb[:, i, npad:4 * i + 3] = 0.0
        cand01[:, i, npad:4 * i + 3] = 1.0
        own01[:, i, 4 * i + 3] = 1.0
    xcT = np.zeros((D, NCc), np.float32)
    pTm = np.zeros((256, NCc), np.float32)
    pp = inp["p_prompt"][0, b]
    for i in range(4):
        g = 4 * i + j
        xcT[:, 256 * i:256 * i + 256] = x[256 * g:256 * g + 256].T
        pTm[:, 256 * i:256 * i + 256] = pp[256 * g:256 * g + 256].T
    xcT[:, 1024:1028] = xs.T
    pTm[:, 1024:1028] = inp["p_sample"][0, 4 * c:4 * c + 4, 0, :].T
    st = inp["state_conv"][0, 4 * c:4 * c + 4]
    stT = np.ascontiguousarray(st.reshape(4, 30, 8, 128).transpose(3, 0, 2, 1))
    ptrep = np.ascontiguousarray(np.broadcast_to(inp["page_table"][4 * c:4 * c + 4].reshape(1, 256), (128, 256)).astype(np.int32))
    return {"xoT": xoT, "hmask": hmask, "xwT": xwT, "candb": candb, "cand01": cand01, "own01": own01,
            "xcT": xcT, "pT": pTm, "stT": stT, "ptrep": ptrep, "st_raw": np.ascontiguousarray(st)}


def kernel(**inp):
    inp = {k: np.asarray(v) for k, v in inp.items()}
    if "nc" not in _NC_CACHE:
        _NC_CACHE["nc"] = build()
    nc = _NC_CACHE["nc"]
    b_in = inp["b_in"][0]
    bcol = np.ascontiguousarray(b_in.reshape(56, 128).T)
    brep = np.ascontiguousarray(np.broadcast_to(np.concatenate([b_in[2048:3072], b_in[3072:4096]])[None, :], (128, 2048)))
    kk = np.arange(128)[:, None]
    ss = np.arange(256)[None, :]
    negmask = np.concatenate([np.where(kk <= ss, 0.0, -1e30), np.where(kk + 128 <= ss, 0.0, -1e30)], axis=1).astype(np.float32)
    shared = {"w_in": np.ascontiguousarray(inp["w_in"][0]), "bcol": bcol, "brep": brep,
              "negmask": negmask, "ident": np.eye(128, dtype=np.float32)}
    def pcol(v, n):
        return v.reshape(n, 128).T
    shared["wdw"] = np.ascontiguousarray(inp["w_dw"][0].reshape(31, 8, 128).transpose(2, 1, 0))
    shared["vec8"] = np.ascontiguousarray(np.stack([pcol(inp[k][0], 8) for k in ("b_dw", "g_cn", "b_cn", "b_pw")], axis=1))
    shared["vec16"] = np.ascontiguousarray(np.stack([pcol(inp[k][0], 16) for k in ("b_out", "g_ln", "b_ln", "b_pg")], axis=1))
    for k in ("w_pw", "w_out", "w_pg", "w_pe"):
        shared[k] = np.ascontiguousarray(inp[k][0])
    shared["ck"] = inp["cache_k"][0].reshape(2560 * 128, 1024)
    shared["cv"] = inp["cache_v"][0].reshape(2560 * 128, 1024)
    shared["iota"] = np.arange(128, dtype=np.float32).reshape(128, 1)
    dsel = np.zeros((128, 32, 16), np.float32)
    for h in range(8):
        dsel[h, :, h] = 1.0
        dsel[h, :, 8 + h] = 1.0
    shared["dsel"] = dsel
    in_maps = []
    for c in range(8):
        m = dict(shared)
        m.update(_prep_core(c, inp))
        in_maps.append(m)
    res = run_bass_kernel_spmd(nc, in_maps, core_ids=list(range(8)))
    R = res.results

    y_prompt = np.zeros((2, 4096, 2048), np.float32)
    y_sample = np.zeros((32, 1, 2048), np.float32)
    k_p = np.zeros((1, 2, 4096, 8, 128), np.float32)
    v_p = np.zeros((1, 2, 4096, 8, 128), np.float32)
    c_p = np.zeros((1, 2, 30, 1024), np.float32)
    k_s = np.zeros((1, 32, 1, 8, 128), np.float32)
    v_s = np.zeros((1, 32, 1, 8, 128), np.float32)
    c_s = np.zeros((1, 32, 30, 1024), np.float32)
    for c in range(8):
        b, j = c // 4, c % 4
        r = R[c]
        kT = r["kT_out"]
        vv = r["v_out"]
        for i in range(4):
            g = 4 * i + j
            t0 = 256 * g
            k_p[0, b, t0:t0 + 256] = kT[:, :, 288 * i + 32:288 * i + 288].transpose(2, 0, 1)
            v_p[0, b, t0:t0 + 256] = vv[256 * i:256 * i + 256].reshape(256, 8, 128)
        k_s[0, 4 * c:4 * c + 4, 0] = kT[:, :, 1152:1156].transpose(2, 0, 1)
        v_s[0, 4 * c:4 * c + 4, 0] = r["vsT_out"].transpose(2, 1, 0)
        yT = r["yT_out"]
        for i in range(4):
            g = 4 * i + j
            y_prompt[b, 256 * g:256 * g + 256] = yT[:, 256 * i:256 * i + 256].T
        y_sample[4 * c:4 * c + 4, 0] = yT[:, 1024:1028].T
        u = r["uT_out"]
        if j == 3:
            c_p[0, b] = u[:, :, 0:30].transpose(2, 1, 0).reshape(30, 1024)
        c_s[0, 4 * c:4 * c + 4, 29] = u[:, :, 30:34].transpose(2, 1, 0).reshape(4, 1024)
    for c in range(8):
        c_s[0, 4 * c:4 * c + 4, 0:29] = R[c]["cs_out"]
    return (y_prompt, y_sample, k_p, v_p, c_p, k_s, v_s, c_s)
```

```python
import numpy as np
import concourse.bass as bass
import concourse.mybir as mybir
from concourse.bass_utils import run_bass_kernel_spmd

F32 = mybir.dt.float32
BF16 = mybir.dt.bfloat16
I32 = mybir.dt.int32
AF = mybir.ActivationFunctionType
ALU = mybir.AluOpType
AX = mybir.AxisListType

D = 2048
NT = 1156
NCc = 1028
ENG = ("pe", "act", "dve", "pool", "sp")


class Prog:
    def __init__(self, nc, es):
        self.nc = nc
        self.es = es
        self.sems = {}
        self.reset()

    def reset(self):
        self.ops = {k: [] for k in ENG}

    def _sem(self, name):
        if name not in self.sems:
            h = self.es.enter_context(self.nc.semaphore(name))
            self.sems[name] = [h, 0]
        return self.sems[name]

    def op(self, eng, fn, waits=(), sig=True):
        dep = None
        s = None
        if sig:
            s = self._sem("m_" + eng)
            s[1] += 1
            dep = (s[0], s[1])
        self.ops[eng].append((tuple(w for w in waits if w is not None), fn, s[0] if sig else None, 1))
        return dep

    def dma(self, eng, fn, semname, waits=()):
        s = self._sem("d_" + semname)
        s[1] += 16
        self.ops[eng].append((tuple(w for w in waits if w is not None), fn, s[0], 16))
        return (s[0], s[1])

    def wait(self, eng, waits):
        self.ops[eng].append((tuple(w for w in waits if w is not None), None, None, 0))

    def emit(self, blk):
        ops = self.ops

        def run(e, lst):
            seen = {}
            for waits, fn, sem, inc in lst:
                for (h, v) in waits:
                    k = id(h)
                    if seen.get(k, -1) >= v:
                        continue
                    seen[k] = v
                    e.wait_ge(h, v)
                if fn is not None:
                    ins = fn(e)
                    if sem is not None:
                        ins.then_inc(sem, inc)

        @blk.tensor
        def _(e):
            run(e, ops["pe"])

        @blk.scalar
        def _(e):
            run(e, ops["act"])

        @blk.vector
        def _(e):
            run(e, ops["dve"])

        @blk.gpsimd
        def _(e):
            run(e, ops["pool"])

        @blk.sync
        def _(e):
            run(e, ops["sp"])

        self.reset()


def emit_p4(nc, P, st4, ck, cv, ptrep_d, iota_d, dsel_d, ident_d, qs, ksT, vsT, gasT, write_out, dbgfn=None, holder=None, shared_ps=None):
    def sb(name, shape, dtype):
        return st4.enter_context(nc.sbuf_tensor(name, list(shape), dtype))
    def ps(name, shape):
        return st4.enter_context(nc.psum_tensor(name, list(shape), F32))
    SC = 128.0 ** -0.5
    ptab = sb("ptab", [128, 256], I32)
    iota = sb("iota4", [128, 1], F32)
    idx = sb("idx4", [128, 256], I32)
    dsel = sb("dsel_sb", [128, 32, 16], F32)
    identf = sb("identf4", [128, 128], F32)
    onesf = sb("onesf4", [128, 128], F32)
    onesb = sb("onesb4", [128, 2], BF16)
    qcol = [sb(f"qcol{i}", [128, 128], F32) for i in range(2)]
    qrep = sb("qrep", [128, 8, 128], F32)
    vrep = sb("vrep", [128, 8, 128], F32)
    garep = sb("garep", [128, 8, 128], F32)
    kpg = [sb(f"kpg{i}", [128, 1024], F32) for i in range(3)]
    prod = [sb(f"prod{i}", [128, 8, 128], F32) for i in range(2)]
    sc = sb("sc4", [128, 32, 16], F32)
    psc = sb("psc4", [128, 32, 16], BF16)
    vpg = [sb(f"vpg{i}", [128, 1024], BF16) for i in range(4)]
    gsel = sb("gsel", [128, 32, 16], F32)
    gT = sb("gT4", [128, 32], F32)
    t8 = sb("t84", [128, 8], F32)
    wsl = sb("wsl4", [128, 32], F32)
    accs = sb("accs4", [128, 8, 128], F32)
    accden = sb("accden4", [128, 2], F32)
    qk = sb("qk4", [128, 8], F32)
    pnew = sb("pnew4", [128, 1], F32)
    rdn = sb("rdn4", [128, 1], F32)
    osT = sb("osT4", [128, 8], F32)
    misc4 = ps("misc4", [128, 512])
    rp_ps = [misc4[:, 0:128], misc4[:, 0:128]]
    ovA = ps("ovA", [128, 512])
    gs_ps = ovA[:, :].rearrange("p (n c) -> p n c", c=16)
    ovB = ps("ovB", [128, 512])
    den_ps = misc4[:, 256:258]
    sn_ps = misc4[:, 260:261]
    tr_ps = misc4[:, 264:272]

    ckv = ck[:, :]
    cvv = cv[:, :]
    l0 = P.dma("sp", lambda e: e.dma_start(out=ptab[:], in_=ptrep_d[:, :]), "c0")
    l1 = P.dma("sp", lambda e: e.dma_start(out=iota[:], in_=iota_d[:, :]), "c1")
    l2 = P.dma("sp", lambda e: e.dma_start(out=dsel[:], in_=dsel_d[:, :, :]), "c2")
    l3 = P.dma("sp", lambda e: e.dma_start(out=identf[:], in_=ident_d[:, :]), "c3")
    m0 = P.op("dve", lambda e: e.memset(onesf[:], 1.0))
    m1 = P.op("dve", lambda e: e.memset(onesb[:], 1.0))
    ix = P.op("dve", lambda e: e.tensor_scalar(out=idx[:], in0=ptab[:], scalar1=128.0, scalar2=iota[:, 0:1], op0=ALU.mult, op1=ALU.add),
              waits=[l0, l1])
    P.wait("pool", [ix])
    P.wait("pe", [l3, m0, m1])
    P.wait("dve", [l2])
    yield
    kfree = [None] * 3
    vfree = [None] * 4
    pfree = [None] * 2
    qc_free = [None] * 2
    rp_free = [None] * 2
    kc = vc = pc = rc = 0
    prev_sample = None
    accd_prev = [None]
    for b_ in range(4):
        rep_last = None
        for (srcT, dst) in ((qs, qrep), (vsT, vrep), (gasT, garep)):
            for h in range(8):
                s_ = rc % 2
                rc += 1
                a = P.op("dve", lambda e, s_=s_, srcT=srcT, h=h, b_=b_: e.tensor_scalar(
                    out=qcol[s_][:], in0=onesf[:], scalar1=srcT[:, h, b_:b_ + 1], scalar2=None, op0=ALU.mult), waits=[qc_free[s_], prev_sample])
                mm = P.op("pe", lambda e, s_=s_: e.matmul(rp_ps[s_][:, :], qcol[s_][:], identf[:], start=True, stop=True), waits=[a, rp_free[0], rp_free[1]])
                qc_free[s_] = mm
                cp = P.op("act", lambda e, s_=s_, dst=dst, h=h: e.activation(out=dst[:, h, :], in_=rp_ps[s_][:, :], func=AF.Identity), waits=[mm, prev_sample])
                rp_free[s_] = cp
                rep_last = cp
            yield
        red = None
        for pg in range(64):
            ks_ = kc % 3
            kc += 1
            col = 64 * b_ + pg
            dk = P.dma("pool", lambda e, ks_=ks_, col=col: e.indirect_dma_start(
                out=kpg[ks_][:], out_offset=None, in_=ckv, in_offset=bass.IndirectOffsetOnAxis(ap=idx[:, col:col + 1], axis=0)),
                f"kpg{ks_}", waits=[kfree[ks_]])
            pr = pc % 2
            pc += 1
            mu = P.op("dve", lambda e, ks_=ks_, pr=pr: e.tensor_tensor(out=prod[pr][:], in0=kpg[ks_][:].rearrange("p (h d) -> p h d", h=8), in1=qrep[:], op=ALU.mult),
                      waits=[dk, rep_last, pfree[pr]])
            kfree[ks_] = mu
            red = P.op("dve", lambda e, pr=pr, pg=pg: e.tensor_reduce(out=sc[:, pg // 2, 8 * (pg % 2):8 * (pg % 2) + 8], in_=prod[pr][:], axis=AX.X, op=ALU.add),
                       waits=[mu, prev_sample])
            pfree[pr] = red
            yield
        g1 = P.op("pe", lambda e: e.matmul(gs_ps, onesf[:], sc[:], start=True, stop=True), waits=[red, prev_sample, accd_prev[0]])
        g2 = P.op("dve", lambda e: e.tensor_tensor(out=gsel[0:8], in0=gs_ps[0:8], in1=dsel[0:8], op=ALU.mult), waits=[g1])
        g3 = P.op("dve", lambda e: e.tensor_reduce(out=gT[0:8, :], in_=gsel[0:8], axis=AX.X, op=ALU.add), waits=[g2])
        g4 = P.op("dve", lambda e: e.max(out=t8[0:8, :], in_=gT[0:8, :]), waits=[g3])
        g5 = P.op("dve", lambda e: e.tensor_scalar(out=wsl[0:8, :], in0=gT[0:8, :], scalar1=t8[0:8, 2:3], scalar2=None, op0=ALU.is_ge), waits=[g4])
        ex = P.op("act", lambda e: e.activation(out=psc[:], in_=sc[:], func=AF.Exp, scale=SC), waits=[red, prev_sample])
        accd = None
        for n_ in range(32):
            dvs = []
            sl = []
            for e_ in range(2):
                vs_ = vc % 4
                vc += 1
                col = 64 * b_ + 2 * n_ + e_
                dvs.append(P.dma("pool", lambda e, vs_=vs_, col=col: e.indirect_dma_start(
                    out=vpg[vs_][:], out_offset=None, in_=cvv, in_offset=bass.IndirectOffsetOnAxis(ap=idx[:, col:col + 1], axis=0)),
                    f"vpg{vs_}", waits=[vfree[vs_]]))
                sl.append(vs_)
            last = None
            for (dst, lo) in ((ovA, 0), (ovB, 4)):
                for e_ in range(2):
                    last = P.op("pe", lambda e, dst=dst, lo=lo, e_=e_, n_=n_, v=sl[e_]: e.matmul(
                        dst[0:8, :], psc[:, n_, 8 * e_:8 * e_ + 8], vpg[v][:, 128 * lo:128 * lo + 512], start=(e_ == 0), stop=(e_ == 1)),
                        waits=[ex, dvs[0], dvs[1], accd, g2] if (lo == 0 and e_ == 0) else [], sig=False)
            for e_ in range(2):
                last = P.op("pe", lambda e, e_=e_, n_=n_: e.matmul(den_ps[0:8, :], psc[:, n_, 8 * e_:8 * e_ + 8], onesb[:], start=(e_ == 0), stop=(e_ == 1)),
                            sig=(e_ == 1))
            vfree[sl[0]] = last
            vfree[sl[1]] = last
            if n_ == 0:
                a1 = P.op("dve", lambda e: e.tensor_scalar(out=accs[0:8, 0:4, :], in0=ovA[0:8, :].rearrange("p (h d) -> p h d", h=4), scalar1=wsl[0:8, 0:1], scalar2=None, op0=ALU.mult), waits=[last, g5, prev_sample])
                a2 = P.op("dve", lambda e: e.tensor_scalar(out=accs[0:8, 4:8, :], in0=ovB[0:8, :].rearrange("p (h d) -> p h d", h=4), scalar1=wsl[0:8, 0:1], scalar2=None, op0=ALU.mult), waits=[a1])
                accd = P.op("dve", lambda e: e.tensor_scalar(out=accden[0:8, :], in0=den_ps[0:8, :], scalar1=wsl[0:8, 0:1], scalar2=None, op0=ALU.mult), waits=[a2])
            else:
                a1 = P.op("dve", lambda e, n_=n_: e.scalar_tensor_tensor(out=accs[0:8, 0:4, :], in0=ovA[0:8, :].rearrange("p (h d) -> p h d", h=4), scalar=wsl[0:8, n_:n_ + 1], in1=accs[0:8, 0:4, :],
                                                                        op0=ALU.mult, op1=ALU.add), waits=[last, accd])
                a2 = P.op("dve", lambda e, n_=n_: e.scalar_tensor_tensor(out=accs[0:8, 4:8, :], in0=ovB[0:8, :].rearrange("p (h d) -> p h d", h=4), scalar=wsl[0:8, n_:n_ + 1], in1=accs[0:8, 4:8, :],
                                                                        op0=ALU.mult, op1=ALU.add), waits=[a1])
                accd = P.op("dve", lambda e, n_=n_: e.scalar_tensor_tensor(out=accden[0:8, :], in0=den_ps[0:8, :], scalar=wsl[0:8, n_:n_ + 1], in1=accden[0:8, :],
                                                                          op0=ALU.mult, op1=ALU.add), waits=[a2])
            yield
        accd_prev[0] = accd
        n1 = P.op("dve", lambda e, b_=b_: e.tensor_tensor(out=qk[:], in0=qs[:, :, b_], in1=ksT[:, :, b_], op=ALU.mult), waits=[prev_sample])
        n2 = P.op("pe", lambda e: e.matmul(sn_ps[0:8, :], qk[:], onesf[:, 0:1], start=True, stop=True), waits=[n1])
        n3 = P.op("act", lambda e: e.activation(out=pnew[0:8, :], in_=sn_ps[0:8, :], func=AF.Exp, scale=SC), waits=[n2])
        n4 = P.op("dve", lambda e: e.scalar_tensor_tensor(out=accs[0:8], in0=vrep[0:8], scalar=pnew[0:8, 0:1],
                                                         in1=accs[0:8], op0=ALU.mult, op1=ALU.add), waits=[n3, accd, rep_last])
        n5 = P.op("dve", lambda e: e.tensor_tensor(out=accden[0:8, 0:1], in0=accden[0:8, 0:1], in1=pnew[0:8, 0:1], op=ALU.add), waits=[n4])
        n6 = P.op("dve", lambda e: e.reciprocal(out=rdn[0:8, :], in_=accden[0:8, 0:1]), waits=[n5])
        n7 = P.op("dve", lambda e: e.scalar_tensor_tensor(out=accs[0:8], in0=accs[0:8], scalar=rdn[0:8, 0:1], in1=garep[0:8],
                                                         op0=ALU.mult, op1=ALU.mult), waits=[n6])
        if dbgfn is not None and b_ == 0:
            dd_ = dbgfn(dict(idx=idx, qrep=qrep, vrep=vrep, garep=garep, sc=sc, gT=gT, t8=t8, wsl=wsl, accs=accs, accden=accden, pnew=pnew, rdn=rdn, kpg0=kpg[0]), n7)
            for en_ in ("dve", "act", "pool", "pe"):
                P.wait(en_, dd_)
        cpl = None
        for h in range(8):
            tp = P.op("pe", lambda e, h=h: e.transpose(tr_ps[:, :], accs[0:8, h, :], identf[0:8, 0:8]), waits=[n7, cpl])
            cpl = P.op("act", lambda e, h=h: e.activation(out=osT[:, h:h + 1], in_=tr_ps[:, h:h + 1], func=AF.Identity), waits=[tp, prev_sample])
        prev_sample = write_out(b_, osT, cpl)
        if holder is not None:
            holder["last"] = prev_sample
        yield


def build(skip4=False, scopes=False):
    from contextlib import ExitStack
    nc = bass.Bass("TRN2", target_bir_lowering=False)
    dt = nc.dram_tensor

    def din(name, shape, dtype=F32):
        return dt(name, list(shape), dtype, kind="ExternalInput").ap()

    def dout(name, shape, dtype=F32):
        return dt(name, list(shape), dtype, kind="ExternalOutput").ap()

    xoT = din("xoT", [D, NT])
    w_in = din("w_in", [D, 7168])
    bcol = din("bcol", [128, 56])
    brep = din("brep", [128, 2048])
    hmask = din("hmask", [128, 4])
    xwT = din("xwT", [D, 4096])
    candb = din("candb", [128, 4, 16])
    cand01 = din("cand01", [128, 4, 16])
    own01 = din("own01", [128, 4, 16])
    negmask_d = din("negmask", [128, 512])
    ident_d = din("ident", [128, 128])
    kT_scr = dt("kT_scr", [8, 128, 4096], BF16, kind="Internal").ap()
    v_scr = dt("v_scr", [4096, 1024], BF16, kind="Internal").ap()
    mix_dbg = dout("mix_dbg", [128, 16, NCc], BF16)
    u_scr = dt("u_scr", [128, 8, NT], F32, kind="Internal").ap()
    wdw_d = din("wdw", [128, 8, 31])
    vec8 = din("vec8", [128, 4, 8])
    vec16 = din("vec16", [128, 4, 16])
    stT_d = din("stT", [128, 4, 8, 30])
    w_pw = din("w_pw", [1024, 1024])
    w_out = din("w_out", [D, D])
    w_pg = din("w_pg", [D, D])
    w_pe = din("w_pe", [256, D])
    xcT = din("xcT", [D, NCc])
    pT = din("pT", [256, NCc])
    yT_out = dout("yT_out", [D, NCc])
    if not skip4:
        ck_d = din("ck", [2560 * 128, 1024])
        cv_d = din("cv", [2560 * 128, 1024])
    ptrep_d = din("ptrep", [128, 256], I32)
    iota_d = din("iota", [128, 1])
    dsel_d = din("dsel", [128, 32, 16])
    st_raw = din("st_raw", [4, 30, 1024])
    cs_out = dout("cs_out", [4, 29, 1024])

    kT_out = dout("kT_out", [8, 128, NT])
    v_out = dout("v_out", [1024, 1024])
    vsT_out = dout("vsT_out", [128, 8, 4])
    uT_out = dout("uT_out", [128, 8, 34])

    es = ExitStack()
    with es:
        P = Prog(nc, es)

        def sb(name, shape, dtype):
            return es.enter_context(nc.sbuf_tensor(name, list(shape), dtype))

        mixT = sb("mixT", [128, 16, NCc], BF16)
        mid = ExitStack()

        def sbm(name, shape, dtype):
            return mid.enter_context(nc.sbuf_tensor(name, list(shape), dtype))
        qs = sb("qs", [128, 8, 4], F32)
        gasT = sb("gasT", [128, 8, 4], F32)
        vsT = sb("vsT", [128, 8, 4], F32)
        ksT = sb("ksT", [128, 8, 4], F32)
        bcol_t = sb("bcol_t", [128, 56], F32)
        brep_t = sb("brep_t", [128, 2048], F32)
        hmask_t = sb("hmask_t", [128, 4], F32)
        qT = sbm("qT", [128, 8, NCc], BF16)
        ga = sbm("ga", [128, 8, 1024], BF16)
        gcT = sbm("gcT", [128, 8, NCc], BF16)

        with ExitStack() as ps1:
            def sb1(name, shape, dtype):
                return ps1.enter_context(nc.sbuf_tensor(name, list(shape), dtype))
            xo = sb1("xo", [128, 16, NT], BF16)
            uT = sb1("uT", [128, 8, NT], F32)
            wt = [sb1(f"wt{i}", [128, 16, 512], BF16) for i in range(2)]
            kst = [sb1(f"kst{i}", [128, 292], F32) for i in range(2)]
            vst = [sb1(f"vst{i}", [128, 512], F32) for i in range(2)]
            sg = [sb1(f"sg{i}", [128, 292], F32) for i in range(2)]
            gtmp = [sb1(f"gtmp{i}", [128, 512], F32) for i in range(2)]
            pbank = [ps1.enter_context(nc.psum_tensor(f"p1b{i}", [128, 512], F32)) for i in range(4)]

            w_in_v = w_in.rearrange("(k p) n -> p k n", p=128)
            d_xo = P.dma("pool", lambda e: e.dma_start(out=xo[:], in_=xoT.rearrange("(k p) n -> p k n", p=128)), "xo")
            d_bc = P.dma("sp", lambda e: e.dma_start(out=bcol_t[:], in_=bcol[:, :]), "c0")
            d_br = P.dma("sp", lambda e: e.dma_start(out=brep_t[:], in_=brep[:, :]), "c1")
            d_hm = P.dma("sp", lambda e: e.dma_start(out=hmask_t[:], in_=hmask[:, :]), "c2")

            wt_free = [None] * 2
            bank_free = [None] * 4
            kst_free = [None] * 2
            vst_free = [None] * 2
            sg_free = [None] * 2
            gtmp_free = [None] * 2
            grp = [0]
            cnt = {"k": 0, "v": 0, "sg": 0, "g": 0}
            out_deps = []
            u_last = [None] * 8

            def mm_group(lhs_fn, rhs_fn, n_out, w_dep):
                b = grp[0] % 4
                grp[0] += 1
                dep = None
                for k in range(16):
                    waits = []
                    if k == 0:
                        waits = [w_dep, d_xo, bank_free[b]]
                    last = (k == 15)
                    dep = P.op("pe", (lambda e, k=k, b=b: e.matmul(pbank[b][:, 0:n_out], lhs_fn(k), rhs_fn(k), start=(k == 0), stop=(k == 15))),
                               waits=waits, sig=last)
                return b, dep

            for t in range(14):
                s = t % 2
                d_w = P.dma("pool", (lambda e, t=t, s=s: e.dma_start(out=wt[s][:], in_=w_in_v[:, :, t * 512:(t + 1) * 512])),
                            f"wt{s}", waits=[wt_free[s]])
                role = ["q", "k", "v", "ga", "a", "bg", "gc"][t // 2]
                last_pe = None
                if role in ("q", "k", "a", "bg", "gc"):
                    for cbl in range(4):
                        cb = 4 * t + cbl
                        hh = cb % 8
                        for i in range(4):
                            n = 292 if i == 3 else 288
                            c0 = 288 * i
                            b, dep = mm_group(lambda k, s=s, cbl=cbl: wt[s][:, k, cbl * 128:(cbl + 1) * 128],
                                              lambda k, c0=c0, n=n: xo[:, k, c0:c0 + n], n, d_w)
                            last_pe = dep
                            bias = bcol_t[:, cb:cb + 1]
                            if role == "q":
                                ev = P.op("act", lambda e, b=b, hh=hh, i=i, bias=bias: e.activation(
                                    out=qT[:, hh, 256 * i:256 * i + 256], in_=pbank[b][:, 32:288], func=AF.Identity, bias=bias, scale=1.0),
                                    waits=[dep, d_bc])
                                if i == 3:
                                    P.op("act", lambda e, b=b, hh=hh, bias=bias: e.activation(
                                        out=qT[:, hh, 1024:1028], in_=pbank[b][:, 288:292], func=AF.Identity, bias=bias, scale=1.0))
                                    ev = P.op("act", lambda e, b=b, hh=hh, bias=bias: e.activation(
                                        out=qs[:, hh, :], in_=pbank[b][:, 288:292], func=AF.Identity, bias=bias, scale=1.0))
                                bank_free[b] = ev
                            elif role == "k":
                                ks = cnt["k"] % 2
                                cnt["k"] += 1
                                ev = P.op("act", lambda e, b=b, ks=ks, n=n, bias=bias: e.activation(
                                    out=kst[ks][:, 0:n], in_=pbank[b][:, 0:n], func=AF.Identity, bias=bias, scale=1.0),
                                    waits=[dep, d_bc, kst_free[ks]])
                                if i == 3:
                                    ev2 = P.op("act", lambda e, b=b, hh=hh, bias=bias: e.activation(
                                        out=ksT[:, hh, :], in_=pbank[b][:, 288:292], func=AF.Identity, bias=bias, scale=1.0))
                                    bank_free[b] = ev2
                                else:
                                    bank_free[b] = ev
                                dd = P.dma("sp", lambda e, ks=ks, hh=hh, c0=c0, n=n: e.dma_start(
                                    out=kT_out[hh, :, c0:c0 + n], in_=kst[ks][:, 0:n]), f"kst{ks}", waits=[ev])
                                kst_free[ks] = dd
                            elif role == "a":
                                ev = P.op("act", lambda e, b=b, hh=hh, c0=c0, n=n, bias=bias: e.activation(
                                    out=uT[:, hh, c0:c0 + n], in_=pbank[b][:, 0:n], func=AF.Identity, bias=bias, scale=1.0),
                                    waits=[dep, d_bc])
                                bank_free[b] = ev
                            elif role == "bg":
                                ss = cnt["sg"] % 2
                                cnt["sg"] += 1
                                ev = P.op("act", lambda e, b=b, ss=ss, n=n, bias=bias: e.activation(
                                    out=sg[ss][:, 0:n], in_=pbank[b][:, 0:n], func=AF.Sigmoid, bias=bias, scale=1.0),
                                    waits=[dep, d_bc, sg_free[ss]])
                                bank_free[b] = ev
                                mu = P.op("dve", lambda e, ss=ss, hh=hh, c0=c0, n=n: e.tensor_tensor(
                                    out=uT[:, hh, c0:c0 + n], in0=uT[:, hh, c0:c0 + n], in1=sg[ss][:, 0:n], op=ALU.mult),
                                    waits=[ev])
                                sg_free[ss] = mu
                                u_last[hh] = mu
                            elif role == "gc":
                                ev = P.op("act", lambda e, b=b, hh=hh, i=i, bias=bias: e.activation(
                                    out=gcT[:, hh, 256 * i:256 * i + 256], in_=pbank[b][:, 32:288], func=AF.Silu, bias=bias, scale=1.0),
                                    waits=[dep, d_bc])
                                if i == 3:
                                    ev = P.op("act", lambda e, b=b, hh=hh, bias=bias: e.activation(
                                        out=gcT[:, hh, 1024:1028], in_=pbank[b][:, 288:292], func=AF.Silu, bias=bias, scale=1.0))
                                bank_free[b] = ev
                else:
                    half = t % 2
                    boff = (0 if role == "v" else 1024) + half * 512
                    for tt in range(8):
                        i, hf = tt // 2, tt % 2
                        c0 = 288 * i + 32 + 128 * hf
                        b, dep = mm_group(lambda k, c0=c0: xo[:, k, c0:c0 + 128],
                                          lambda k, s=s: wt[s][:, k, :], 512, d_w)
                        last_pe = dep
                        if role == "v":
                            vs = cnt["v"] % 2
                            cnt["v"] += 1
                            ev = P.op("dve", lambda e, b=b, vs=vs, boff=boff: e.tensor_tensor(
                                out=vst[vs][:], in0=pbank[b][:, :], in1=brep_t[:, boff:boff + 512], op=ALU.add),
                                waits=[dep, d_br, vst_free[vs]])
                            bank_free[b] = ev
                            dd = P.dma("sp", lambda e, vs=vs, tt=tt, half=half: e.dma_start(
                                out=v_out[tt * 128:(tt + 1) * 128, half * 512:(half + 1) * 512], in_=vst[vs][:]),
                                f"vst{vs}", waits=[ev])
                            vst_free[vs] = dd
                        else:
                            gs = cnt["g"] % 2
                            cnt["g"] += 1
                            ev = P.op("dve", lambda e, b=b, gs=gs, boff=boff: e.tensor_tensor(
                                out=gtmp[gs][:], in0=pbank[b][:, :], in1=brep_t[:, boff:boff + 512], op=ALU.add),
                                waits=[dep, d_br, gtmp_free[gs]])
                            bank_free[b] = ev
                            a2 = P.op("act", lambda e, gs=gs, tt=tt, half=half: e.activation(
                                out=ga[:, tt, half * 512:(half + 1) * 512], in_=gtmp[gs][:], func=AF.Silu),
                                waits=[ev])
                            gtmp_free[gs] = a2
                    for cbl in range(4):
                        cb = 4 * t + cbl
                        hh = cb % 8
                        b, dep = mm_group(lambda k, s=s, cbl=cbl: wt[s][:, k, cbl * 128:(cbl + 1) * 128],
                                          lambda k: xo[:, k, 1152:1156], 4, d_w)
                        last_pe = dep
                        bias = bcol_t[:, cb:cb + 1]
                        if role == "v":
                            ev = P.op("act", lambda e, b=b, hh=hh, bias=bias: e.activation(
                                out=vsT[:, hh, :], in_=pbank[b][:, 0:4], func=AF.Identity, bias=bias, scale=1.0),
                                waits=[dep, d_bc])
                        else:
                            ev = P.op("act", lambda e, b=b, hh=hh, bias=bias: e.activation(
                                out=gasT[:, hh, :], in_=pbank[b][:, 0:4], func=AF.Silu, bias=bias, scale=1.0),
                                waits=[dep, d_bc])
                        bank_free[b] = ev
                wt_free[s] = last_pe

            hm = None
            for i in range(4):
                hm = P.op("dve", lambda e, i=i: e.tensor_scalar(
                    out=uT[:, :, 288 * i:288 * i + 32], in0=uT[:, :, 288 * i:288 * i + 32],
                    scalar1=hmask_t[:, i:i + 1], scalar2=None, op0=ALU.mult),
                    waits=[d_hm] + [u for u in u_last])
            fin = []
            fin.append(P.dma("sp", lambda e: e.dma_start(out=uT_out[:, :, 0:30], in_=uT[:, :, 1122:1152]), "fo0", waits=[hm]))
            fin.append(P.dma("sp", lambda e: e.dma_start(out=uT_out[:, :, 30:34], in_=uT[:, :, 1152:1156]), "fo1", waits=[hm]))
            fin.append(P.dma("sp", lambda e: e.dma_start(out=u_scr[:, :, :], in_=uT[:]), "fo3", waits=[hm]))
            fin.append(P.dma("sp", lambda e: e.dma_start(out=cs_out[:, :, :], in_=st_raw[:, 1:30, :]), "fo4"))
            lastact = P.op("act", lambda e: e.activation(out=sg[0][:, 0:4], in_=vsT[:, 0, :], func=AF.Identity), waits=[sg_free[0], sg_free[1]])
            fin.append(P.dma("sp", lambda e: e.dma_start(out=vsT_out[:, :, :], in_=vsT[:]), "fo2", waits=[lastact]))
            P.wait("sp", fin + [kst_free[0], kst_free[1], vst_free[0], vst_free[1]])
            P.wait("act", [gtmp_free[0], gtmp_free[1], lastact])
            P.wait("dve", [hm])
            with (nc.named_scope(f'PH1') if scopes else ExitStack()):
                with nc.Block() as blk:
                    P.emit(blk)

        kmT = sbm("kmT", [128, 8, 16], BF16)
        ksum = sbm("ksum", [128, 8, 16], F32)
        with ExitStack() as ps2:
            def sb2(name, shape, dtype):
                return ps2.enter_context(nc.sbuf_tensor(name, list(shape), dtype))
            wk = sb2("wk", [128, 16, 1024], BF16)
            wv = sb2("wv", [128, 16, 1024], BF16)
            xw = [sb2(f"xw{i}", [128, 16, 512], BF16) for i in range(2)]
            kstg = [sb2(f"kstg{i}", [128, 8, 512], BF16) for i in range(1)]
            vstg = [sb2(f"vstg{i}", [128, 4, 1024], BF16) for i in range(1)]
            pb2 = [ps2.enter_context(nc.psum_tensor(f"p2b{i}", [128, 512], F32)) for i in range(4)]
            xwT_v = xwT.rearrange("(k p) n -> p k n", p=128)
            d_wk = P.dma("pool", lambda e: e.dma_start(out=wk[:], in_=w_in_v[:, :, 1024:2048]), "wk")
            d_x = [None] * 8
            d_x[0] = P.dma("pool", lambda e: e.dma_start(out=xw[0][:], in_=xwT_v[:, :, 0:512]), "xw0")
            d_wv = P.dma("pool", lambda e: e.dma_start(out=wv[:], in_=w_in_v[:, :, 2048:3072]), "wv")
            xw_free = [None, None]
            kstg_free = [None, None]
            vstg_free = [None, None]
            bfree = [None] * 4
            g2 = [0]
            red_last = None

            def mm2(lhs_fn, rhs_fn, waits):
                b = g2[0] % 4
                g2[0] += 1
                dep = None
                for k in range(16):
                    dep = P.op("pe", (lambda e, k=k, b=b: e.matmul(pb2[b][:, :], lhs_fn(k), rhs_fn(k), start=(k == 0), stop=(k == 15))),
                               waits=(list(waits) + [bfree[b]]) if k == 0 else [], sig=(k == 15))
                return b, dep

            for c in range(8):
                s_ = c % 2
                if c + 1 < 8:
                    d_x[c + 1] = P.dma("pool", (lambda e, c=c: e.dma_start(out=xw[(c + 1) % 2][:], in_=xwT_v[:, :, 512 * (c + 1):512 * (c + 2)])),
                                       f"xw{(c + 1) % 2}", waits=[xw_free[(c + 1) % 2]])
                evs = []
                for h in range(8):
                    b, dep = mm2(lambda k, h=h: wk[:, k, 128 * h:128 * h + 128], lambda k, s_=s_: xw[s_][:, k, :], [d_wk, d_x[c]])
                    ev = P.op("act", lambda e, b=b, s_=s_, h=h: e.activation(
                        out=kstg[0][:, h, :], in_=pb2[b][:, :], func=AF.Identity, bias=bcol_t[:, 8 + h:9 + h], scale=1.0),
                        waits=[dep, kstg_free[0]])
                    bfree[b] = ev
                    evs.append(ev)
                r = None
                for blk_ in range(2):
                    r = P.op("dve", lambda e, s_=s_, c=c, blk_=blk_: e.tensor_reduce(
                        out=ksum[:, :, 2 * c + blk_], in_=kstg[0][:, :, 256 * blk_:256 * blk_ + 256], axis=AX.X, op=ALU.add),
                        waits=[evs[-1]])
                red_last = r
                dk = P.dma("sp", lambda e, s_=s_, c=c: e.dma_start(
                    out=kT_scr.rearrange("h d t -> d h t")[:, :, 512 * c:512 * c + 512], in_=kstg[0][:]), "kstg0", waits=[evs[-1]])
                kstg_free[0] = dk
                P.wait("act", [r]) if False else None
                kred = r
                vev = None
                lastpe = None
                for tt in range(4):
                    for half in range(2):
                        b, dep = mm2(lambda k, s_=s_, tt=tt: xw[s_][:, k, 128 * tt:128 * tt + 128],
                                     lambda k, half=half: wv[:, k, 512 * half:512 * half + 512], [d_wv, d_x[c]])
                        lastpe = dep
                        vev = P.op("dve", lambda e, b=b, s_=s_, tt=tt, half=half: e.tensor_tensor(
                            out=vstg[0][:, tt, 512 * half:512 * half + 512], in0=pb2[b][:, :], in1=brep_t[:, 512 * half:512 * half + 512], op=ALU.add),
                            waits=[dep, vstg_free[0]])
                        bfree[b] = vev
                dv = P.dma("sp", lambda e, s_=s_, c=c: e.dma_start(
                    out=v_scr[512 * c:512 * c + 512, :].rearrange("(t p) n -> p t n", p=128), in_=vstg[0][:]), "vstg0", waits=[vev])
                vstg_free[0] = dv
                xw_free[s_] = lastpe
                P.wait("act", [kred])
            km = P.op("dve", lambda e: e.tensor_scalar(out=kmT[:], in0=ksum[:], scalar1=1.0 / 256.0, scalar2=None, op0=ALU.mult),
                      waits=[red_last])
            P.wait("sp", [kstg_free[0], vstg_free[0]])
            P.wait("dve", [km])
            with (nc.named_scope(f'PH2') if scopes else ExitStack()):
                with nc.Block() as blk:
                    P.emit(blk)

        with ExitStack() as ps3:
            def sb3(name, shape, dtype):
                return ps3.enter_context(nc.sbuf_tensor(name, list(shape), dtype))
            KTh = [sb3(f"KTh{i}", [128, 4096], BF16) for i in range(2)]
            Vh = [sb3(f"Vh{i}", [128, 32, 130], BF16) for i in range(2)]
            Pt = [sb3(f"Pt{i}", [128, 512], BF16) for i in range(3)]
            gm = [sb3(f"gm{i}", [128, 16], F32) for i in range(2)]
            top8 = [sb3(f"top8{i}", [128, 8], F32) for i in range(2)]
            wsel = [sb3(f"wsel{i}", [128, 2, 16], F32) for i in range(2)]
            acc = [sb3(f"acc{i}", [128, 2, 129], F32) for i in range(2)]
            rden = [sb3(f"rden{i}", [128, 2], F32) for i in range(2)]
            attn_n = [sb3(f"attn_n{i}", [128, 128], F32) for i in range(2)]
            candb_t = sb3("candb_t", [128, 4, 16], F32)
            cand01_t = sb3("cand01_t", [128, 4, 16], F32)
            own01_t = sb3("own01_t", [128, 4, 16], F32)
            negm = sb3("negm", [128, 512], BF16)
            identb = sb3("identb", [128, 128], BF16)
            identf = sb3("identf", [128, 128], F32)
            misc3 = ps3.enter_context(nc.psum_tensor("misc3", [128, 512], F32))
            gps = misc3
            Sps = [ps3.enter_context(nc.psum_tensor(f"Sps{i}", [128, 512], F32)) for i in range(2)]
            Ops = [ps3.enter_context(nc.psum_tensor(f"Ops{i}", [128, 2, 256], F32)) for i in range(2)]
            Tps = [misc3[:, 128:256], misc3[:, 128:256]]

            cdeps = [
                P.dma("sp", lambda e: e.dma_start(out=candb_t[:], in_=candb[:, :, :]), "c0"),
                P.dma("sp", lambda e: e.dma_start(out=cand01_t[:], in_=cand01[:, :, :]), "c1"),
                P.dma("sp", lambda e: e.dma_start(out=own01_t[:], in_=own01[:, :, :]), "c2"),
                P.dma("sp", lambda e: e.dma_start(out=identf[:], in_=ident_d[:, :]), "c3"),
                P.dma("pool", lambda e: e.dma_start(out=negm[:], in_=negmask_d[:, :]), "c4"),
                P.dma("pool", lambda e: e.dma_start(out=identb[:], in_=ident_d[:, :]), "c5"),
            ]
            ones_dep = [P.op("dve", lambda e, i=i: e.memset(Vh[i][:, :, 128:130], 1.0)) for i in range(2)]
            P.wait("dve", cdeps[0:3])
            P.wait("pe", cdeps[3:6])
            holder4 = {}
            g4 = None
            if not skip4:
                def write_out(b_, osT, dep):
                    return P.op("act", lambda e, b_=b_: e.activation(out=mixT[:, 0:8, 1024 + b_], in_=osT[:, :], func=AF.Identity), waits=[dep])
                g4 = emit_p4(nc, P, ps3, ck_d, cv_d, ptrep_d, iota_d, dsel_d, ident_d, qs, ksT, vsT, gasT, write_out, holder=holder4)
            vis4 = [0]

            def step4():
                if g4 is None:
                    return
                vis4[0] += 1
                next(g4, None)
                if vis4[0] % 4 == 0:
                    next(g4, None)
            kv_free = [None, None]
            S_free = [None, None]
            O_free = [None, None]
            T_free = [None, None]
            Pt_free = [None, None, None]
            an_free = [None, None]
            gps_free = None
            cS = cO = cP = cT = 0
            SCALE = 128.0 ** -0.5
            mix_last = None
            qb = 0
            for h in range(8):
                s_ = h % 2
                d_k = P.dma("sp", lambda e, s_=s_, h=h: e.dma_start(out=KTh[s_][:], in_=kT_scr[h]), f"KTh{s_}", waits=[kv_free[s_]])
                d_v = P.dma("sp", lambda e, s_=s_, h=h: e.dma_start(
                    out=Vh[s_][:, :, 0:128], in_=v_scr.rearrange("(t p) n -> p t n", p=128)[:, :, 128 * h:128 * h + 128]),
                    f"Vh{s_}", waits=[kv_free[s_]])
                last_pe_h = None
                for i in range(4):
                    par = qb % 2
                    qb += 1
                    qc = 256 * i
                    gdep = None
                    for t in range(2):
                        gdep = P.op("pe", lambda e, t=t, h=h, qc=qc: e.matmul(
                            gps[:, 16 * t:16 * t + 16], qT[:, h, qc + 128 * t:qc + 128 * t + 128], kmT[:, h, :], start=True, stop=True),
                            waits=[gps_free, T_free[0], T_free[1]] if t == 0 else [])
                    wdeps = []
                    for t in range(2):
                        a1 = P.op("dve", lambda e, t=t, i=i: e.tensor_tensor(out=gm[t][:], in0=gps[:, 16 * t:16 * t + 16], in1=candb_t[:, i, :], op=ALU.add),
                                  waits=[gdep])
                        a2 = P.op("dve", lambda e, t=t: e.max(out=top8[t][:], in_=gm[t][:]), waits=[a1])
                        a3 = P.op("dve", lambda e, t=t, i=i, par=par: e.scalar_tensor_tensor(
                            out=wsel[par][:, t, :], in0=gm[t][:], scalar=top8[t][:, 2:3], in1=cand01_t[:, i, :], op0=ALU.is_ge, op1=ALU.mult),
                            waits=[a2])
                        a4 = P.op("dve", lambda e, t=t, i=i, par=par: e.tensor_tensor(
                            out=wsel[par][:, t, :], in0=wsel[par][:, t, :], in1=own01_t[:, i, :], op=ALU.add), waits=[a3])
                        wdeps.append(a4)
                        gps_free = a1
                    accdep = [None, None]
                    nblk = 4 * i + 4

                    def emit_qk(n_, h=h, s_=s_, qc=qc, nblk=nblk, d_k=d_k):
                        nonlocal cS
                        own = (n_ == nblk - 1)
                        sbk = cS % 2
                        cS += 1
                        first_w = [S_free[sbk], d_k, ones_dep[s_]]
                        if own:
                            P.op("pe", lambda e, sbk=sbk: e.matmul(Sps[sbk][:, :], identb[:], negm[:], start=True, stop=False),
                                 waits=first_w, sig=False)
                        sdep = None
                        for kt in range(2):
                            sdep = P.op("pe", lambda e, sbk=sbk, kt=kt, n_=n_, s_=s_, h=h, qc=qc, own=own: e.matmul(
                                Sps[sbk][:, 256 * kt:256 * kt + 256], KTh[s_][:, 256 * n_ + 128 * kt:256 * n_ + 128 * kt + 128],
                                qT[:, h, qc:qc + 256], start=(not own), stop=((not own) or kt == 1)),
                                waits=(first_w if (kt == 0 and not own) else []), sig=(kt == 1))
                        return sbk, sdep

                    pend = emit_qk(0)
                    for n_ in range(nblk):
                        sbk, sdep = pend
                        pp = cP % 3
                        cP += 1
                        edep = P.op("act", lambda e, pp=pp, sbk=sbk: e.activation(out=Pt[pp][:], in_=Sps[sbk][:, :], func=AF.Exp, scale=SCALE),
                                    waits=[sdep, Pt_free[pp]])
                        S_free[sbk] = edep
                        if n_ + 1 < nblk:
                            pend = emit_qk(n_ + 1)
                        ob = cO % 2
                        cO += 1
                        pvdep = None
                        for t in range(2):
                            for kt in range(2):
                                pvdep = P.op("pe", lambda e, ob=ob, t=t, kt=kt, pp=pp, s_=s_, n_=n_: e.matmul(
                                    Ops[ob][:, t, 0:129], Pt[pp][:, 256 * kt + 128 * t:256 * kt + 128 * t + 128],
                                    Vh[s_][:, 2 * n_ + kt, 0:129], start=(kt == 0), stop=(kt == 1)),
                                    waits=[edep, O_free[ob], d_v] if (t == 0 and kt == 0) else [], sig=(t == 1 and kt == 1))
                        Pt_free[pp] = pvdep
                        last_pe_h = pvdep
                        for t in range(2):
                            if n_ == 0:
                                accdep[t] = P.op("dve", lambda e, t=t, ob=ob, par=par: e.tensor_scalar(
                                    out=acc[par][:, t, :], in0=Ops[ob][:, t, 0:129], scalar1=wsel[par][:, t, 0:1], scalar2=None, op0=ALU.mult),
                                    waits=[pvdep, wdeps[t], an_free[par]])
                            else:
                                accdep[t] = P.op("dve", lambda e, t=t, ob=ob, par=par, n_=n_: e.scalar_tensor_tensor(
                                    out=acc[par][:, t, :], in0=Ops[ob][:, t, 0:129], scalar=wsel[par][:, t, n_:n_ + 1], in1=acc[par][:, t, :],
                                    op0=ALU.mult, op1=ALU.add), waits=[pvdep, accdep[t]])
                        O_free[ob] = accdep[1]
                        step4()
                    fin_last = None
                    for t in range(2):
                        r1 = P.op("dve", lambda e, t=t, par=par: e.reciprocal(out=rden[par][:, t:t + 1], in_=acc[par][:, t, 128:129]),
                                  waits=[accdep[t]])
                        asl = cT % 2
                        r2 = P.op("dve", lambda e, t=t, par=par, asl=asl, i=i, h=h: e.scalar_tensor_tensor(
                            out=attn_n[asl][:], in0=acc[par][:, t, 0:128], scalar=rden[par][:, t:t + 1], in1=ga[:, 2 * i + t, 128 * h:128 * h + 128],
                            op0=ALU.mult, op1=ALU.mult), waits=[r1, T_free[asl]])
                        tb = cT % 2
                        cT += 1
                        tp = P.op("pe", lambda e, tb=tb, asl=asl: e.transpose(Tps[tb][:, :], attn_n[asl][:], identf[:]),
                                  waits=[r2, T_free[0], T_free[1]])
                        cp = P.op("act", lambda e, tb=tb, h=h, qc=qc, t=t: e.activation(
                            out=mixT[:, h, qc + 128 * t:qc + 128 * t + 128], in_=Tps[tb][:, :], func=AF.Identity), waits=[tp])
                        T_free[tb] = cp
                        mix_last = cp
                        fin_last = r2
                    an_free[par] = fin_last
                kv_free[s_] = last_pe_h
            if g4 is not None:
                for _ in g4:
                    pass
                P.wait("act", [holder4["last"]])
            else:
                mz = P.op("dve", lambda e: e.memset(mixT[:, 0:8, 1024:1028], 0.0))
                P.wait("dve", [mz])
            P.wait("act", [mix_last])
            with (nc.named_scope(f'PH3') if scopes else ExitStack()):
                with nc.Block() as blk:
                    P.emit(blk)

        CH = [(0, 344), (344, 342), (686, 342)]
        with ExitStack() as ps5:
            def sb5(name, shape, dtype):
                return ps5.enter_context(nc.sbuf_tensor(name, list(shape), dtype))
            uB = sb5("uB", [128, 8, 4, 288], BF16)
            uTs = sb5("uTs", [128, 8, 4], F32)
            cTb = sb5("cTb", [128, 8, 4, 256], F32)
            cTs = sb5("cTs", [128, 8, 4], F32)
            cnT = sb5("cnT", [128, 8, NCc], BF16)
            wdw_t = sb5("wdw_t", [128, 8, 31], F32)
            vec8_t = sb5("vec8_t", [128, 4, 8], F32)
            stT = sb5("stT_sb", [128, 4, 8, 30], F32)
            stmp = sb5("stmp", [128, 8, 30], F32)
            s1t = sb5("s1t", [128, 8], F32)
            wpw = sb5("wpw", [128, 8, 1024], BF16)
            onesf = sb5("onesf", [128, 128], F32)
            epsT = sb5("epsT", [128, 1], F32)
            sqt = [sb5(f"sqt{i}", [128, 256], F32) for i in range(2)]
            mean_t = sb5("mean_t", [128, 256], F32)
            m2_t = sb5("m2_t", [128, 256], F32)
            rstd_t = sb5("rstd_t", [128, 256], F32)
            ntmp = [sb5(f"ntmp{i}", [128, 256], F32) for i in range(2)]
            S1p = ps5.enter_context(nc.psum_tensor("S1p", [128, 256], F32))
            S2p = ps5.enter_context(nc.psum_tensor("S2p", [128, 256], F32))
            pwb = [ps5.enter_context(nc.psum_tensor(f"pwb{i}", [128, 512], F32)) for i in range(2)]

            l_u = P.dma("pool", lambda e: e.dma_start(out=uB[:], in_=u_scr[:, :, 0:1152].rearrange("p g (i c) -> p g i c", c=288)), "c0")
            l_us = P.dma("sp", lambda e: e.dma_start(out=uTs[:], in_=u_scr[:, :, 1152:1156]), "c1")
            l_w = P.dma("sp", lambda e: e.dma_start(out=wdw_t[:], in_=wdw_d[:, :, :]), "c2")
            l_v8 = P.dma("sp", lambda e: e.dma_start(out=vec8_t[:], in_=vec8[:, :, :]), "c3")
            l_st = P.dma("sp", lambda e: e.dma_start(out=stT[:], in_=stT_d[:, :, :, :]), "c4")
            l_pw = P.dma("pool", lambda e: e.dma_start(out=wpw[:], in_=w_pw.rearrange("(k p) n -> p k n", p=128)), "c5")
            m1 = P.op("dve", lambda e: e.memset(onesf[:], 1.0))
            m2 = P.op("dve", lambda e: e.memset(epsT[:], 1e-5))
            P.wait("dve", [l_u, l_us, l_w, l_v8, l_st])
            identb5 = sb5("identb5", [128, 128], BF16)
            dg = [sb5(f"dg{i}", [128, 31, 128], BF16) for i in range(2)]
            cps = [ps5.enter_context(nc.psum_tensor(f"cps{i}", [128, 256], F32)) for i in range(2)]
            l_id = P.dma("pool", lambda e: e.dma_start(out=identb5[:], in_=ident_d[:, :]), "c6")
            P.wait("dve", [l_id, l_w])
            ub_dep = [l_u] * 8
            cdep = [None] * 8
            dg_free = [None, None]
            cps_free = [None, None]
            ccnt = 0
            for g in range(8):
                ds_ = g % 2
                dgd = None
                for tap in range(31):
                    dgd = P.op("dve", lambda e, g=g, tap=tap, ds_=ds_: e.tensor_scalar(
                        out=dg[ds_][:, tap, :], in0=identb5[:], scalar1=wdw_t[:, g, tap:tap + 1], scalar2=None, op0=ALU.mult),
                        waits=[dg_free[ds_]] if tap == 0 else [])
                lastmm = None
                ev = None
                for i in range(4):
                    cs_ = ccnt % 2
                    ccnt += 1
                    for tap in range(31):
                        lastmm = P.op("pe", lambda e, g=g, i=i, tap=tap, ds_=ds_, cs_=cs_: e.matmul(
                            cps[cs_][:, :], dg[ds_][:, tap, :], uB[:, g, i, 2 + tap:258 + tap], start=(tap == 0), stop=(tap == 30)),
                            waits=[dgd, ub_dep[g], cps_free[cs_]] if tap == 0 else [], sig=(tap == 30))
                    ev = P.op("act", lambda e, g=g, i=i, cs_=cs_: e.activation(
                        out=cTb[:, g, i, :], in_=cps[cs_][:, :], func=AF.Identity, bias=vec8_t[:, 0, g:g + 1], scale=1.0), waits=[lastmm, l_v8])
                    cps_free[cs_] = ev
                dg_free[ds_] = lastmm
                cdep[g] = ev
            sdep = None
            for b_ in range(4):
                d1 = P.op("dve", lambda e, b_=b_: e.tensor_tensor(out=stmp[:], in0=stT[:, b_, :, :], in1=wdw_t[:, :, 0:30], op=ALU.mult), waits=[sdep])
                d2 = P.op("dve", lambda e: e.tensor_reduce(out=s1t[:], in_=stmp[:], axis=AX.X, op=ALU.add), waits=[d1])
                d3 = P.op("dve", lambda e, b_=b_: e.tensor_tensor(out=cTs[:, :, b_], in0=uTs[:, :, b_], in1=wdw_t[:, :, 30], op=ALU.mult), waits=[d2])
                d4 = P.op("dve", lambda e, b_=b_: e.tensor_tensor(out=cTs[:, :, b_], in0=cTs[:, :, b_], in1=s1t[:], op=ALU.add), waits=[d3])
                sdep = P.op("dve", lambda e, b_=b_: e.tensor_tensor(out=cTs[:, :, b_], in0=cTs[:, :, b_], in1=vec8_t[:, 0, :], op=ALU.add), waits=[d4])
            groups = [(lambda g, i=i: cTb[:, g, i, :], 256, 256 * i) for i in range(4)] + [(lambda g: cTs[:, g, :], 4, 1024)]
            sq_free = [None, None]
            st_free = None
            nt_free = [None, None]
            cnt5 = 0
            cn_last = None
            for (src, n, c0) in groups:
                mm = None
                for g in range(8):
                    sl = cnt5 % 2
                    cnt5 += 1
                    a_sq = P.op("act", lambda e, g=g, sl=sl, src=src, n=n: e.activation(out=sqt[sl][:, 0:n], in_=src(g), func=AF.Square),
                                waits=[cdep[g], sdep, sq_free[sl]])
                    P.op("pe", lambda e, g=g, src=src, n=n: e.matmul(S1p[:, 0:n], onesf[:], src(g), start=(g == 0), stop=(g == 7)),
                         waits=[cdep[g], sdep, m1, st_free] if g == 0 else [cdep[g]], sig=False)
                    mm = P.op("pe", lambda e, g=g, sl=sl, n=n: e.matmul(S2p[:, 0:n], onesf[:], sqt[sl][:, 0:n], start=(g == 0), stop=(g == 7)),
                              waits=[a_sq])
                    sq_free[sl] = mm
                e1 = P.op("act", lambda e, n=n: e.activation(out=mean_t[:, 0:n], in_=S1p[:, 0:n], func=AF.Identity, scale=1.0 / 1024.0), waits=[mm, cn_last])
                e2 = P.op("dve", lambda e, n=n: e.tensor_tensor(out=m2_t[:, 0:n], in0=mean_t[:, 0:n], in1=mean_t[:, 0:n], op=ALU.mult), waits=[e1, cn_last])
                e3 = P.op("dve", lambda e, n=n: e.scalar_tensor_tensor(out=m2_t[:, 0:n], in0=S2p[:, 0:n], scalar=1.0 / 1024.0, in1=m2_t[:, 0:n],
                                                                      op0=ALU.mult, op1=ALU.subtract), waits=[e2, mm])
                st_free = e3
                e4 = P.op("act", lambda e, n=n: e.activation(out=rstd_t[:, 0:n], in_=m2_t[:, 0:n], func=AF.Sqrt, bias=epsT[:, 0:1], scale=1.0), waits=[e3, m2])
                e5 = P.op("dve", lambda e, n=n: e.reciprocal(out=rstd_t[:, 0:n], in_=rstd_t[:, 0:n]), waits=[e4])
                for g in range(8):
                    sl = cnt5 % 2
                    cnt5 += 1
                    f1 = P.op("dve", lambda e, g=g, sl=sl, src=src, n=n: e.tensor_tensor(out=ntmp[sl][:, 0:n], in0=src(g), in1=mean_t[:, 0:n], op=ALU.subtract),
                              waits=[e5, nt_free[sl]])
                    f2 = P.op("dve", lambda e, sl=sl, n=n: e.tensor_tensor(out=ntmp[sl][:, 0:n], in0=ntmp[sl][:, 0:n], in1=rstd_t[:, 0:n], op=ALU.mult), waits=[f1])
                    f3 = P.op("act", lambda e, g=g, sl=sl, n=n, c0=c0: e.activation(
                        out=cnT[:, g, c0:c0 + n], in_=ntmp[sl][:, 0:n], func=AF.Silu, bias=vec8_t[:, 2, g:g + 1], scale=vec8_t[:, 1, g:g + 1]), waits=[f2])
                    nt_free[sl] = f3
                    cn_last = f3
            pw_free = [None, None]
            cntp = 0
            pw_last = None
            for cb in range(8):
                for (c0, n) in CH:
                    bsl = cntp % 2
                    cntp += 1
                    mm = None
                    for k in range(8):
                        mm = P.op("pe", lambda e, k=k, cb=cb, c0=c0, n=n, bsl=bsl: e.matmul(
                            pwb[bsl][:, 0:n], wpw[:, k, 128 * cb:128 * cb + 128], cnT[:, k, c0:c0 + n], start=(k == 0), stop=(k == 7)),
                            waits=[l_pw, cn_last, pw_free[bsl]] if k == 0 else [], sig=(k == 7))
                    pw_last = P.op("dve", lambda e, cb=cb, c0=c0, n=n, bsl=bsl: e.scalar_tensor_tensor(
                        out=mixT[:, 8 + cb, c0:c0 + n], in0=pwb[bsl][:, 0:n], scalar=vec8_t[:, 3, cb:cb + 1], in1=gcT[:, cb, c0:c0 + n],
                        op0=ALU.add, op1=ALU.mult), waits=[mm])
                    pw_free[bsl] = pw_last
            dbg = P.dma("sp", lambda e: e.dma_start(out=mix_dbg[:, :, :], in_=mixT[:]), "dbg", waits=[pw_last])
            P.wait("sp", [dbg])
            P.wait("act", [cn_last])
            P.wait("dve", [pw_last])
            with (nc.named_scope(f'PH5') if scopes else ExitStack()):
                with nc.Block() as blk:
                    P.emit(blk)

        ALPHA = 2.0 ** 0.25
        mid.close()
        with ExitStack() as ps6:
            def sb6(name, shape, dtype):
                return ps6.enter_context(nc.sbuf_tensor(name, list(shape), dtype))
            rT = sb6("rT", [128, 16, NCc], F32)
            hbf = sb6("hbf", [128, 16, NCc], BF16)
            wo = [sb6(f"wo{i}", [128, 16, 512], BF16) for i in range(2)]
            wpe = sb6("wpe", [128, 2, D], BF16)
            pTb = sb6("pTb", [128, 2, NCc], BF16)
            vec16_t = sb6("vec16_t", [128, 4, 16], F32)
            xr = [sb6(f"xr{i}", [128, 344], F32) for i in range(2)]
            rtmp = [sb6(f"rtmp{i}", [128, 344], F32) for i in range(2)]
            sq6 = [sb6(f"sq6{i}", [128, 344], F32) for i in range(2)]
            onesf6 = sb6("onesf6", [128, 128], F32)
            eps6 = sb6("eps6", [128, 1], F32)
            mean6 = sb6("mean6", [128, NCc], F32)
            rstd6 = sb6("rstd6", [128, NCc], F32)
            sgm = [sb6(f"sgm{i}", [128, 344], F32) for i in range(2)]
            yst = [sb6(f"yst{i}", [128, 344], F32) for i in range(2)]
            S1 = [ps6.enter_context(nc.psum_tensor(f"S1_{i}", [128, 512], F32)) for i in range(3)]
            S2 = [ps6.enter_context(nc.psum_tensor(f"S2_{i}", [128, 512], F32)) for i in range(3)]
            mb = [ps6.enter_context(nc.psum_tensor(f"mb{i}", [128, 512], F32)) for i in range(2)]

            l_v16 = P.dma("sp", lambda e: e.dma_start(out=vec16_t[:], in_=vec16[:, :, :]), "c0")
            l_pe = P.dma("pool", lambda e: e.dma_start(out=wpe[:], in_=w_pe.rearrange("(k p) n -> p k n", p=128)), "c1")
            l_pt = P.dma("pool", lambda e: e.dma_start(out=pTb[:], in_=pT.rearrange("(k p) n -> p k n", p=128)), "c2")
            o1 = P.op("dve", lambda e: e.memset(onesf6[:], 1.0))
            o2 = P.op("dve", lambda e: e.memset(eps6[:], 1e-5))
            wo_free = [None, None]
            mb_free = [None, None]
            xr_free = [None, None]
            rt_free = [None, None]
            sq_free6 = [None, None]
            w_out_v = w_out.rearrange("(k p) n -> p k n", p=128)
            w_pg_v = w_pg.rearrange("(k p) n -> p k n", p=128)
            cnt6 = 0
            wt_i = 0
            stat_last = None
            for t in range(4):
                ws = wt_i % 2
                wt_i += 1
                d_w = P.dma("pool", lambda e, t=t, ws=ws: e.dma_start(out=wo[ws][:], in_=w_out_v[:, :, 512 * t:512 * t + 512]), f"wo{ws}", waits=[wo_free[ws]])
                lastpe = None
                for cbl in range(4):
                    cb = 4 * t + cbl
                    for ci, (c0, n) in enumerate(CH):
                        sl = cnt6 % 2
                        cnt6 += 1
                        d_x = P.dma("sp", lambda e, sl=sl, cb=cb, c0=c0, n=n: e.dma_start(out=xr[sl][:, 0:n], in_=xcT[128 * cb:128 * cb + 128, c0:c0 + n]),
                                    f"xr{sl}", waits=[xr_free[sl]])
                        mm = None
                        for k in range(16):
                            mm = P.op("pe", lambda e, k=k, ws=ws, cbl=cbl, c0=c0, n=n, sl=sl: e.matmul(
                                mb[sl][:, 0:n], wo[ws][:, k, 128 * cbl:128 * cbl + 128], mixT[:, k, c0:c0 + n], start=(k == 0), stop=(k == 15)),
                                waits=[d_w, mb_free[sl]] if k == 0 else [], sig=(k == 15))
                        lastpe = mm
                        r1 = P.op("dve", lambda e, sl=sl, n=n: e.scalar_tensor_tensor(
                            out=rtmp[sl][:, 0:n], in0=xr[sl][:, 0:n], scalar=ALPHA, in1=mb[sl][:, 0:n], op0=ALU.mult, op1=ALU.add),
                            waits=[mm, d_x, rt_free[sl]])
                        mb_free[sl] = r1
                        xr_free[sl] = r1
                        r2 = P.op("act", lambda e, sl=sl, cb=cb, c0=c0, n=n: e.activation(
                            out=rT[:, cb, c0:c0 + n], in_=rtmp[sl][:, 0:n], func=AF.Identity, bias=vec16_t[:, 0, cb:cb + 1], scale=1.0), waits=[r1, l_v16])
                        r3 = P.op("act", lambda e, sl=sl, cb=cb, n=n: e.activation(
                            out=sq6[sl][:, 0:n], in_=rtmp[sl][:, 0:n], func=AF.Square, bias=vec16_t[:, 0, cb:cb + 1], scale=1.0), waits=[sq_free6[sl]])
                        rt_free[sl] = r3
                        P.op("pe", lambda e, cb=cb, ci=ci, c0=c0, n=n: e.matmul(S1[ci][:, 0:n], onesf6[:], rT[:, cb, c0:c0 + n], start=(cb == 0), stop=(cb == 15)),
                             waits=[r2, o1], sig=False)
                        stat_last = P.op("pe", lambda e, cb=cb, ci=ci, n=n, sl=sl: e.matmul(S2[ci][:, 0:n], onesf6[:], sq6[sl][:, 0:n], start=(cb == 0), stop=(cb == 15)),
                                         waits=[r3])
                        sq_free6[sl] = stat_last
                wo_free[ws] = lastpe
            h_last = None
            for ci, (c0, n) in enumerate(CH):
                e1 = P.op("act", lambda e, ci=ci, c0=c0, n=n: e.activation(out=mean6[:, c0:c0 + n], in_=S1[ci][:, 0:n], func=AF.Identity, scale=1.0 / D), waits=[stat_last])
                e2 = P.op("dve", lambda e, c0=c0, n=n: e.tensor_tensor(out=rstd6[:, c0:c0 + n], in0=mean6[:, c0:c0 + n], in1=mean6[:, c0:c0 + n], op=ALU.mult), waits=[e1])
                e3 = P.op("dve", lambda e, ci=ci, c0=c0, n=n: e.scalar_tensor_tensor(out=rstd6[:, c0:c0 + n], in0=S2[ci][:, 0:n], scalar=1.0 / D, in1=rstd6[:, c0:c0 + n],
                                                                                    op0=ALU.mult, op1=ALU.subtract), waits=[e2, stat_last])
                e4 = P.op("act", lambda e, c0=c0, n=n: e.activation(out=rstd6[:, c0:c0 + n], in_=rstd6[:, c0:c0 + n], func=AF.Sqrt, bias=eps6[:, 0:1], scale=1.0), waits=[e3, o2])
                e5 = P.op("dve", lambda e, c0=c0, n=n: e.reciprocal(out=rstd6[:, c0:c0 + n], in_=rstd6[:, c0:c0 + n]), waits=[e4])
                for cb in range(16):
                    f1 = P.op("dve", lambda e, cb=cb, c0=c0, n=n: e.tensor_tensor(out=rT[:, cb, c0:c0 + n], in0=rT[:, cb, c0:c0 + n], in1=mean6[:, c0:c0 + n], op=ALU.subtract), waits=[e5])
                    f2 = P.op("dve", lambda e, cb=cb, c0=c0, n=n: e.tensor_tensor(out=rT[:, cb, c0:c0 + n], in0=rT[:, cb, c0:c0 + n], in1=rstd6[:, c0:c0 + n], op=ALU.mult), waits=[f1])
                    f3 = P.op("act", lambda e, cb=cb, c0=c0, n=n: e.activation(out=rT[:, cb, c0:c0 + n], in_=rT[:, cb, c0:c0 + n], func=AF.Identity,
                                                                               bias=vec16_t[:, 2, cb:cb + 1], scale=vec16_t[:, 1, cb:cb + 1]), waits=[f2])
                    h_last = P.op("act", lambda e, cb=cb, c0=c0, n=n: e.activation(out=hbf[:, cb, c0:c0 + n], in_=rT[:, cb, c0:c0 + n], func=AF.Identity), waits=[f3])
            sg_free = [None, None]
            ys_free = [None, None]
            for t in range(4):
                ws = wt_i % 2
                wt_i += 1
                d_w = P.dma("pool", lambda e, t=t, ws=ws: e.dma_start(out=wo[ws][:], in_=w_pg_v[:, :, 512 * t:512 * t + 512]), f"wo{ws}", waits=[wo_free[ws]])
                lastpe = None
                for cbl in range(4):
                    cb = 4 * t + cbl
                    for ci, (c0, n) in enumerate(CH):
                        mmA = None
                        for k in range(16):
                            mmA = P.op("pe", lambda e, k=k, ws=ws, cbl=cbl, c0=c0, n=n: e.matmul(
                                mb[0][:, 0:n], wo[ws][:, k, 128 * cbl:128 * cbl + 128], hbf[:, k, c0:c0 + n], start=(k == 0), stop=(k == 15)),
                                waits=[d_w, h_last, mb_free[0]] if k == 0 else [], sig=(k == 15))
                        mmB = None
                        for k in range(2):
                            mmB = P.op("pe", lambda e, k=k, cb=cb, c0=c0, n=n: e.matmul(
                                mb[1][:, 0:n], wpe[:, k, 128 * cb:128 * cb + 128], pTb[:, k, c0:c0 + n], start=(k == 0), stop=(k == 1)),
                                waits=[l_pe, l_pt, mb_free[1]] if k == 0 else [], sig=(k == 1))
                        lastpe = mmB
                        sl = cnt6 % 2
                        cnt6 += 1
                        g1 = P.op("act", lambda e, sl=sl, cb=cb, n=n: e.activation(out=sgm[sl][:, 0:n], in_=mb[0][:, 0:n], func=AF.Sigmoid,
                                                                                 bias=vec16_t[:, 3, cb:cb + 1], scale=1.0), waits=[mmA, sg_free[sl]])
                        mb_free[0] = g1
                        g2 = P.op("dve", lambda e, sl=sl, n=n: e.tensor_tensor(out=yst[sl][:, 0:n], in0=sgm[sl][:, 0:n], in1=mb[1][:, 0:n], op=ALU.mult),
                                  waits=[g1, mmB, ys_free[sl]])
                        mb_free[1] = g2
                        sg_free[sl] = g2
                        g3 = P.op("dve", lambda e, sl=sl, cb=cb, c0=c0, n=n: e.tensor_tensor(out=yst[sl][:, 0:n], in0=yst[sl][:, 0:n], in1=rT[:, cb, c0:c0 + n], op=ALU.add),
                                  waits=[g2])
                        dd = P.dma("sp", lambda e, sl=sl, cb=cb, c0=c0, n=n: e.dma_start(out=yT_out[128 * cb:128 * cb + 128, c0:c0 + n], in_=yst[sl][:, 0:n]),
                                   f"yst{sl}", waits=[g3])
                        ys_free[sl] = dd
                wo_free[ws] = lastpe
            P.wait("sp", [ys_free[0], ys_free[1]])
            P.wait("act", [h_last])
            with (nc.named_scope(f'PH6') if scopes else ExitStack()):
                with nc.Block() as blk:
                    P.emit(blk)
    return nc


_NC_CACHE = {}


def _prep_core(c, inp):
    b, j = c // 4, c % 4
    x = inp["x_prompt"][b]
    xs = inp["x_sample"][4 * c:4 * c + 4, 0, :]
    xoT = np.zeros((D, NT), np.float32)
    hmask = np.ones((128, 4), np.float32)
    for i in range(4):
        g = 4 * i + j
        t0 = 256 * g
        if g > 0:
            xoT[:, 288 * i:288 * i + 32] = x[t0 - 32:t0].T
        else:
            hmask[:, i] = 0.0
        xoT[:, 288 * i + 32:288 * i + 288] = x[t0:t0 + 256].T
    xoT[:, 1152:1156] = xs.T
    npad = 3 - j
    xwT = np.zeros((D, 4096), np.float32)
    for n_ in range(16):
        g = n_ - npad
        if g >= 0:
            xwT[:, 256 * n_:256 * n_ + 256] = x[256 * g:256 * g + 256].T
    candb = np.full((128, 4, 16), -1e30, np.float32)
    cand01 = np.zeros((128, 4, 16), np.float32)
    own01 = np.zeros((128, 4, 16), np.float32)
    for i in range(4):
        candb[:, i, npad:4 * i + 3] = 0.0
        cand01[:, i, npad:4 * i + 3] = 1.0
        own01[:, i, 4 * i + 3] = 1.0
    xcT = np.zeros((D, NCc), np.float32)
    pTm = np.zeros((256, NCc), np.float32)
    pp = inp["p_prompt"][0, b]
    for i in range(4):
        g = 4 * i + j
        xcT[:, 256 * i:256 * i + 256] = x[256 * g:256 * g + 256].T
        pTm[:, 256 * i:256 * i + 256] = pp[256 * g:256 * g + 256].T
    xcT[:, 1024:1028] = xs.T
    pTm[:, 1024:1028] = inp["p_sample"][0, 4 * c:4 * c + 4, 0, :].T
    st = inp["state_conv"][0, 4 * c:4 * c + 4]
    stT = np.ascontiguousarray(st.reshape(4, 30, 8, 128).transpose(3, 0, 2, 1))
    ptrep = np.ascontiguousarray(np.broadcast_to(inp["page_table"][4 * c:4 * c + 4].reshape(1, 256), (128, 256)).astype(np.int32))
    return {"xoT": xoT, "hmask": hmask, "xwT": xwT, "candb": candb, "cand01": cand01, "own01": own01,
            "xcT": xcT, "pT": pTm, "stT": stT, "ptrep": ptrep, "st_raw": np.ascontiguousarray(st)}


def kernel(**inp):
    inp = {k: np.asarray(v) for k, v in inp.items()}
    if "nc" not in _NC_CACHE:
        _NC_CACHE["nc"] = build()
    nc = _NC_CACHE["nc"]
    b_in = inp["b_in"][0]
    bcol = np.ascontiguousarray(b_in.reshape(56, 128).T)
    brep = np.ascontiguousarray(np.broadcast_to(np.concatenate([b_in[2048:3072], b_in[3072:4096]])[None, :], (128, 2048)))
    kk = np.arange(128)[:, None]
    ss = np.arange(256)[None, :]
    negmask = np.concatenate([np.where(kk <= ss, 0.0, -1e30), np.where(kk + 128 <= ss, 0.0, -1e30)], axis=1).astype(np.float32)
    shared = {"w_in": np.ascontiguousarray(inp["w_in"][0]), "bcol": bcol, "brep": brep,
              "negmask": negmask, "ident": np.eye(128, dtype=np.float32)}
    def pcol(v, n):
        return v.reshape(n, 128).T
    shared["wdw"] = np.ascontiguousarray(inp["w_dw"][0].reshape(31, 8, 128).transpose(2, 1, 0))
    shared["vec8"] = np.ascontiguousarray(np.stack([pcol(inp[k][0], 8) for k in ("b_dw", "g_cn", "b_cn", "b_pw")], axis=1))
    shared["vec16"] = np.ascontiguousarray(np.stack([pcol(inp[k][0], 16) for k in ("b_out", "g_ln", "b_ln", "b_pg")], axis=1))
    for k in ("w_pw", "w_out", "w_pg", "w_pe"):
        shared[k] = np.ascontiguousarray(inp[k][0])
    shared["ck"] = inp["cache_k"][0].reshape(2560 * 128, 1024)
    shared["cv"] = inp["cache_v"][0].reshape(2560 * 128, 1024)
    shared["iota"] = np.arange(128, dtype=np.float32).reshape(128, 1)
    dsel = np.zeros((128, 32, 16), np.float32)
    for h in range(8):
        dsel[h, :, h] = 1.0
        dsel[h, :, 8 + h] = 1.0
    shared["dsel"] = dsel
    in_maps = []
    for c in range(8):
        m = dict(shared)
        m.update(_prep_core(c, inp))
        in_maps.append(m)
    res = run_bass_kernel_spmd(nc, in_maps, core_ids=list(range(8)))
    R = res.results

    y_prompt = np.zeros((2, 4096, 2048), np.float32)
    y_sample = np.zeros((32, 1, 2048), np.float32)
    k_p = np.zeros((1, 2, 4096, 8, 128), np.float32)
    v_p = np.zeros((1, 2, 4096, 8, 128), np.float32)
    c_p = np.zeros((1, 2, 30, 1024), np.float32)
    k_s = np.zeros((1, 32, 1, 8, 128), np.float32)
    v_s = np.zeros((1, 32, 1, 8, 128), np.float32)
    c_s = np.zeros((1, 32, 30, 1024), np.float32)
    for c in range(8):
        b, j = c // 4, c % 4
        r = R[c]
        kT = r["kT_out"]
        vv = r["v_out"]
        for i in range(4):
            g = 4 * i + j
            t0 = 256 * g
            k_p[0, b, t0:t0 + 256] = kT[:, :, 288 * i + 32:288 * i + 288].transpose(2, 0, 1)
            v_p[0, b, t0:t0 + 256] = vv[256 * i:256 * i + 256].reshape(256, 8, 128)
        k_s[0, 4 * c:4 * c + 4, 0] = kT[:, :, 1152:1156].transpose(2, 0, 1)
        v_s[0, 4 * c:4 * c + 4, 0] = r["vsT_out"].transpose(2, 1, 0)
        yT = r["yT_out"]
        for i in range(4):
            g = 4 * i + j
            y_prompt[b, 256 * g:256 * g + 256] = yT[:, 256 * i:256 * i + 256].T
        y_sample[4 * c:4 * c + 4, 0] = yT[:, 1024:1028].T
        u = r["uT_out"]
        if j == 3:
            c_p[0, b] = u[:, :, 0:30].transpose(2, 1, 0).reshape(30, 1024)
        c_s[0, 4 * c:4 * c + 4, 29] = u[:, :, 30:34].transpose(2, 1, 0).reshape(4, 1024)
    for c in range(8):
        c_s[0, 4 * c:4 * c + 4, 0:29] = R[c]["cs_out"]
    return (y_prompt, y_sample, k_p, v_p, c_p, k_s, v_s, c_s)
```

```python
import numpy as np
import concourse.bass as bass
import concourse.mybir as mybir
from concourse.bass_utils import run_bass_kernel_spmd

F32 = mybir.dt.float32
BF16 = mybir.dt.bfloat16
I32 = mybir.dt.int32
AF = mybir.ActivationFunctionType
ALU = mybir.AluOpType
AX = mybir.AxisListType

D = 2048
NT = 1156
NCc = 1028
ENG = ("pe", "act", "dve", "pool", "sp")


class Prog:
    def __init__(self, nc, es):
        self.nc = nc
        self.es = es
        self.sems = {}
        self.reset()

    def reset(self):
        self.ops = {k: [] for k in ENG}

    def _sem(self, name):
        if name not in self.sems:
            h = self.es.enter_context(self.nc.semaphore(name))
            self.sems[name] = [h, 0]
        return self.sems[name]

    def op(self, eng, fn, waits=(), sig=True):
        dep = None
        s = None
        if sig:
            s = self._sem("m_" + eng)
            s[1] += 1
            dep = (s[0], s[1])
        self.ops[eng].append((tuple(w for w in waits if w is not None), fn, s[0] if sig else None, 1))
        return dep

    def dma(self, eng, fn, semname, waits=()):
        s = self._sem("d_" + semname)
        s[1] += 16
        self.ops[eng].append((tuple(w for w in waits if w is not None), fn, s[0], 16))
        return (s[0], s[1])

    def wait(self, eng, waits):
        self.ops[eng].append((tuple(w for w in waits if w is not None), None, None, 0))

    def emit(self, blk):
        ops = self.ops

        def run(e, lst):
            seen = {}
            for waits, fn, sem, inc in lst:
                for (h, v) in waits:
                    k = id(h)
                    if seen.get(k, -1) >= v:
                        continue
                    seen[k] = v
                    e.wait_ge(h, v)
                if fn is not None:
                    ins = fn(e)
                    if sem is not None:
                        ins.then_inc(sem, inc)

        @blk.tensor
        def _(e):
            run(e, ops["pe"])

        @blk.scalar
        def _(e):
            run(e, ops["act"])

        @blk.vector
        def _(e):
            run(e, ops["dve"])

        @blk.gpsimd
        def _(e):
            run(e, ops["pool"])

        @blk.sync
        def _(e):
            run(e, ops["sp"])

        self.reset()


def emit_p4(nc, P, st4, ck, cv, ptrep_d, iota_d, dsel_d, ident_d, qs, ksT, vsT, gasT, write_out, dbgfn=None, holder=None, shared_ps=None):
    def sb(name, shape, dtype):
        return st4.enter_context(nc.sbuf_tensor(name, list(shape), dtype))
    def ps(name, shape):
        return st4.enter_context(nc.psum_tensor(name, list(shape), F32))
    SC = 128.0 ** -0.5
    ptab = sb("ptab", [128, 256], I32)
    iota = sb("iota4", [128, 1], F32)
    idx = sb("idx4", [128, 256], I32)
    dsel = sb("dsel_sb", [128, 32, 16], F32)
    identf = sb("identf4", [128, 128], F32)
    onesf = sb("onesf4", [128, 128], F32)
    onesb = sb("onesb4", [128, 2], BF16)
    qcol = [sb(f"qcol{i}", [128, 128], F32) for i in range(2)]
    qrep = sb("qrep", [128, 8, 128], F32)
    vrep = sb("vrep", [128, 8, 128], F32)
    garep = sb("garep", [128, 8, 128], F32)
    kpg = [sb(f"kpg{i}", [128, 1024], F32) for i in range(3)]
    prod = [sb(f"prod{i}", [128, 8, 128], F32) for i in range(2)]
    sc = sb("sc4", [128, 32, 16], F32)
    psc = sb("psc4", [128, 32, 16], BF16)
    vpg = [sb(f"vpg{i}", [128, 1024], BF16) for i in range(4)]
    gsel = sb("gsel", [128, 32, 16], F32)
    gT = sb("gT4", [128, 32], F32)
    t8 = sb("t84", [128, 8], F32)
    wsl = sb("wsl4", [128, 32], F32)
    accs = sb("accs4", [128, 8, 128], F32)
    accden = sb("accden4", [128, 2], F32)
    qk = sb("qk4", [128, 8], F32)
    pnew = sb("pnew4", [128, 1], F32)
    rdn = sb("rdn4", [128, 1], F32)
    osT = sb("osT4", [128, 8], F32)
    misc4 = ps("misc4", [128, 512])
    rp_ps = [misc4[:, 0:128], misc4[:, 0:128]]
    ovA = ps("ovA", [128, 512])
    gs_ps = ovA[:, :].rearrange("p (n c) -> p n c", c=16)
    ovB = ps("ovB", [128, 512])
    den_ps = misc4[:, 256:258]
    sn_ps = misc4[:, 260:261]
    tr_ps = misc4[:, 264:272]

    ckv = ck[:, :]
    cvv = cv[:, :]
    l0 = P.dma("sp", lambda e: e.dma_start(out=ptab[:], in_=ptrep_d[:, :]), "c0")
    l1 = P.dma("sp", lambda e: e.dma_start(out=iota[:], in_=iota_d[:, :]), "c1")
    l2 = P.dma("sp", lambda e: e.dma_start(out=dsel[:], in_=dsel_d[:, :, :]), "c2")
    l3 = P.dma("sp", lambda e: e.dma_start(out=identf[:], in_=ident_d[:, :]), "c3")
    m0 = P.op("dve", lambda e: e.memset(onesf[:], 1.0))
    m1 = P.op("dve", lambda e: e.memset(onesb[:], 1.0))
    ix = P.op("dve", lambda e: e.tensor_scalar(out=idx[:], in0=ptab[:], scalar1=128.0, scalar2=iota[:, 0:1], op0=ALU.mult, op1=ALU.add),
              waits=[l0, l1])
    P.wait("pool", [ix])
    P.wait("pe", [l3, m0, m1])
    P.wait("dve", [l2])
    yield
    kfree = [None] * 3
    vfree = [None] * 4
    pfree = [None] * 2
    qc_free = [None] * 2
    rp_free = [None] * 2
    kc = vc = pc = rc = 0
    prev_sample = None
    accd_prev = [None]
    for b_ in range(4):
        rep_last = None
        for (srcT, dst) in ((qs, qrep), (vsT, vrep), (gasT, garep)):
            for h in range(8):
                s_ = rc % 2
                rc += 1
                a = P.op("dve", lambda e, s_=s_, srcT=srcT, h=h, b_=b_: e.tensor_scalar(
                    out=qcol[s_][:], in0=onesf[:], scalar1=srcT[:, h, b_:b_ + 1], scalar2=None, op0=ALU.mult), waits=[qc_free[s_], prev_sample])
                mm = P.op("pe", lambda e, s_=s_: e.matmul(rp_ps[s_][:, :], qcol[s_][:], identf[:], start=True, stop=True), waits=[a, rp_free[0], rp_free[1]])
                qc_free[s_] = mm
                cp = P.op("act", lambda e, s_=s_, dst=dst, h=h: e.activation(out=dst[:, h, :], in_=rp_ps[s_][:, :], func=AF.Identity), waits=[mm, prev_sample])
                rp_free[s_] = cp
                rep_last = cp
            yield
        red = None
        for pg in range(64):
            ks_ = kc % 3
            kc += 1
            col = 64 * b_ + pg
            dk = P.dma("pool", lambda e, ks_=ks_, col=col: e.indirect_dma_start(
                out=kpg[ks_][:], out_offset=None, in_=ckv, in_offset=bass.IndirectOffsetOnAxis(ap=idx[:, col:col + 1], axis=0)),
                f"kpg{ks_}", waits=[kfree[ks_]])
            pr = pc % 2
            pc += 1
            mu = P.op("dve", lambda e, ks_=ks_, pr=pr: e.tensor_tensor(out=prod[pr][:], in0=kpg[ks_][:].rearrange("p (h d) -> p h d", h=8), in1=qrep[:], op=ALU.mult),
                      waits=[dk, rep_last, pfree[pr]])
            kfree[ks_] = mu
            red = P.op("dve", lambda e, pr=pr, pg=pg: e.tensor_reduce(out=sc[:, pg // 2, 8 * (pg % 2):8 * (pg % 2) + 8], in_=prod[pr][:], axis=AX.X, op=ALU.add),
                       waits=[mu, prev_sample])
            pfree[pr] = red
            yield
        g1 = P.op("pe", lambda e: e.matmul(gs_ps, onesf[:], sc[:], start=True, stop=True), waits=[red, prev_sample, accd_prev[0]])
        g2 = P.op("dve", lambda e: e.tensor_tensor(out=gsel[0:8], in0=gs_ps[0:8], in1=dsel[0:8], op=ALU.mult), waits=[g1])
        g3 = P.op("dve", lambda e: e.tensor_reduce(out=gT[0:8, :], in_=gsel[0:8], axis=AX.X, op=ALU.add), waits=[g2])
        g4 = P.op("dve", lambda e: e.max(out=t8[0:8, :], in_=gT[0:8, :]), waits=[g3])
        g5 = P.op("dve", lambda e: e.tensor_scalar(out=wsl[0:8, :], in0=gT[0:8, :], scalar1=t8[0:8, 2:3], scalar2=None, op0=ALU.is_ge), waits=[g4])
        ex = P.op("act", lambda e: e.activation(out=psc[:], in_=sc[:], func=AF.Exp, scale=SC), waits=[red, prev_sample])
        accd = None
        for n_ in range(32):
            dvs = []
            sl = []
            for e_ in range(2):
                vs_ = vc % 4
                vc += 1
                col = 64 * b_ + 2 * n_ + e_
                dvs.append(P.dma("pool", lambda e, vs_=vs_, col=col: e.indirect_dma_start(
                    out=vpg[vs_][:], out_offset=None, in_=cvv, in_offset=bass.IndirectOffsetOnAxis(ap=idx[:, col:col + 1], axis=0)),
                    f"vpg{vs_}", waits=[vfree[vs_]]))
                sl.append(vs_)
            last = None
            for (dst, lo) in ((ovA, 0), (ovB, 4)):
                for e_ in range(2):
                    last = P.op("pe", lambda e, dst=dst, lo=lo, e_=e_, n_=n_, v=sl[e_]: e.matmul(
                        dst[0:8, :], psc[:, n_, 8 * e_:8 * e_ + 8], vpg[v][:, 128 * lo:128 * lo + 512], start=(e_ == 0), stop=(e_ == 1)),
                        waits=[ex, dvs[0], dvs[1], accd, g2] if (lo == 0 and e_ == 0) else [], sig=False)
            for e_ in range(2):
                last = P.op("pe", lambda e, e_=e_, n_=n_: e.matmul(den_ps[0:8, :], psc[:, n_, 8 * e_:8 * e_ + 8], onesb[:], start=(e_ == 0), stop=(e_ == 1)),
                            sig=(e_ == 1))
            vfree[sl[0]] = last
            vfree[sl[1]] = last
            if n_ == 0:
                a1 = P.op("dve", lambda e: e.tensor_scalar(out=accs[0:8, 0:4, :], in0=ovA[0:8, :].rearrange("p (h d) -> p h d", h=4), scalar1=wsl[0:8, 0:1], scalar2=None, op0=ALU.mult), waits=[last, g5, prev_sample])
                a2 = P.op("dve", lambda e: e.tensor_scalar(out=accs[0:8, 4:8, :], in0=ovB[0:8, :].rearrange("p (h d) -> p h d", h=4), scalar1=wsl[0:8, 0:1], scalar2=None, op0=ALU.mult), waits=[a1])
                accd = P.op("dve", lambda e: e.tensor_scalar(out=accden[0:8, :], in0=den_ps[0:8, :], scalar1=wsl[0:8, 0:1], scalar2=None, op0=ALU.mult), waits=[a2])
            else:
                a1 = P.op("dve", lambda e, n_=n_: e.scalar_tensor_tensor(out=accs[0:8, 0:4, :], in0=ovA[0:8, :].rearrange("p (h d) -> p h d", h=4), scalar=wsl[0:8, n_:n_ + 1], in1=accs[0:8, 0:4, :],
                                                                        op0=ALU.mult, op1=ALU.add), waits=[last, accd])
                a2 = P.op("dve", lambda e, n_=n_: e.scalar_tensor_tensor(out=accs[0:8, 4:8, :], in0=ovB[0:8, :].rearrange("p (h d) -> p h d", h=4), scalar=wsl[0:8, n_:n_ + 1], in1=accs[0:8, 4:8, :],
                                                                        op0=ALU.mult, op1=ALU.add), waits=[a1])
                accd = P.op("dve", lambda e, n_=n_: e.scalar_tensor_tensor(out=accden[0:8, :], in0=den_ps[0:8, :], scalar=wsl[0:8, n_:n_ + 1], in1=accden[0:8, :],
                                                                          op0=ALU.mult, op1=ALU.add), waits=[a2])
            yield
        accd_prev[0] = accd
        n1 = P.op("dve", lambda e, b_=b_: e.tensor_tensor(out=qk[:], in0=qs[:, :, b_], in1=ksT[:, :, b_], op=ALU.mult), waits=[prev_sample])
        n2 = P.op("pe", lambda e: e.matmul(sn_ps[0:8, :], qk[:], onesf[:, 0:1], start=True, stop=True), waits=[n1])
        n3 = P.op("act", lambda e: e.activation(out=pnew[0:8, :], in_=sn_ps[0:8, :], func=AF.Exp, scale=SC), waits=[n2])
        n4 = P.op("dve", lambda e: e.scalar_tensor_tensor(out=accs[0:8], in0=vrep[0:8], scalar=pnew[0:8, 0:1],
                                                         in1=accs[0:8], op0=ALU.mult, op1=ALU.add), waits=[n3, accd, rep_last])
        n5 = P.op("dve", lambda e: e.tensor_tensor(out=accden[0:8, 0:1], in0=accden[0:8, 0:1], in1=pnew[0:8, 0:1], op=ALU.add), waits=[n4])
        n6 = P.op("dve", lambda e: e.reciprocal(out=rdn[0:8, :], in_=accden[0:8, 0:1]), waits=[n5])
        n7 = P.op("dve", lambda e: e.scalar_tensor_tensor(out=accs[0:8], in0=accs[0:8], scalar=rdn[0:8, 0:1], in1=garep[0:8],
                                                         op0=ALU.mult, op1=ALU.mult), waits=[n6])
        if dbgfn is not None and b_ == 0:
            dd_ = dbgfn(dict(idx=idx, qrep=qrep, vrep=vrep, garep=garep, sc=sc, gT=gT, t8=t8, wsl=wsl, accs=accs, accden=accden, pnew=pnew, rdn=rdn, kpg0=kpg[0]), n7)
            for en_ in ("dve", "act", "pool", "pe"):
                P.wait(en_, dd_)
        cpl = None
        for h in range(8):
            tp = P.op("pe", lambda e, h=h: e.transpose(tr_ps[:, :], accs[0:8, h, :], identf[0:8, 0:8]), waits=[n7, cpl])
            cpl = P.op("act", lambda e, h=h: e.activation(out=osT[:, h:h + 1], in_=tr_ps[:, h:h + 1], func=AF.Identity), waits=[tp, prev_sample])
        prev_sample = write_out(b_, osT, cpl)
        if holder is not None:
            holder["last"] = prev_sample
        yield


def build(skip4=False, scopes=False):
    from contextlib import ExitStack
    nc = bass.Bass("TRN2", target_bir_lowering=False)
    dt = nc.dram_tensor

    def din(name, shape, dtype=F32):
        return dt(name, list(shape), dtype, kind="ExternalInput").ap()

    def dout(name, shape, dtype=F32):
        return dt(name, list(shape), dtype, kind="ExternalOutput").ap()

    xoT = din("xoT", [D, NT])
    w_in = din("w_in", [D, 7168])
    bcol = din("bcol", [128, 56])
    brep = din("brep", [128, 2048])
    hmask = din("hmask", [128, 4])
    xwT = din("xwT", [D, 4096])
    candb = din("candb", [128, 4, 16])
    cand01 = din("cand01", [128, 4, 16])
    own01 = din("own01", [128, 4, 16])
    negmask_d = din("negmask", [128, 512])
    ident_d = din("ident", [128, 128])
    kT_scr = dt("kT_scr", [8, 128, 4096], BF16, kind="Internal").ap()
    v_scr = dt("v_scr", [4096, 1024], BF16, kind="Internal").ap()
    u_scr = dt("u_scr", [128, 8, NT], F32, kind="Internal").ap()
    wdw_d = din("wdw", [128, 8, 31])
    vec8 = din("vec8", [128, 4, 8])
    vec16 = din("vec16", [128, 4, 16])
    stT_d = din("stT", [128, 4, 8, 30])
    w_pw = din("w_pw", [1024, 1024])
    w_out = din("w_out", [D, D])
    w_pg = din("w_pg", [D, D])
    w_pe = din("w_pe", [256, D])
    xcT = din("xcT", [D, NCc])
    pT = din("pT", [256, NCc])
    yT_out = dout("yT_out", [D, NCc])
    if not skip4:
        ck_d = din("ck", [2560 * 128, 1024])
        cv_d = din("cv", [2560 * 128, 1024])
    ptrep_d = din("ptrep", [128, 256], I32)
    iota_d = din("iota", [128, 1])
    dsel_d = din("dsel", [128, 32, 16])
    st_raw = din("st_raw", [4, 30, 1024])
    cs_out = dout("cs_out", [4, 29, 1024])

    kT_out = dout("kT_out", [8, 128, NT])
    v_out = dout("v_out", [1024, 1024])
    vsT_out = dout("vsT_out", [128, 8, 4])
    uT_out = dout("uT_out", [128, 8, 34])

    es = ExitStack()
    with es:
        P = Prog(nc, es)

        def sb(name, shape, dtype):
            return es.enter_context(nc.sbuf_tensor(name, list(shape), dtype))

        mixT = sb("mixT", [128, 16, NCc], BF16)
        mid = ExitStack()

        def sbm(name, shape, dtype):
            return mid.enter_context(nc.sbuf_tensor(name, list(shape), dtype))
        qs = sb("qs", [128, 8, 4], F32)
        gasT = sb("gasT", [128, 8, 4], F32)
        vsT = sb("vsT", [128, 8, 4], F32)
        ksT = sb("ksT", [128, 8, 4], F32)
        bcol_t = sb("bcol_t", [128, 56], F32)
        brep_t = sb("brep_t", [128, 2048], F32)
        hmask_t = sb("hmask_t", [128, 4], F32)
        qT = sbm("qT", [128, 8, NCc], BF16)
        ga = sbm("ga", [128, 8, 1024], BF16)
        gcT = sbm("gcT", [128, 8, NCc], BF16)

        with ExitStack() as ps1:
            def sb1(name, shape, dtype):
                return ps1.enter_context(nc.sbuf_tensor(name, list(shape), dtype))
            xo = sb1("xo", [128, 16, NT], BF16)
            uT = sb1("uT", [128, 8, NT], F32)
            wt = [sb1(f"wt{i}", [128, 16, 512], BF16) for i in range(2)]
            kst = [sb1(f"kst{i}", [128, 292], F32) for i in range(2)]
            vst = [sb1(f"vst{i}", [128, 512], F32) for i in range(2)]
            sg = [sb1(f"sg{i}", [128, 292], F32) for i in range(2)]
            gtmp = [sb1(f"gtmp{i}", [128, 512], F32) for i in range(2)]
            pbank = [ps1.enter_context(nc.psum_tensor(f"p1b{i}", [128, 512], F32)) for i in range(4)]

            w_in_v = w_in.rearrange("(k p) n -> p k n", p=128)
            d_xo = P.dma("pool", lambda e: e.dma_start(out=xo[:], in_=xoT.rearrange("(k p) n -> p k n", p=128)), "xo")
            d_bc = P.dma("sp", lambda e: e.dma_start(out=bcol_t[:], in_=bcol[:, :]), "c0")
            d_br = P.dma("sp", lambda e: e.dma_start(out=brep_t[:], in_=brep[:, :]), "c1")
            d_hm = P.dma("sp", lambda e: e.dma_start(out=hmask_t[:], in_=hmask[:, :]), "c2")

            wt_free = [None] * 2
            bank_free = [None] * 4
            kst_free = [None] * 2
            vst_free = [None] * 2
            sg_free = [None] * 2
            gtmp_free = [None] * 2
            grp = [0]
            cnt = {"k": 0, "v": 0, "sg": 0, "g": 0}
            out_deps = []
            u_last = [None] * 8

            def mm_group(lhs_fn, rhs_fn, n_out, w_dep):
                b = grp[0] % 4
                grp[0] += 1
                dep = None
                for k in range(16):
                    waits = []
                    if k == 0:
                        waits = [w_dep, d_xo, bank_free[b]]
                    last = (k == 15)
                    dep = P.op("pe", (lambda e, k=k, b=b: e.matmul(pbank[b][:, 0:n_out], lhs_fn(k), rhs_fn(k), start=(k == 0), stop=(k == 15))),
                               waits=waits, sig=last)
                return b, dep

            for t in range(14):
                s = t % 2
                d_w = P.dma("pool", (lambda e, t=t, s=s: e.dma_start(out=wt[s][:], in_=w_in_v[:, :, t * 512:(t + 1) * 512])),
                            f"wt{s}", waits=[wt_free[s]])
                role = ["q", "k", "v", "ga", "a", "bg", "gc"][t // 2]
                last_pe = None
                if role in ("q", "k", "a", "bg", "gc"):
                    for cbl in range(4):
                        cb = 4 * t + cbl
                        hh = cb % 8
                        for i in range(4):
                            n = 292 if i == 3 else 288
                            c0 = 288 * i
                            b, dep = mm_group(lambda k, s=s, cbl=cbl: wt[s][:, k, cbl * 128:(cbl + 1) * 128],
                                              lambda k, c0=c0, n=n: xo[:, k, c0:c0 + n], n, d_w)
                            last_pe = dep
                            bias = bcol_t[:, cb:cb + 1]
                            if role == "q":
                                ev = P.op("act", lambda e, b=b, hh=hh, i=i, bias=bias: e.activation(
                                    out=qT[:, hh, 256 * i:256 * i + 256], in_=pbank[b][:, 32:288], func=AF.Identity, bias=bias, scale=1.0),
                                    waits=[dep, d_bc])
                                if i == 3:
                                    P.op("act", lambda e, b=b, hh=hh, bias=bias: e.activation(
                                        out=qT[:, hh, 1024:1028], in_=pbank[b][:, 288:292], func=AF.Identity, bias=bias, scale=1.0))
                                    ev = P.op("act", lambda e, b=b, hh=hh, bias=bias: e.activation(
                                        out=qs[:, hh, :], in_=pbank[b][:, 288:292], func=AF.Identity, bias=bias, scale=1.0))
                                bank_free[b] = ev
                            elif role == "k":
                                ks = cnt["k"] % 2
                                cnt["k"] += 1
                                ev = P.op("act", lambda e, b=b, ks=ks, n=n, bias=bias: e.activation(
                                    out=kst[ks][:, 0:n], in_=pbank[b][:, 0:n], func=AF.Identity, bias=bias, scale=1.0),
                                    waits=[dep, d_bc, kst_free[ks]])
                                if i == 3:
                                    ev2 = P.op("act", lambda e, b=b, hh=hh, bias=bias: e.activation(
                                        out=ksT[:, hh, :], in_=pbank[b][:, 288:292], func=AF.Identity, bias=bias, scale=1.0))
                                    bank_free[b] = ev2
                                else:
                                    bank_free[b] = ev
                                dd = P.dma("sp", lambda e, ks=ks, hh=hh, c0=c0, n=n: e.dma_start(
                                    out=kT_out[hh, :, c0:c0 + n], in_=kst[ks][:, 0:n]), f"kst{ks}", waits=[ev])
                                kst_free[ks] = dd
                            elif role == "a":
                                ev = P.op("act", lambda e, b=b, hh=hh, c0=c0, n=n, bias=bias: e.activation(
                                    out=uT[:, hh, c0:c0 + n], in_=pbank[b][:, 0:n], func=AF.Identity, bias=bias, scale=1.0),
                                    waits=[dep, d_bc])
                                bank_free[b] = ev
                            elif role == "bg":
                                ss = cnt["sg"] % 2
                                cnt["sg"] += 1
                                ev = P.op("act", lambda e, b=b, ss=ss, n=n, bias=bias: e.activation(
                                    out=sg[ss][:, 0:n], in_=pbank[b][:, 0:n], func=AF.Sigmoid, bias=bias, scale=1.0),
                                    waits=[dep, d_bc, sg_free[ss]])
                                bank_free[b] = ev
                                mu = P.op("dve", lambda e, ss=ss, hh=hh, c0=c0, n=n: e.tensor_tensor(
                                    out=uT[:, hh, c0:c0 + n], in0=uT[:, hh, c0:c0 + n], in1=sg[ss][:, 0:n], op=ALU.mult),
                                    waits=[ev])
                                sg_free[ss] = mu
                                u_last[hh] = mu
                            elif role == "gc":
                                ev = P.op("act", lambda e, b=b, hh=hh, i=i, bias=bias: e.activation(
                                    out=gcT[:, hh, 256 * i:256 * i + 256], in_=pbank[b][:, 32:288], func=AF.Silu, bias=bias, scale=1.0),
                                    waits=[dep, d_bc])
                                if i == 3:
                                    ev = P.op("act", lambda e, b=b, hh=hh, bias=bias: e.activation(
                                        out=gcT[:, hh, 1024:1028], in_=pbank[b][:, 288:292], func=AF.Silu, bias=bias, scale=1.0))
                                bank_free[b] = ev
                else:
                    half = t % 2
                    boff = (0 if role == "v" else 1024) + half * 512
                    for tt in range(8):
                        i, hf = tt // 2, tt % 2
                        c0 = 288 * i + 32 + 128 * hf
                        b, dep = mm_group(lambda k, c0=c0: xo[:, k, c0:c0 + 128],
                                          lambda k, s=s: wt[s][:, k, :], 512, d_w)
                        last_pe = dep
                        if role == "v":
                            vs = cnt["v"] % 2
                            cnt["v"] += 1
                            ev = P.op("dve", lambda e, b=b, vs=vs, boff=boff: e.tensor_tensor(
                                out=vst[vs][:], in0=pbank[b][:, :], in1=brep_t[:, boff:boff + 512], op=ALU.add),
                                waits=[dep, d_br, vst_free[vs]])
                            bank_free[b] = ev
                            dd = P.dma("sp", lambda e, vs=vs, tt=tt, half=half: e.dma_start(
                                out=v_out[tt * 128:(tt + 1) * 128, half * 512:(half + 1) * 512], in_=vst[vs][:]),
                                f"vst{vs}", waits=[ev])
                            vst_free[vs] = dd
                        else:
                            gs = cnt["g"] % 2
                            cnt["g"] += 1
                            ev = P.op("dve", lambda e, b=b, gs=gs, boff=boff: e.tensor_tensor(
                                out=gtmp[gs][:], in0=pbank[b][:, :], in1=brep_t[:, boff:boff + 512], op=ALU.add),
                                waits=[dep, d_br, gtmp_free[gs]])
                            bank_free[b] = ev
                            a2 = P.op("act", lambda e, gs=gs, tt=tt, half=half: e.activation(
                                out=ga[:, tt, half * 512:(half + 1) * 512], in_=gtmp[gs][:], func=AF.Silu),
                                waits=[ev])
                            gtmp_free[gs] = a2
                    for cbl in range(4):
                        cb = 4 * t + cbl
                        hh = cb % 8
                        b, dep = mm_group(lambda k, s=s, cbl=cbl: wt[s][:, k, cbl * 128:(cbl + 1) * 128],
                                          lambda k: xo[:, k, 1152:1156], 4, d_w)
                        last_pe = dep
                        bias = bcol_t[:, cb:cb + 1]
                        if role == "v":
                            ev = P.op("act", lambda e, b=b, hh=hh, bias=bias: e.activation(
                                out=vsT[:, hh, :], in_=pbank[b][:, 0:4], func=AF.Identity, bias=bias, scale=1.0),
                                waits=[dep, d_bc])
                        else:
                            ev = P.op("act", lambda e, b=b, hh=hh, bias=bias: e.activation(
                                out=gasT[:, hh, :], in_=pbank[b][:, 0:4], func=AF.Silu, bias=bias, scale=1.0),
                                waits=[dep, d_bc])
                        bank_free[b] = ev
                wt_free[s] = last_pe

            hm = None
            for i in range(4):
                hm = P.op("dve", lambda e, i=i: e.tensor_scalar(
                    out=uT[:, :, 288 * i:288 * i + 32], in0=uT[:, :, 288 * i:288 * i + 32],
                    scalar1=hmask_t[:, i:i + 1], scalar2=None, op0=ALU.mult),
                    waits=[d_hm] + [u for u in u_last])
            fin = []
            fin.append(P.dma("sp", lambda e: e.dma_start(out=uT_out[:, :, 0:30], in_=uT[:, :, 1122:1152]), "fo0", waits=[hm]))
            fin.append(P.dma("sp", lambda e: e.dma_start(out=uT_out[:, :, 30:34], in_=uT[:, :, 1152:1156]), "fo1", waits=[hm]))
            fin.append(P.dma("sp", lambda e: e.dma_start(out=u_scr[:, :, :], in_=uT[:]), "fo3", waits=[hm]))
            fin.append(P.dma("sp", lambda e: e.dma_start(out=cs_out[:, :, :], in_=st_raw[:, 1:30, :]), "fo4"))
            lastact = P.op("act", lambda e: e.activation(out=sg[0][:, 0:4], in_=vsT[:, 0, :], func=AF.Identity), waits=[sg_free[0], sg_free[1]])
            fin.append(P.dma("sp", lambda e: e.dma_start(out=vsT_out[:, :, :], in_=vsT[:]), "fo2", waits=[lastact]))
            P.wait("sp", fin + [kst_free[0], kst_free[1], vst_free[0], vst_free[1]])
            P.wait("act", [gtmp_free[0], gtmp_free[1], lastact])
            P.wait("dve", [hm])
            with (nc.named_scope(f'PH1') if scopes else ExitStack()):
                with nc.Block() as blk:
                    P.emit(blk)

        kmT = sbm("kmT", [128, 8, 16], BF16)
        ksum = sbm("ksum", [128, 8, 16], F32)
        with ExitStack() as ps2:
            def sb2(name, shape, dtype):
                return ps2.enter_context(nc.sbuf_tensor(name, list(shape), dtype))
            wk = sb2("wk", [128, 16, 1024], BF16)
            wv = sb2("wv", [128, 16, 1024], BF16)
            xw = [sb2(f"xw{i}", [128, 16, 512], BF16) for i in range(2)]
            kstg = [sb2(f"kstg{i}", [128, 8, 512], BF16) for i in range(1)]
            vstg = [sb2(f"vstg{i}", [128, 4, 1024], BF16) for i in range(1)]
            pb2 = [ps2.enter_context(nc.psum_tensor(f"p2b{i}", [128, 512], F32)) for i in range(4)]
            xwT_v = xwT.rearrange("(k p) n -> p k n", p=128)
            d_wk = P.dma("pool", lambda e: e.dma_start(out=wk[:], in_=w_in_v[:, :, 1024:2048]), "wk")
            d_x = [None] * 8
            d_x[0] = P.dma("pool", lambda e: e.dma_start(out=xw[0][:], in_=xwT_v[:, :, 0:512]), "xw0")
            d_wv = P.dma("pool", lambda e: e.dma_start(out=wv[:], in_=w_in_v[:, :, 2048:3072]), "wv")
            xw_free = [None, None]
            kstg_free = [None, None]
            vstg_free = [None, None]
            bfree = [None] * 4
            g2 = [0]
            red_last = None

            def mm2(lhs_fn, rhs_fn, waits):
                b = g2[0] % 4
                g2[0] += 1
                dep = None
                for k in range(16):
                    dep = P.op("pe", (lambda e, k=k, b=b: e.matmul(pb2[b][:, :], lhs_fn(k), rhs_fn(k), start=(k == 0), stop=(k == 15))),
                               waits=(list(waits) + [bfree[b]]) if k == 0 else [], sig=(k == 15))
                return b, dep

            for c in range(8):
                s_ = c % 2
                if c + 1 < 8:
                    d_x[c + 1] = P.dma("pool", (lambda e, c=c: e.dma_start(out=xw[(c + 1) % 2][:], in_=xwT_v[:, :, 512 * (c + 1):512 * (c + 2)])),
                                       f"xw{(c + 1) % 2}", waits=[xw_free[(c + 1) % 2]])
                evs = []
                for h in range(8):
                    b, dep = mm2(lambda k, h=h: wk[:, k, 128 * h:128 * h + 128], lambda k, s_=s_: xw[s_][:, k, :], [d_wk, d_x[c]])
                    ev = P.op("act", lambda e, b=b, s_=s_, h=h: e.activation(
                        out=kstg[0][:, h, :], in_=pb2[b][:, :], func=AF.Identity, bias=bcol_t[:, 8 + h:9 + h], scale=1.0),
                        waits=[dep, kstg_free[0]])
                    bfree[b] = ev
                    evs.append(ev)
                r = None
                for blk_ in range(2):
                    r = P.op("dve", lambda e, s_=s_, c=c, blk_=blk_: e.tensor_reduce(
                        out=ksum[:, :, 2 * c + blk_], in_=kstg[0][:, :, 256 * blk_:256 * blk_ + 256], axis=AX.X, op=ALU.add),
                        waits=[evs[-1]])
                red_last = r
                dk = P.dma("sp", lambda e, s_=s_, c=c: e.dma_start(
                    out=kT_scr.rearrange("h d t -> d h t")[:, :, 512 * c:512 * c + 512], in_=kstg[0][:]), "kstg0", waits=[evs[-1]])
                kstg_free[0] = dk
                P.wait("act", [r]) if False else None
                kred = r
                vev = None
                lastpe = None
                for tt in range(4):
                    for half in range(2):
                        b, dep = mm2(lambda k, s_=s_, tt=tt: xw[s_][:, k, 128 * tt:128 * tt + 128],
                                     lambda k, half=half: wv[:, k, 512 * half:512 * half + 512], [d_wv, d_x[c]])
                        lastpe = dep
                        vev = P.op("dve", lambda e, b=b, s_=s_, tt=tt, half=half: e.tensor_tensor(
                            out=vstg[0][:, tt, 512 * half:512 * half + 512], in0=pb2[b][:, :], in1=brep_t[:, 512 * half:512 * half + 512], op=ALU.add),
                            waits=[dep, vstg_free[0]])
                        bfree[b] = vev
                dv = P.dma("sp", lambda e, s_=s_, c=c: e.dma_start(
                    out=v_scr[512 * c:512 * c + 512, :].rearrange("(t p) n -> p t n", p=128), in_=vstg[0][:]), "vstg0", waits=[vev])
                vstg_free[0] = dv
                xw_free[s_] = lastpe
                P.wait("act", [kred])
            km = P.op("dve", lambda e: e.tensor_scalar(out=kmT[:], in0=ksum[:], scalar1=1.0 / 256.0, scalar2=None, op0=ALU.mult),
                      waits=[red_last])
            P.wait("sp", [kstg_free[0], vstg_free[0]])
            P.wait("dve", [km])
            with (nc.named_scope(f'PH2') if scopes else ExitStack()):
                with nc.Block() as blk:
                    P.emit(blk)

        with ExitStack() as ps3:
            def sb3(name, shape, dtype):
                return ps3.enter_context(nc.sbuf_tensor(name, list(shape), dtype))
            KTh = [sb3(f"KTh{i}", [128, 4096], BF16) for i in range(2)]
            Vh = [sb3(f"Vh{i}", [128, 32, 130], BF16) for i in range(2)]
            Pt = [sb3(f"Pt{i}", [128, 512], BF16) for i in range(3)]
            gm = [sb3(f"gm{i}", [128, 16], F32) for i in range(2)]
            top8 = [sb3(f"top8{i}", [128, 8], F32) for i in range(2)]
            wsel = [sb3(f"wsel{i}", [128, 2, 16], F32) for i in range(2)]
            acc = [sb3(f"acc{i}", [128, 2, 129], F32) for i in range(2)]
            rden = [sb3(f"rden{i}", [128, 2], F32) for i in range(2)]
            attn_n = [sb3(f"attn_n{i}", [128, 128], F32) for i in range(2)]
            candb_t = sb3("candb_t", [128, 4, 16], F32)
            cand01_t = sb3("cand01_t", [128, 4, 16], F32)
            own01_t = sb3("own01_t", [128, 4, 16], F32)
            negm = sb3("negm", [128, 512], BF16)
            identb = sb3("identb", [128, 128], BF16)
            identf = sb3("identf", [128, 128], F32)
            misc3 = ps3.enter_context(nc.psum_tensor("misc3", [128, 512], F32))
            gps = misc3
            Sps = [ps3.enter_context(nc.psum_tensor(f"Sps{i}", [128, 512], F32)) for i in range(2)]
            Ops = [ps3.enter_context(nc.psum_tensor(f"Ops{i}", [128, 2, 256], F32)) for i in range(2)]
            Tps = [misc3[:, 128:256], misc3[:, 128:256]]

            cdeps = [
                P.dma("sp", lambda e: e.dma_start(out=candb_t[:], in_=candb[:, :, :]), "c0"),
                P.dma("sp", lambda e: e.dma_start(out=cand01_t[:], in_=cand01[:, :, :]), "c1"),
                P.dma("sp", lambda e: e.dma_start(out=own01_t[:], in_=own01[:, :, :]), "c2"),
                P.dma("sp", lambda e: e.dma_start(out=identf[:], in_=ident_d[:, :]), "c3"),
                P.dma("pool", lambda e: e.dma_start(out=negm[:], in_=negmask_d[:, :]), "c4"),
                P.dma("pool", lambda e: e.dma_start(out=identb[:], in_=ident_d[:, :]), "c5"),
            ]
            ones_dep = [P.op("dve", lambda e, i=i: e.memset(Vh[i][:, :, 128:130], 1.0)) for i in range(2)]
            P.wait("dve", cdeps[0:3])
            P.wait("pe", cdeps[3:6])
            holder4 = {}
            g4 = None
            if not skip4:
                def write_out(b_, osT, dep):
                    return P.op("act", lambda e, b_=b_: e.activation(out=mixT[:, 0:8, 1024 + b_], in_=osT[:, :], func=AF.Identity), waits=[dep])
                g4 = emit_p4(nc, P, ps3, ck_d, cv_d, ptrep_d, iota_d, dsel_d, ident_d, qs, ksT, vsT, gasT, write_out, holder=holder4)
            vis4 = [0]

            def step4():
                if g4 is None:
                    return
                vis4[0] += 1
                next(g4, None)
                if vis4[0] % 4 == 0:
                    next(g4, None)
            kv_free = [None, None]
            S_free = [None, None]
            O_free = [None, None]
            T_free = [None, None]
            Pt_free = [None, None, None]
            an_free = [None, None]
            gps_free = None
            cS = cO = cP = cT = 0
            SCALE = 128.0 ** -0.5
            mix_last = None
            qb = 0
            for h in range(8):
                s_ = h % 2
                d_k = P.dma("sp", lambda e, s_=s_, h=h: e.dma_start(out=KTh[s_][:], in_=kT_scr[h]), f"KTh{s_}", waits=[kv_free[s_]])
                d_v = P.dma("sp", lambda e, s_=s_, h=h: e.dma_start(
                    out=Vh[s_][:, :, 0:128], in_=v_scr.rearrange("(t p) n -> p t n", p=128)[:, :, 128 * h:128 * h + 128]),
                    f"Vh{s_}", waits=[kv_free[s_]])
                last_pe_h = None
                for i in range(4):
                    par = qb % 2
                    qb += 1
                    qc = 256 * i
                    gdep = None
                    for t in range(2):
                        gdep = P.op("pe", lambda e, t=t, h=h, qc=qc: e.matmul(
                            gps[:, 16 * t:16 * t + 16], qT[:, h, qc + 128 * t:qc + 128 * t + 128], kmT[:, h, :], start=True, stop=True),
                            waits=[gps_free, T_free[0], T_free[1]] if t == 0 else [])
                    wdeps = []
                    for t in range(2):
                        a1 = P.op("dve", lambda e, t=t, i=i: e.tensor_tensor(out=gm[t][:], in0=gps[:, 16 * t:16 * t + 16], in1=candb_t[:, i, :], op=ALU.add),
                                  waits=[gdep])
                        a2 = P.op("dve", lambda e, t=t: e.max(out=top8[t][:], in_=gm[t][:]), waits=[a1])
                        a3 = P.op("dve", lambda e, t=t, i=i, par=par: e.scalar_tensor_tensor(
                            out=wsel[par][:, t, :], in0=gm[t][:], scalar=top8[t][:, 2:3], in1=cand01_t[:, i, :], op0=ALU.is_ge, op1=ALU.mult),
                            waits=[a2])
                        a4 = P.op("dve", lambda e, t=t, i=i, par=par: e.tensor_tensor(
                            out=wsel[par][:, t, :], in0=wsel[par][:, t, :], in1=own01_t[:, i, :], op=ALU.add), waits=[a3])
                        wdeps.append(a4)
                        gps_free = a1
                    accdep = [None, None]
                    nblk = 4 * i + 4

                    def emit_qk(n_, h=h, s_=s_, qc=qc, nblk=nblk, d_k=d_k):
                        nonlocal cS
                        own = (n_ == nblk - 1)
                        sbk = cS % 2
                        cS += 1
                        first_w = [S_free[sbk], d_k, ones_dep[s_]]
                        if own:
                            P.op("pe", lambda e, sbk=sbk: e.matmul(Sps[sbk][:, :], identb[:], negm[:], start=True, stop=False),
                                 waits=first_w, sig=False)
                        sdep = None
                        for kt in range(2):
                            sdep = P.op("pe", lambda e, sbk=sbk, kt=kt, n_=n_, s_=s_, h=h, qc=qc, own=own: e.matmul(
                                Sps[sbk][:, 256 * kt:256 * kt + 256], KTh[s_][:, 256 * n_ + 128 * kt:256 * n_ + 128 * kt + 128],
                                qT[:, h, qc:qc + 256], start=(not own), stop=((not own) or kt == 1)),
                                waits=(first_w if (kt == 0 and not own) else []), sig=(kt == 1))
                        return sbk, sdep

                    pend = emit_qk(0)
                    for n_ in range(nblk):
                        sbk, sdep = pend
                        pp = cP % 3
                        cP += 1
                        edep = P.op("act", lambda e, pp=pp, sbk=sbk: e.activation(out=Pt[pp][:], in_=Sps[sbk][:, :], func=AF.Exp, scale=SCALE),
                                    waits=[sdep, Pt_free[pp]])
                        S_free[sbk] = edep
                        if n_ + 1 < nblk:
                            pend = emit_qk(n_ + 1)
                        ob = cO % 2
                        cO += 1
                        pvdep = None
                        for t in range(2):
                            for kt in range(2):
                                pvdep = P.op("pe", lambda e, ob=ob, t=t, kt=kt, pp=pp, s_=s_, n_=n_: e.matmul(
                                    Ops[ob][:, t, 0:129], Pt[pp][:, 256 * kt + 128 * t:256 * kt + 128 * t + 128],
                                    Vh[s_][:, 2 * n_ + kt, 0:129], start=(kt == 0), stop=(kt == 1)),
                                    waits=[edep, O_free[ob], d_v] if (t == 0 and kt == 0) else [], sig=(t == 1 and kt == 1))
                        Pt_free[pp] = pvdep
                        last_pe_h = pvdep
                        for t in range(2):
                            if n_ == 0:
                                accdep[t] = P.op("dve", lambda e, t=t, ob=ob, par=par: e.tensor_scalar(
                                    out=acc[par][:, t, :], in0=Ops[ob][:, t, 0:129], scalar1=wsel[par][:, t, 0:1], scalar2=None, op0=ALU.mult),
                                    waits=[pvdep, wdeps[t], an_free[par]])
                            else:
                                accdep[t] = P.op("dve", lambda e, t=t, ob=ob, par=par, n_=n_: e.scalar_tensor_tensor(
                                    out=acc[par][:, t, :], in0=Ops[ob][:, t, 0:129], scalar=wsel[par][:, t, n_:n_ + 1], in1=acc[par][:, t, :],
                                    op0=ALU.mult, op1=ALU.add), waits=[pvdep, accdep[t]])
                        O_free[ob] = accdep[1]
                        step4()
                    fin_last = None
                    for t in range(2):
                        r1 = P.op("dve", lambda e, t=t, par=par: e.reciprocal(out=rden[par][:, t:t + 1], in_=acc[par][:, t, 128:129]),
                                  waits=[accdep[t]])
                        asl = cT % 2
                        r2 = P.op("dve", lambda e, t=t, par=par, asl=asl, i=i, h=h: e.scalar_tensor_tensor(
                            out=attn_n[asl][:], in0=acc[par][:, t, 0:128], scalar=rden[par][:, t:t + 1], in1=ga[:, 2 * i + t, 128 * h:128 * h + 128],
                            op0=ALU.mult, op1=ALU.mult), waits=[r1, T_free[asl]])
                        tb = cT % 2
                        cT += 1
                        tp = P.op("pe", lambda e, tb=tb, asl=asl: e.transpose(Tps[tb][:, :], attn_n[asl][:], identf[:]),
                                  waits=[r2, T_free[0], T_free[1]])
                        cp = P.op("act", lambda e, tb=tb, h=h, qc=qc, t=t: e.activation(
                            out=mixT[:, h, qc + 128 * t:qc + 128 * t + 128], in_=Tps[tb][:, :], func=AF.Identity), waits=[tp])
                        T_free[tb] = cp
                        mix_last = cp
                        fin_last = r2
                    an_free[par] = fin_last
                kv_free[s_] = last_pe_h
            if g4 is not None:
                for _ in g4:
                    pass
                P.wait("act", [holder4["last"]])
            else:
                mz = P.op("dve", lambda e: e.memset(mixT[:, 0:8, 1024:1028], 0.0))
                P.wait("dve", [mz])
            P.wait("act", [mix_last])
            with (nc.named_scope(f'PH3') if scopes else ExitStack()):
                with nc.Block() as blk:
                    P.emit(blk)

        CH = [(0, 344), (344, 342), (686, 342)]
        with ExitStack() as ps5:
            def sb5(name, shape, dtype):
                return ps5.enter_context(nc.sbuf_tensor(name, list(shape), dtype))
            uB = sb5("uB", [128, 8, 4, 288], BF16)
            uTs = sb5("uTs", [128, 8, 4], F32)
            cTb = sb5("cTb", [128, 8, 4, 256], F32)
            cTs = sb5("cTs", [128, 8, 4], F32)
            cnT = sb5("cnT", [128, 8, NCc], BF16)
            wdw_t = sb5("wdw_t", [128, 8, 31], F32)
            vec8_t = sb5("vec8_t", [128, 4, 8], F32)
            stT = sb5("stT_sb", [128, 4, 8, 30], F32)
            stmp = sb5("stmp", [128, 8, 30], F32)
            s1t = sb5("s1t", [128, 8], F32)
            wpw = sb5("wpw", [128, 8, 1024], BF16)
            onesf = sb5("onesf", [128, 128], F32)
            epsT = sb5("epsT", [128, 1], F32)
            sqt = [sb5(f"sqt{i}", [128, 256], F32) for i in range(2)]
            mean_t = sb5("mean_t", [128, 256], F32)
            m2_t = sb5("m2_t", [128, 256], F32)
            rstd_t = sb5("rstd_t", [128, 256], F32)
            ntmp = [sb5(f"ntmp{i}", [128, 256], F32) for i in range(2)]
            S1p = ps5.enter_context(nc.psum_tensor("S1p", [128, 256], F32))
            S2p = ps5.enter_context(nc.psum_tensor("S2p", [128, 256], F32))
            pwb = [ps5.enter_context(nc.psum_tensor(f"pwb{i}", [128, 512], F32)) for i in range(2)]

            l_u = P.dma("pool", lambda e: e.dma_start(out=uB[:], in_=u_scr[:, :, 0:1152].rearrange("p g (i c) -> p g i c", c=288)), "c0")
            l_us = P.dma("sp", lambda e: e.dma_start(out=uTs[:], in_=u_scr[:, :, 1152:1156]), "c1")
            l_w = P.dma("sp", lambda e: e.dma_start(out=wdw_t[:], in_=wdw_d[:, :, :]), "c2")
            l_v8 = P.dma("sp", lambda e: e.dma_start(out=vec8_t[:], in_=vec8[:, :, :]), "c3")
            l_st = P.dma("sp", lambda e: e.dma_start(out=stT[:], in_=stT_d[:, :, :, :]), "c4")
            l_pw = P.dma("pool", lambda e: e.dma_start(out=wpw[:], in_=w_pw.rearrange("(k p) n -> p k n", p=128)), "c5")
            m1 = P.op("dve", lambda e: e.memset(onesf[:], 1.0))
            m2 = P.op("dve", lambda e: e.memset(epsT[:], 1e-5))
            P.wait("dve", [l_u, l_us, l_w, l_v8, l_st])
            identb5 = sb5("identb5", [128, 128], BF16)
            dg = [sb5(f"dg{i}", [128, 31, 128], BF16) for i in range(2)]
            cps = [ps5.enter_context(nc.psum_tensor(f"cps{i}", [128, 256], F32)) for i in range(2)]
            l_id = P.dma("pool", lambda e: e.dma_start(out=identb5[:], in_=ident_d[:, :]), "c6")
            P.wait("dve", [l_id, l_w])
            ub_dep = [l_u] * 8
            cdep = [None] * 8
            dg_free = [None, None]
            cps_free = [None, None]
            ccnt = 0
            for g in range(8):
                ds_ = g % 2
                dgd = None
                for tap in range(31):
                    dgd = P.op("dve", lambda e, g=g, tap=tap, ds_=ds_: e.tensor_scalar(
                        out=dg[ds_][:, tap, :], in0=identb5[:], scalar1=wdw_t[:, g, tap:tap + 1], scalar2=None, op0=ALU.mult),
                        waits=[dg_free[ds_]] if tap == 0 else [])
                lastmm = None
                ev = None
                for i in range(4):
                    cs_ = ccnt % 2
                    ccnt += 1
                    for tap in range(31):
                        lastmm = P.op("pe", lambda e, g=g, i=i, tap=tap, ds_=ds_, cs_=cs_: e.matmul(
                            cps[cs_][:, :], dg[ds_][:, tap, :], uB[:, g, i, 2 + tap:258 + tap], start=(tap == 0), stop=(tap == 30)),
                            waits=[dgd, ub_dep[g], cps_free[cs_]] if tap == 0 else [], sig=(tap == 30))
                    ev = P.op("act", lambda e, g=g, i=i, cs_=cs_: e.activation(
                        out=cTb[:, g, i, :], in_=cps[cs_][:, :], func=AF.Identity, bias=vec8_t[:, 0, g:g + 1], scale=1.0), waits=[lastmm, l_v8])
                    cps_free[cs_] = ev
                dg_free[ds_] = lastmm
                cdep[g] = ev
            sdep = None
            for b_ in range(4):
                d1 = P.op("dve", lambda e, b_=b_: e.tensor_tensor(out=stmp[:], in0=stT[:, b_, :, :], in1=wdw_t[:, :, 0:30], op=ALU.mult), waits=[sdep])
                d2 = P.op("dve", lambda e: e.tensor_reduce(out=s1t[:], in_=stmp[:], axis=AX.X, op=ALU.add), waits=[d1])
                d3 = P.op("dve", lambda e, b_=b_: e.tensor_tensor(out=cTs[:, :, b_], in0=uTs[:, :, b_], in1=wdw_t[:, :, 30], op=ALU.mult), waits=[d2])
                d4 = P.op("dve", lambda e, b_=b_: e.tensor_tensor(out=cTs[:, :, b_], in0=cTs[:, :, b_], in1=s1t[:], op=ALU.add), waits=[d3])
                sdep = P.op("dve", lambda e, b_=b_: e.tensor_tensor(out=cTs[:, :, b_], in0=cTs[:, :, b_], in1=vec8_t[:, 0, :], op=ALU.add), waits=[d4])
            groups = [(lambda g, i=i: cTb[:, g, i, :], 256, 256 * i) for i in range(4)] + [(lambda g: cTs[:, g, :], 4, 1024)]
            sq_free = [None, None]
            st_free = None
            nt_free = [None, None]
            cnt5 = 0
            cn_last = None
            for (src, n, c0) in groups:
                mm = None
                for g in range(8):
                    sl = cnt5 % 2
                    cnt5 += 1
                    a_sq = P.op("act", lambda e, g=g, sl=sl, src=src, n=n: e.activation(out=sqt[sl][:, 0:n], in_=src(g), func=AF.Square),
                                waits=[cdep[g], sdep, sq_free[sl]])
                    P.op("pe", lambda e, g=g, src=src, n=n: e.matmul(S1p[:, 0:n], onesf[:], src(g), start=(g == 0), stop=(g == 7)),
                         waits=[cdep[g], sdep, m1, st_free] if g == 0 else [cdep[g]], sig=False)
                    mm = P.op("pe", lambda e, g=g, sl=sl, n=n: e.matmul(S2p[:, 0:n], onesf[:], sqt[sl][:, 0:n], start=(g == 0), stop=(g == 7)),
                              waits=[a_sq])
                    sq_free[sl] = mm
                e1 = P.op("act", lambda e, n=n: e.activation(out=mean_t[:, 0:n], in_=S1p[:, 0:n], func=AF.Identity, scale=1.0 / 1024.0), waits=[mm, cn_last])
                e2 = P.op("dve", lambda e, n=n: e.tensor_tensor(out=m2_t[:, 0:n], in0=mean_t[:, 0:n], in1=mean_t[:, 0:n], op=ALU.mult), waits=[e1, cn_last])
                e3 = P.op("dve", lambda e, n=n: e.scalar_tensor_tensor(out=m2_t[:, 0:n], in0=S2p[:, 0:n], scalar=1.0 / 1024.0, in1=m2_t[:, 0:n],
                                                                      op0=ALU.mult, op1=ALU.subtract), waits=[e2, mm])
                st_free = e3
                e4 = P.op("act", lambda e, n=n: e.activation(out=rstd_t[:, 0:n], in_=m2_t[:, 0:n], func=AF.Sqrt, bias=epsT[:, 0:1], scale=1.0), waits=[e3, m2])
                e5 = P.op("dve", lambda e, n=n: e.reciprocal(out=rstd_t[:, 0:n], in_=rstd_t[:, 0:n]), waits=[e4])
                for g in range(8):
                    sl = cnt5 % 2
                    cnt5 += 1
                    f1 = P.op("dve", lambda e, g=g, sl=sl, src=src, n=n: e.tensor_tensor(out=ntmp[sl][:, 0:n], in0=src(g), in1=mean_t[:, 0:n], op=ALU.subtract),
                              waits=[e5, nt_free[sl]])
                    f2 = P.op("dve", lambda e, sl=sl, n=n: e.tensor_tensor(out=ntmp[sl][:, 0:n], in0=ntmp[sl][:, 0:n], in1=rstd_t[:, 0:n], op=ALU.mult), waits=[f1])
                    f3 = P.op("act", lambda e, g=g, sl=sl, n=n, c0=c0: e.activation(
                        out=cnT[:, g, c0:c0 + n], in_=ntmp[sl][:, 0:n], func=AF.Silu, bias=vec8_t[:, 2, g:g + 1], scale=vec8_t[:, 1, g:g + 1]), waits=[f2])
                    nt_free[sl] = f3
                    cn_last = f3
            pw_free = [None, None]
            cntp = 0
            pw_last = None
            for cb in range(8):
                for (c0, n) in CH:
                    bsl = cntp % 2
                    cntp += 1
                    mm = None
                    for k in range(8):
                        mm = P.op("pe", lambda e, k=k, cb=cb, c0=c0, n=n, bsl=bsl: e.matmul(
                            pwb[bsl][:, 0:n], wpw[:, k, 128 * cb:128 * cb + 128], cnT[:, k, c0:c0 + n], start=(k == 0), stop=(k == 7)),
                            waits=[l_pw, cn_last, pw_free[bsl]] if k == 0 else [], sig=(k == 7))
                    pw_last = P.op("dve", lambda e, cb=cb, c0=c0, n=n, bsl=bsl: e.scalar_tensor_tensor(
                        out=mixT[:, 8 + cb, c0:c0 + n], in0=pwb[bsl][:, 0:n], scalar=vec8_t[:, 3, cb:cb + 1], in1=gcT[:, cb, c0:c0 + n],
                        op0=ALU.add, op1=ALU.mult), waits=[mm])
                    pw_free[bsl] = pw_last
            P.wait("act", [cn_last])
            P.wait("dve", [pw_last])
            with (nc.named_scope(f'PH5') if scopes else ExitStack()):
                with nc.Block() as blk:
                    P.emit(blk)

        ALPHA = 2.0 ** 0.25
        mid.close()
        with ExitStack() as ps6:
            def sb6(name, shape, dtype):
                return ps6.enter_context(nc.sbuf_tensor(name, list(shape), dtype))
            rT = sb6("rT", [128, 16, NCc], F32)
            hbf = sb6("hbf", [128, 16, NCc], BF16)
            wo = [sb6(f"wo{i}", [128, 16, 512], BF16) for i in range(2)]
            wpe = sb6("wpe", [128, 2, D], BF16)
            pTb = sb6("pTb", [128, 2, NCc], BF16)
            vec16_t = sb6("vec16_t", [128, 4, 16], F32)
            xr = [sb6(f"xr{i}", [128, 344], F32) for i in range(2)]
            rtmp = [sb6(f"rtmp{i}", [128, 344], F32) for i in range(2)]
            sq6 = [sb6(f"sq6{i}", [128, 344], F32) for i in range(2)]
            onesf6 = sb6("onesf6", [128, 128], F32)
            eps6 = sb6("eps6", [128, 1], F32)
            mean6 = sb6("mean6", [128, NCc], F32)
            rstd6 = sb6("rstd6", [128, NCc], F32)
            sgm = [sb6(f"sgm{i}", [128, 344], F32) for i in range(2)]
            yst = [sb6(f"yst{i}", [128, 344], F32) for i in range(2)]
            S1 = [ps6.enter_context(nc.psum_tensor(f"S1_{i}", [128, 512], F32)) for i in range(3)]
            S2 = [ps6.enter_context(nc.psum_tensor(f"S2_{i}", [128, 512], F32)) for i in range(3)]
            mb = [ps6.enter_context(nc.psum_tensor(f"mb{i}", [128, 512], F32)) for i in range(2)]

            l_v16 = P.dma("sp", lambda e: e.dma_start(out=vec16_t[:], in_=vec16[:, :, :]), "c0")
            l_pe = P.dma("pool", lambda e: e.dma_start(out=wpe[:], in_=w_pe.rearrange("(k p) n -> p k n", p=128)), "c1")
            l_pt = P.dma("pool", lambda e: e.dma_start(out=pTb[:], in_=pT.rearrange("(k p) n -> p k n", p=128)), "c2")
            o1 = P.op("dve", lambda e: e.memset(onesf6[:], 1.0))
            o2 = P.op("dve", lambda e: e.memset(eps6[:], 1e-5))
            wo_free = [None, None]
            mb_free = [None, None]
            xr_free = [None, None]
            rt_free = [None, None]
            sq_free6 = [None, None]
            w_out_v = w_out.rearrange("(k p) n -> p k n", p=128)
            w_pg_v = w_pg.rearrange("(k p) n -> p k n", p=128)
            cnt6 = 0
            wt_i = 0
            stat_last = None
            for t in range(4):
                ws = wt_i % 2
                wt_i += 1
                d_w = P.dma("pool", lambda e, t=t, ws=ws: e.dma_start(out=wo[ws][:], in_=w_out_v[:, :, 512 * t:512 * t + 512]), f"wo{ws}", waits=[wo_free[ws]])
                lastpe = None
                for cbl in range(4):
                    cb = 4 * t + cbl
                    for ci, (c0, n) in enumerate(CH):
                        sl = cnt6 % 2
                        cnt6 += 1
                        d_x = P.dma("sp", lambda e, sl=sl, cb=cb, c0=c0, n=n: e.dma_start(out=xr[sl][:, 0:n], in_=xcT[128 * cb:128 * cb + 128, c0:c0 + n]),
                                    f"xr{sl}", waits=[xr_free[sl]])
                        mm = None
                        for k in range(16):
                            mm = P.op("pe", lambda e, k=k, ws=ws, cbl=cbl, c0=c0, n=n, sl=sl: e.matmul(
                                mb[sl][:, 0:n], wo[ws][:, k, 128 * cbl:128 * cbl + 128], mixT[:, k, c0:c0 + n], start=(k == 0), stop=(k == 15)),
                                waits=[d_w, mb_free[sl]] if k == 0 else [], sig=(k == 15))
                        lastpe = mm
                        r1 = P.op("dve", lambda e, sl=sl, n=n: e.scalar_tensor_tensor(
                            out=rtmp[sl][:, 0:n], in0=xr[sl][:, 0:n], scalar=ALPHA, in1=mb[sl][:, 0:n], op0=ALU.mult, op1=ALU.add),
                            waits=[mm, d_x, rt_free[sl]])
                        mb_free[sl] = r1
                        xr_free[sl] = r1
                        r2 = P.op("act", lambda e, sl=sl, cb=cb, c0=c0, n=n: e.activation(
                            out=rT[:, cb, c0:c0 + n], in_=rtmp[sl][:, 0:n], func=AF.Identity, bias=vec16_t[:, 0, cb:cb + 1], scale=1.0), waits=[r1, l_v16])
                        r3 = P.op("act", lambda e, sl=sl, cb=cb, n=n: e.activation(
                            out=sq6[sl][:, 0:n], in_=rtmp[sl][:, 0:n], func=AF.Square, bias=vec16_t[:, 0, cb:cb + 1], scale=1.0), waits=[sq_free6[sl]])
                        rt_free[sl] = r3
                        P.op("pe", lambda e, cb=cb, ci=ci, c0=c0, n=n: e.matmul(S1[ci][:, 0:n], onesf6[:], rT[:, cb, c0:c0 + n], start=(cb == 0), stop=(cb == 15)),
                             waits=[r2, o1], sig=False)
                        stat_last = P.op("pe", lambda e, cb=cb, ci=ci, n=n, sl=sl: e.matmul(S2[ci][:, 0:n], onesf6[:], sq6[sl][:, 0:n], start=(cb == 0), stop=(cb == 15)),
                                         waits=[r3])
                        sq_free6[sl] = stat_last
                wo_free[ws] = lastpe
            h_last = None
            for ci, (c0, n) in enumerate(CH):
                e1 = P.op("act", lambda e, ci=ci, c0=c0, n=n: e.activation(out=mean6[:, c0:c0 + n], in_=S1[ci][:, 0:n], func=AF.Identity, scale=1.0 / D), waits=[stat_last])
                e2 = P.op("dve", lambda e, c0=c0, n=n: e.tensor_tensor(out=rstd6[:, c0:c0 + n], in0=mean6[:, c0:c0 + n], in1=mean6[:, c0:c0 + n], op=ALU.mult), waits=[e1])
                e3 = P.op("dve", lambda e, ci=ci, c0=c0, n=n: e.scalar_tensor_tensor(out=rstd6[:, c0:c0 + n], in0=S2[ci][:, 0:n], scalar=1.0 / D, in1=rstd6[:, c0:c0 + n],
                                                                                    op0=ALU.mult, op1=ALU.subtract), waits=[e2, stat_last])
                e4 = P.op("act", lambda e, c0=c0, n=n: e.activation(out=rstd6[:, c0:c0 + n], in_=rstd6[:, c0:c0 + n], func=AF.Sqrt, bias=eps6[:, 0:1], scale=1.0), waits=[e3, o2])
                e5 = P.op("dve", lambda e, c0=c0, n=n: e.reciprocal(out=rstd6[:, c0:c0 + n], in_=rstd6[:, c0:c0 + n]), waits=[e4])
                for cb in range(16):
                    f1 = P.op("dve", lambda e, cb=cb, c0=c0, n=n: e.tensor_tensor(out=rT[:, cb, c0:c0 + n], in0=rT[:, cb, c0:c0 + n], in1=mean6[:, c0:c0 + n], op=ALU.subtract), waits=[e5])
                    f2 = P.op("dve", lambda e, cb=cb, c0=c0, n=n: e.tensor_tensor(out=rT[:, cb, c0:c0 + n], in0=rT[:, cb, c0:c0 + n], in1=rstd6[:, c0:c0 + n], op=ALU.mult), waits=[f1])
                    f3 = P.op("act", lambda e, cb=cb, c0=c0, n=n: e.activation(out=rT[:, cb, c0:c0 + n], in_=rT[:, cb, c0:c0 + n], func=AF.Identity,
                                                                               bias=vec16_t[:, 2, cb:cb + 1], scale=vec16_t[:, 1, cb:cb + 1]), waits=[f2])
                    h_last = P.op("act", lambda e, cb=cb, c0=c0, n=n: e.activation(out=hbf[:, cb, c0:c0 + n], in_=rT[:, cb, c0:c0 + n], func=AF.Identity), waits=[f3])
            sg_free = [None, None]
            ys_free = [None, None]
            for t in range(4):
                ws = wt_i % 2
                wt_i += 1
                d_w = P.dma("pool", lambda e, t=t, ws=ws: e.dma_start(out=wo[ws][:], in_=w_pg_v[:, :, 512 * t:512 * t + 512]), f"wo{ws}", waits=[wo_free[ws]])
                lastpe = None
                for cbl in range(4):
                    cb = 4 * t + cbl
                    for ci, (c0, n) in enumerate(CH):
                        mmA = None
                        for k in range(16):
                            mmA = P.op("pe", lambda e, k=k, ws=ws, cbl=cbl, c0=c0, n=n: e.matmul(
                                mb[0][:, 0:n], wo[ws][:, k, 128 * cbl:128 * cbl + 128], hbf[:, k, c0:c0 + n], start=(k == 0), stop=(k == 15)),
                                waits=[d_w, h_last, mb_free[0]] if k == 0 else [], sig=(k == 15))
                        mmB = None
                        for k in range(2):
                            mmB = P.op("pe", lambda e, k=k, cb=cb, c0=c0, n=n: e.matmul(
                                mb[1][:, 0:n], wpe[:, k, 128 * cb:128 * cb + 128], pTb[:, k, c0:c0 + n], start=(k == 0), stop=(k == 1)),
                                waits=[l_pe, l_pt, mb_free[1]] if k == 0 else [], sig=(k == 1))
                        lastpe = mmB
                        sl = cnt6 % 2
                        cnt6 += 1
                        g1 = P.op("act", lambda e, sl=sl, cb=cb, n=n: e.activation(out=sgm[sl][:, 0:n], in_=mb[0][:, 0:n], func=AF.Sigmoid,
                                                                                 bias=vec16_t[:, 3, cb:cb + 1], scale=1.0), waits=[mmA, sg_free[sl]])
                        mb_free[0] = g1
                        g2 = P.op("dve", lambda e, sl=sl, n=n: e.tensor_tensor(out=yst[sl][:, 0:n], in0=sgm[sl][:, 0:n], in1=mb[1][:, 0:n], op=ALU.mult),
                                  waits=[g1, mmB, ys_free[sl]])
                        mb_free[1] = g2
                        sg_free[sl] = g2
                        g3 = P.op("dve", lambda e, sl=sl, cb=cb, c0=c0, n=n: e.tensor_tensor(out=yst[sl][:, 0:n], in0=yst[sl][:, 0:n], in1=rT[:, cb, c0:c0 + n], op=ALU.add),
                                  waits=[g2])
                        dd = P.dma("sp", lambda e, sl=sl, cb=cb, c0=c0, n=n: e.dma_start(out=yT_out[128 * cb:128 * cb + 128, c0:c0 + n], in_=yst[sl][:, 0:n]),
                                   f"yst{sl}", waits=[g3])
                        ys_free[sl] = dd
                wo_free[ws] = lastpe
            P.wait("sp", [ys_free[0], ys_free[1]])
            P.wait("act", [h_last])
            with (nc.named_scope(f'PH6') if scopes else ExitStack()):
                with nc.Block() as blk:
                    P.emit(blk)
    return nc


_NC_CACHE = {}


def _prep_core(c, inp):
    b, j = c // 4, c % 4
    x = inp["x_prompt"][b]
    xs = inp["x_sample"][4 * c:4 * c + 4, 0, :]
    xoT = np.zeros((D, NT), np.float32)
    hmask = np.ones((128, 4), np.float32)
    for i in range(4):
        g = 4 * i + j
        t0 = 256 * g
        if g > 0:
            xoT[:, 288 * i:288 * i + 32] = x[t0 - 32:t0].T
        else:
            hmask[:, i] = 0.0
        xoT[:, 288 * i + 32:288 * i + 288] = x[t0:t0 + 256].T
    xoT[:, 1152:1156] = xs.T
    npad = 3 - j
    xwT = np.zeros((D, 4096), np.float32)
    for n_ in range(16):
        g = n_ - npad
        if g >= 0:
            xwT[:, 256 * n_:256 * n_ + 256] = x[256 * g:256 * g + 256].T
    candb = np.full((128, 4, 16), -1e30, np.float32)
    cand01 = np.zeros((128, 4, 16), np.float32)
    own01 = np.zeros((128, 4, 16), np.float32)
    for i in range(4):
        candb[:, i, npad:4 * i + 3] = 0.0
        cand01[:, i, npad:4 * i + 3] = 1.0
        own01[:, i, 4 * i + 3] = 1.0
    xcT = np.zeros((D, NCc), np.float32)
    pTm = np.zeros((256, NCc), np.float32)
    pp = inp["p_prompt"][0, b]
    for i in range(4):
        g = 4 * i + j
        xcT[:, 256 * i:256 * i + 256] = x[256 * g:256 * g + 256].T
        pTm[:, 256 * i:256 * i + 256] = pp[256 * g:256 * g + 256].T
    xcT[:, 1024:1028] = xs.T
    pTm[:, 1024:1028] = inp["p_sample"][0, 4 * c:4 * c + 4, 0, :].T
    st = inp["state_conv"][0, 4 * c:4 * c + 4]
    stT = np.ascontiguousarray(st.reshape(4, 30, 8, 128).transpose(3, 0, 2, 1))
    ptrep = np.ascontiguousarray(np.broadcast_to(inp["page_table"][4 * c:4 * c + 4].reshape(1, 256), (128, 256)).astype(np.int32))
    return {"xoT": xoT, "hmask": hmask, "xwT": xwT, "candb": candb, "cand01": cand01, "own01": own01,
            "xcT": xcT, "pT": pTm, "stT": stT, "ptrep": ptrep, "st_raw": np.ascontiguousarray(st)}


def kernel(**inp):
    inp = {k: np.asarray(v) for k, v in inp.items()}
    if "nc" not in _NC_CACHE:
        _NC_CACHE["nc"] = build()
    nc = _NC_CACHE["nc"]
    b_in = inp["b_in"][0]
    bcol = np.ascontiguousarray(b_in.reshape(56, 128).T)
    brep = np.ascontiguousarray(np.broadcast_to(np.concatenate([b_in[2048:3072], b_in[3072:4096]])[None, :], (128, 2048)))
    kk = np.arange(128)[:, None]
    ss = np.arange(256)[None, :]
    negmask = np.concatenate([np.where(kk <= ss, 0.0, -1e30), np.where(kk + 128 <= ss, 0.0, -1e30)], axis=1).astype(np.float32)
    shared = {"w_in": np.ascontiguousarray(inp["w_in"][0]), "bcol": bcol, "brep": brep,
              "negmask": negmask, "ident": np.eye(128, dtype=np.float32)}
    def pcol(v, n):
        return v.reshape(n, 128).T
    shared["wdw"] = np.ascontiguousarray(inp["w_dw"][0].reshape(31, 8, 128).transpose(2, 1, 0))
    shared["vec8"] = np.ascontiguousarray(np.stack([pcol(inp[k][0], 8) for k in ("b_dw", "g_cn", "b_cn", "b_pw")], axis=1))
    shared["vec16"] = np.ascontiguousarray(np.stack([pcol(inp[k][0], 16) for k in ("b_out", "g_ln", "b_ln", "b_pg")], axis=1))
    for k in ("w_pw", "w_out", "w_pg", "w_pe"):
        shared[k] = np.ascontiguousarray(inp[k][0])
    shared["ck"] = inp["cache_k"][0].reshape(2560 * 128, 1024)
    shared["cv"] = inp["cache_v"][0].reshape(2560 * 128, 1024)
    shared["iota"] = np.arange(128, dtype=np.float32).reshape(128, 1)
    dsel = np.zeros((128, 32, 16), np.float32)
    for h in range(8):
        dsel[h, :, h] = 1.0
        dsel[h, :, 8 + h] = 1.0
    shared["dsel"] = dsel
    in_maps = []
    for c in range(8):
        m = dict(shared)
        m.update(_prep_core(c, inp))
        in_maps.append(m)
    res = run_bass_kernel_spmd(nc, in_maps, core_ids=list(range(8)))
    R = res.results

    y_prompt = np.zeros((2, 4096, 2048), np.float32)
    y_sample = np.zeros((32, 1, 2048), np.float32)
    k_p = np.zeros((1, 2, 4096, 8, 128), np.float32)
    v_p = np.zeros((1, 2, 4096, 8, 128), np.float32)
    c_p = np.zeros((1, 2, 30, 1024), np.float32)
    k_s = np.zeros((1, 32, 1, 8, 128), np.float32)
    v_s = np.zeros((1, 32, 1, 8, 128), np.float32)
    c_s = np.zeros((1, 32, 30, 1024), np.float32)
    for c in range(8):
        b, j = c // 4, c % 4
        r = R[c]
        kT = r["kT_out"]
        vv = r["v_out"]
        for i in range(4):
            g = 4 * i + j
            t0 = 256 * g
            k_p[0, b, t0:t0 + 256] = kT[:, :, 288 * i + 32:288 * i + 288].transpose(2, 0, 1)
            v_p[0, b, t0:t0 + 256] = vv[256 * i:256 * i + 256].reshape(256, 8, 128)
        k_s[0, 4 * c:4 * c + 4, 0] = kT[:, :, 1152:1156].transpose(2, 0, 1)
        v_s[0, 4 * c:4 * c + 4, 0] = r["vsT_out"].transpose(2, 1, 0)
        yT = r["yT_out"]
        for i in range(4):
            g = 4 * i + j
            y_prompt[b, 256 * g:256 * g + 256] = yT[:, 256 * i:256 * i + 256].T
        y_sample[4 * c:4 * c + 4, 0] = yT[:, 1024:1028].T
        u = r["uT_out"]
        if j == 3:
            c_p[0, b] = u[:, :, 0:30].transpose(2, 1, 0).reshape(30, 1024)
        c_s[0, 4 * c:4 * c + 4, 29] = u[:, :, 30:34].transpose(2, 1, 0).reshape(4, 1024)
    for c in range(8):
        c_s[0, 4 * c:4 * c + 4, 0:29] = R[c]["cs_out"]
    return (y_prompt, y_sample, k_p, v_p, c_p, k_s, v_s, c_s)
```
